# Optimizing a Trainium2 kernel written in Bass

```python
import jax, jax.numpy as jnp
from jax import lax
import numpy as np

D_MODEL = 1024
BATCH = 8
SEQ = 2048
DEPTH = 1

RWKV_HEAD_DIM = 64
RWKV_HEADS = 8
RWKV_WIDTH = RWKV_HEADS * RWKV_HEAD_DIM
DECAY_LORA = 32
ICLR_LORA = 32
GATE_LORA = 96
N_DIR = 2
RWKV_COLS = 3 * RWKV_WIDTH + N_DIR * DECAY_LORA + N_DIR * ICLR_LORA + GATE_LORA
LN_X_EPS = 64e-5

MLA_HEADS = 8
QK_NOPE_DIM = 64
QK_ROPE_DIM = 32
V_HEAD_DIM = 64
MLA_WIDTH = MLA_HEADS * V_HEAD_DIM
Q_LORA_RANK = 256
KV_LORA_RANK = 128
MLA_COLS = Q_LORA_RANK + KV_LORA_RANK + QK_ROPE_DIM
ROPE_THETA = 10000.0
Q_BLOCK = 128

D_IN = RWKV_COLS + MLA_COLS
D_MIX = RWKV_WIDTH + MLA_WIDTH

MEM_TOKENS = 256
MEM_HEADS = 4
MEM_HEAD_DIM = D_MODEL // MEM_HEADS

D_FF = 4 * D_MODEL
NORM_EPS = 1e-6

kernel_name = "hymba_rwkv7_mla_memxattn_encoder"


def rms_norm(x, g):
    xf = x.astype(jnp.float32)
    y = xf * lax.rsqrt(jnp.mean(xf * xf, axis=-1, keepdims=True) + NORM_EPS)
    return (y * g.astype(jnp.float32)).astype(x.dtype)


def short_conv(z, c):
    prev = jnp.pad(z[:, :-1], ((0, 0), (1, 0), (0, 0)))
    nxt = jnp.pad(z[:, 1:], ((0, 0), (0, 1), (0, 0)))
    return c[0] * prev + c[1] * z + c[2] * nxt


def rope_tables(positions):
    inv_freq = ROPE_THETA ** (-jnp.arange(0, QK_ROPE_DIM, 2, dtype=jnp.float32) / QK_ROPE_DIM)
    ang = positions.astype(jnp.float32)[..., None] * inv_freq
    return jnp.cos(ang), jnp.sin(ang)


def apply_rope(x, cos, sin):
    xf = x.astype(jnp.float32)
    x1, x2 = jnp.split(xf, 2, axis=-1)
    return jnp.concatenate([x1 * cos - x2 * sin, x2 * cos + x1 * sin], axis=-1).astype(x.dtype)


def _dirs_time_major(t):
    B, S = t.shape[0], t.shape[1]
    t = t.reshape(B, S, N_DIR, RWKV_HEADS, RWKV_HEAD_DIM).transpose(1, 2, 0, 3, 4)
    return jnp.stack([t[:, 0], t[::-1, 1]], axis=1)


def _wkv7_step(state, inp):
    r, w, k, v, a, b = inp
    sa = jnp.einsum('dbhvk,dbhk->dbhv', state, a)
    state = state * w[..., None, :] + sa[..., :, None] * b[..., None, :] + v[..., :, None] * k[..., None, :]
    out = jnp.einsum('dbhvk,dbhk->dbhv', state, r)
    return state, out


def rwkv7_bidirectional(z, w0, w2, a0, a2, g2, k_k, k_a, r_k, lnx_w, lnx_b):
    B, S, _ = z.shape
    C, H, N = RWKV_WIDTH, RWKV_HEADS, RWKV_HEAD_DIM
    f32 = jnp.float32
    cuts = [C, 2 * C, 3 * C, 3 * C + N_DIR * DECAY_LORA, 3 * C + N_DIR * (DECAY_LORA + ICLR_LORA)]
    r, k, v, xw, xa, xg = jnp.split(z.astype(f32), cuts, axis=-1)
    xw = xw.reshape(B, S, N_DIR, DECAY_LORA)
    xa = xa.reshape(B, S, N_DIR, ICLR_LORA)
    w_log = -jax.nn.softplus(-(w0.astype(f32) + jnp.einsum('bsdr,drc->bsdc', jnp.tanh(xw), w2.astype(f32)))) - 0.5
    decay = jnp.exp(-jnp.exp(w_log))
    a = jax.nn.sigmoid(a0.astype(f32) + jnp.einsum('bsdr,drc->bsdc', xa, a2.astype(f32)))
    g = jax.nn.sigmoid(xg) @ g2.astype(f32)
    kk = (k * k_k.astype(f32)).reshape(B, S, H, N)
    kk = (kk * lax.rsqrt(jnp.maximum(jnp.sum(kk * kk, axis=-1, keepdims=True), 1e-24))).reshape(B, S, C)
    k_dir = k[:, :, None, :] * (1.0 + (a - 1.0) * k_a.astype(f32))
    shared = lambda t: jnp.broadcast_to(t[:, :, None, :], (B, S, N_DIR, C))
    inputs = tuple(_dirs_time_major(t) for t in
                   (shared(r), decay, k_dir, shared(v), shared(-kk), kk[:, :, None, :] * a))
    state0 = jnp.zeros((N_DIR, B, H, N, N), f32)
    _, o = lax.scan(_wkv7_step, state0, inputs)
    o = (o[:, 0] + o[::-1, 1]).transpose(1, 0, 2, 3)
    mu = jnp.mean(o, axis=-1, keepdims=True)
    var = jnp.mean(jnp.square(o - mu), axis=-1, keepdims=True)
    o = ((o - mu) * lax.rsqrt(var + LN_X_EPS)).reshape(B, S, C) * lnx_w.astype(f32) + lnx_b.astype(f32)
    bonus = jnp.sum((r * jnp.sum(k_dir, axis=2)).reshape(B, S, H, N) * r_k.astype(f32), axis=-1, keepdims=True)
    bonus = (bonus * v.reshape(B, S, H, N)).reshape(B, S, C)
    return ((o + bonus) * g).astype(z.dtype)


def mla_bidirectional(z, positions, q_norm, w_uq, kv_norm, w_ukv):
    B, S, _ = z.shape
    c_q, c_kv, k_rope = jnp.split(z, [Q_LORA_RANK, Q_LORA_RANK + KV_LORA_RANK], axis=-1)
    q = (rms_norm(c_q, q_norm) @ w_uq).reshape(B, S, MLA_HEADS, QK_NOPE_DIM + QK_ROPE_DIM)
    q_nope, q_rope = jnp.split(q, [QK_NOPE_DIM], axis=-1)
    kv = (rms_norm(c_kv, kv_norm) @ w_ukv).reshape(B, S, MLA_HEADS, QK_NOPE_DIM + V_HEAD_DIM)
    k_nope, v = jnp.split(kv, [QK_NOPE_DIM], axis=-1)
    cos, sin = rope_tables(positions)
    q_rope = apply_rope(q_rope, cos[:, :, None, :], sin[:, :, None, :])
    k_rope = apply_rope(k_rope, cos, sin)
    n_blocks = S // Q_BLOCK
    scale = (QK_NOPE_DIM + QK_ROPE_DIM) ** -0.5

    def to_blocks(t):
        return t.reshape(B, n_blocks, Q_BLOCK, t.shape[2], t.shape[3]).swapaxes(0, 1)

    def attend_block(qb):
        qn, qr = qb
        s = jnp.einsum('bqhd,bkhd->bhqk', qn, k_nope) + jnp.einsum('bqhr,bkr->bhqk', qr, k_rope)
        p = jax.nn.softmax(s.astype(jnp.float32) * scale, axis=-1).astype(v.dtype)
        return jnp.einsum('bhqk,bkhd->bqhd', p, v)

    o = lax.map(attend_block, (to_blocks(q_nope), to_blocks(q_rope)))
    return o.swapaxes(0, 1).reshape(B, S, MLA_WIDTH)


def memory_cross_attention(h, mem, g_mem, wq, wkv, wo):
    B, S, _ = h.shape
    m = rms_norm(mem, g_mem)
    q = (h @ wq).reshape(B, S, MEM_HEADS, MEM_HEAD_DIM)
    k, v = jnp.split(m @ wkv, 2, axis=-1)
    k = k.reshape(B, MEM_TOKENS, MEM_HEADS, MEM_HEAD_DIM)
    v = v.reshape(B, MEM_TOKENS, MEM_HEADS, MEM_HEAD_DIM)
    s = jnp.einsum('bqhd,bkhd->bhqk', q, k).astype(jnp.float32) * (MEM_HEAD_DIM ** -0.5)
    p = jax.nn.softmax(s, axis=-1).astype(v.dtype)
    o = jnp.einsum('bhqk,bkhd->bqhd', p, v).reshape(B, S, MEM_HEADS * MEM_HEAD_DIM)
    return o @ wo


def setup_inputs(seed: int = 0) -> dict:
    key = jax.random.key(seed)
    keys = iter(jax.random.split(key, 40))
    L, D, C = DEPTH, D_MODEL, RWKV_WIDTH

    def nrm(shape, scale):
        return scale * jax.random.normal(next(keys), shape, jnp.float32)

    def gain(n):
        return 1.0 + nrm((L, n), 0.05)

    x = nrm((BATCH, SEQ, D), 1.0)
    mem = nrm((BATCH, MEM_TOKENS, D), 1.0)
    positions = (jnp.cumsum(jax.random.randint(next(keys), (BATCH, SEQ), 1, 3, dtype=jnp.int32), axis=1) - 1).astype(jnp.int32)
    return {
        'x': x,
        'mem': mem,
        'positions': positions,
        'norm_mix_pre': gain(D),
        'w_in': nrm((L, D, D_IN), D ** -0.5),
        'conv_rwkv': jnp.array([0.25, 0.5, 0.25], jnp.float32)[None, :, None] + nrm((L, 3, RWKV_COLS), 0.05),
        'rwkv_w0': jax.random.uniform(next(keys), (L, N_DIR, C), jnp.float32, -6.0, 2.0),
        'rwkv_w2': nrm((L, N_DIR, DECAY_LORA, C), 0.1 * DECAY_LORA ** -0.5),
        'rwkv_a0': nrm((L, N_DIR, C), 0.5),
        'rwkv_a2': nrm((L, N_DIR, ICLR_LORA, C), ICLR_LORA ** -0.5),
        'rwkv_g2': nrm((L, GATE_LORA, C), GATE_LORA ** -0.5),
        'rwkv_k_k': 0.85 + nrm((L, C), 0.05),
        'rwkv_k_a': 1.0 + nrm((L, C), 0.05),
        'rwkv_r_k': nrm((L, RWKV_HEADS, RWKV_HEAD_DIM), 0.1),
        'rwkv_lnx_w': gain(C),
        'rwkv_lnx_b': nrm((L, C), 0.02),
        'mla_q_norm': gain(Q_LORA_RANK),
        'mla_w_uq': nrm((L, Q_LORA_RANK, MLA_HEADS * (QK_NOPE_DIM + QK_ROPE_DIM)), Q_LORA_RANK ** -0.5),
        'mla_kv_norm': gain(KV_LORA_RANK),
        'mla_w_ukv': nrm((L, KV_LORA_RANK, MLA_HEADS * (QK_NOPE_DIM + V_HEAD_DIM)), KV_LORA_RANK ** -0.5),
        'w_out': nrm((L, D_MIX, D), D_MIX ** -0.5),
        'norm_mix_post': gain(D),
        'norm_mem_pre': gain(D),
        'norm_memtok': gain(D),
        'mem_wq': nrm((L, D, MEM_HEADS * MEM_HEAD_DIM), D ** -0.5),
        'mem_wkv': nrm((L, D, 2 * MEM_HEADS * MEM_HEAD_DIM), D ** -0.5),
        'mem_wo': nrm((L, MEM_HEADS * MEM_HEAD_DIM, D), (MEM_HEADS * MEM_HEAD_DIM) ** -0.5),
        'norm_mem_post': gain(D),
        'norm_mlp_pre': gain(D),
        'mlp_w1': nrm((L, D, D_FF), D ** -0.5),
        'mlp_w2': nrm((L, D_FF, D), D_FF ** -0.5),
        'norm_mlp_post': gain(D),
    }


def reference(x, mem, positions, norm_mix_pre, w_in, conv_rwkv, rwkv_w0, rwkv_w2, rwkv_a0, rwkv_a2,
              rwkv_g2, rwkv_k_k, rwkv_k_a, rwkv_r_k, rwkv_lnx_w, rwkv_lnx_b, mla_q_norm, mla_w_uq,
              mla_kv_norm, mla_w_ukv, w_out, norm_mix_post, norm_mem_pre, norm_memtok, mem_wq, mem_wkv,
              mem_wo, norm_mem_post, norm_mlp_pre, mlp_w1, mlp_w2, norm_mlp_post):
    for l in range(DEPTH):
        h = rms_norm(x, norm_mix_pre[l])
        z = h @ w_in[l]
        z_rwkv, z_mla = z[..., :RWKV_COLS], z[..., RWKV_COLS:]
        y_rwkv = rwkv7_bidirectional(short_conv(z_rwkv, conv_rwkv[l]), rwkv_w0[l], rwkv_w2[l], rwkv_a0[l],
                                     rwkv_a2[l], rwkv_g2[l], rwkv_k_k[l], rwkv_k_a[l], rwkv_r_k[l],
                                     rwkv_lnx_w[l], rwkv_lnx_b[l])
        y_mla = mla_bidirectional(z_mla, positions, mla_q_norm[l], mla_w_uq[l], mla_kv_norm[l], mla_w_ukv[l])
        y = jnp.concatenate([y_rwkv, y_mla], axis=-1) @ w_out[l]
        x = x + rms_norm(y, norm_mix_post[l])
        h = rms_norm(x, norm_mem_pre[l])
        y = memory_cross_attention(h, mem, norm_memtok[l], mem_wq[l], mem_wkv[l], mem_wo[l])
        x = x + rms_norm(y, norm_mem_post[l])
        h = rms_norm(x, norm_mlp_pre[l])
        y = jnp.square(jax.nn.relu(h @ mlp_w1[l])) @ mlp_w2[l]
        x = x + rms_norm(y, norm_mlp_post[l])
    return x
```

```python
import contextlib
import math
import types
import numpy as np
import concourse.bass as bass
import concourse.mybir as mybir
from concourse.bass_utils import run_bass_kernel_spmd

F32 = mybir.dt.float32
BF16 = mybir.dt.bfloat16
I32 = mybir.dt.int32
AF = mybir.ActivationFunctionType
ALU = mybir.AluOpType

T = 2048
NB = 16
D = 1024
STRICT = False
COMPUTE = ("pe", "act", "dve", "pool")
QUEUES = ("qsp", "qpool")


class Op:
    __slots__ = ("eng", "fn", "reads", "writes", "idx", "deps", "sig", "sem", "val", "pos", "alias")

    def __init__(self, eng, fn, reads, writes):
        self.eng, self.fn, self.reads, self.writes = eng, fn, reads, writes
        self.deps = set()
        self.sig = False
        self.sem = None
        self.val = 0
        self.alias = None


class Sched:
    def __init__(self, nc, n_dma_sems=20):
        self.nc = nc
        self.ops = []
        self.n_dma_sems = n_dma_sems

    @staticmethod
    def _freeze(fn):
        if fn is None or getattr(fn, "__closure__", None) is None:
            return fn
        cells = []
        for c in fn.__closure__:
            try:
                cells.append(types.CellType(c.cell_contents))
            except ValueError:
                cells.append(c)
        return types.FunctionType(fn.__code__, fn.__globals__, fn.__name__, fn.__defaults__, tuple(cells))

    def add(self, eng, fn, reads=(), writes=()):
        fn = self._freeze(fn)
        op = Op(eng, fn, tuple(reads), tuple(writes))
        (self._cap if self._cap is not None else self.ops).append(op)
        return op

    _cap = None

    def capture(self, emit_fn):
        prev = self._cap
        self._cap = []
        try:
            emit_fn()
        finally:
            out, self._cap = self._cap, prev
        return out

    def extend(self, ops):
        (self._cap if self._cap is not None else self.ops).extend(ops)

    @staticmethod
    def merge(a, b, bfrac=1.0):
        out, i, j = [], 0, 0
        na, nb = len(a), len(b)
        while i < na or j < nb:
            if j >= nb or (i < na and i * nb * bfrac <= j * na):
                out.append(a[i]); i += 1
            else:
                out.append(b[j]); j += 1
        return out

    def pe(self, fn, r, w): return self.add("pe", fn, r, w)
    def act(self, fn, r, w): return self.add("act", fn, r, w)
    def dve(self, fn, r, w): return self.add("dve", fn, r, w)
    def pool(self, fn, r, w): return self.add("pool", fn, r, w)

    def dma(self, q, out, in_, r, w):
        return self.add(q, lambda e: e.dma_start(out=out, in_=in_), r, w)

    def alias(self, new_key, old_keys):
        op = self.add("alias", None, (), ())
        op.alias = (new_key, tuple(old_keys))

    def build(self, final_wait_ops=()):
        nc = self.nc
        ops = self.ops
        for i_, op in enumerate(ops):
            op.idx = i_
        writers, readers = {}, {}
        eng_pos = {e: 0 for e in COMPUTE + QUEUES}
        for op in ops:
            if op.eng == "alias":
                continue
            op.pos = eng_pos[op.eng]
            eng_pos[op.eng] += 1
        pending = {}
        allkeys = set()

        def first_use(b):
            if b in allkeys:
                return
            allkeys.add(b)
            for nk, (w, r) in pending.items():
                if b.startswith(nk):
                    writers[b] = list(writers.get(b, [])) + w
                    readers[b] = list(readers.get(b, [])) + r

        for op in ops:
            if op.eng == "alias":
                nk, olds = op.alias
                w, r = [], []
                for k in list(allkeys):
                    if any(k.startswith(o) for o in olds):
                        w += writers.get(k, [])
                        r += readers.get(k, [])
                for o in olds:
                    if o in pending:
                        w += pending[o][0]
                        r += pending[o][1]
                pending[nk] = (w, r)
                for k in list(allkeys):
                    if k.startswith(nk):
                        allkeys.discard(k)
                        writers.pop(k, None)
                        readers.pop(k, None)
                continue
            for b in op.reads + op.writes:
                first_use(b)
            for b in op.reads:
                for lw in writers.get(b, ()):
                    op.deps.add(lw.idx)
            for b in op.writes:
                for lw in writers.get(b, ()):
                    op.deps.add(lw.idx)
                for rd in readers.get(b, ()):
                    op.deps.add(rd.idx)
            for b in op.reads:
                readers.setdefault(b, []).append(op)
            for b in op.writes:
                writers[b] = [op]
                readers[b] = []
            op.deps.discard(op.idx)
        dma_last = {}
        qcount = {q: 0 for q in QUEUES}
        for op in ops:
            if op.eng in QUEUES:
                slot = (op.eng, qcount[op.eng] % self.n_dma_sems)
                qcount[op.eng] += 1
                prev = dma_last.get(slot)
                if prev is not None:
                    op.deps.add(prev.idx)
                dma_last[slot] = op
                op.sem = slot
                op.sig = True
        for op in ops:
            if op.eng == "alias":
                continue
            keep = set()
            for di in op.deps:
                d = ops[di]
                if d.eng == op.eng and op.eng in COMPUTE:
                    if op.eng == "pe":
                        continue
                    raw = any(b in d.writes for b in op.reads)
                    if not STRICT and (not raw or (op.pos - d.pos) > 3):
                        continue
                keep.add(di)
                d.sig = True
            op.deps = keep
        for fo in final_wait_ops:
            fo.sig = True
        cnt = {e: 0 for e in COMPUTE}
        dcnt = {}
        for op in ops:
            if op.eng == "alias" or not op.sig:
                continue
            if op.eng in COMPUTE:
                cnt[op.eng] += 1
                op.sem = op.eng
                op.val = cnt[op.eng]
            else:
                dcnt[op.sem] = dcnt.get(op.sem, 0) + 16
                op.val = dcnt[op.sem]
        self.stats = dict(eng_pos)
        with contextlib.ExitStack() as es:
            sems = {}
            for e in COMPUTE:
                sems[e] = es.enter_context(nc.semaphore("s_" + e))
            for q in QUEUES:
                for i in range(self.n_dma_sems):
                    sems[(q, i)] = es.enter_context(nc.semaphore("d_%s_%d" % (q, i)))
            block = es.enter_context(nc.Block())

            def emit(engnames, eng, final=False):
                known = {}
                for op in ops:
                    if op.eng not in engnames:
                        continue
                    need = {}
                    for di in op.deps:
                        d = ops[di]
                        if need.get(d.sem, 0) < d.val:
                            need[d.sem] = d.val
                    for s, v in need.items():
                        if known.get(s, 0) >= v:
                            continue
                        eng.wait_ge(sems[s], v)
                        known[s] = v
                    ins = op.fn(eng)
                    if op.sig:
                        ins.then_inc(sems[op.sem], 16 if op.eng in QUEUES else 1)
                if final:
                    for fo in final_wait_ops:
                        if known.get(fo.sem, 0) < fo.val:
                            eng.wait_ge(sems[fo.sem], fo.val)
                            known[fo.sem] = fo.val

            @block.tensor
            def _(e): emit(("pe",), e)

            @block.scalar
            def _(e): emit(("act",), e)

            @block.vector
            def _(e): emit(("dve",), e)

            @block.gpsimd
            def _(e): emit(("pool", "qpool"), e)

            @block.sync
            def _(e): emit(("qsp",), e, final=True)


def _pcol_names():
    names = []
    for ci in range(15):
        for tap in range(3):
            names.append("conv%d_%d" % (ci, tap))
    for hp in range(4):
        names += ["kk%d" % hp, "ka%d" % hp, "lnw%d" % hp, "lnb%d" % hp, "rk%d" % hp]
        for d in range(2):
            names += ["w0_%d_%d" % (d, hp), "a0_%d_%d" % (d, hp)]
    names += ["invf", "sgn"]
    for kc in range(8):
        names += ["gm%d" % kc, "gt%d" % kc, "gp%d" % kc]
    return names


PCN = _pcol_names()
PCI = {n: i for i, n in enumerate(PCN)}
NPC = len(PCN)


def _pack_pcols(inp):
    pc = np.zeros((128, NPC), np.float32)
    conv = inp["conv_rwkv"][0]
    for ci in range(15):
        if ci < 12:
            cols = np.arange(ci * 128, ci * 128 + 128)
        elif ci == 12:
            cols = np.arange(1536, 1600)
        elif ci == 13:
            cols = np.arange(1600, 1664)
        else:
            cols = np.arange(1664, 1760)
        for tap in range(3):
            pc[: len(cols), PCI["conv%d_%d" % (ci, tap)]] = conv[tap, cols]
    for hp in range(4):
        sl = slice(hp * 128, hp * 128 + 128)
        pc[:, PCI["kk%d" % hp]] = inp["rwkv_k_k"][0][sl]
        pc[:, PCI["ka%d" % hp]] = inp["rwkv_k_a"][0][sl]
        pc[:, PCI["lnw%d" % hp]] = inp["rwkv_lnx_w"][0][sl]
        pc[:, PCI["lnb%d" % hp]] = inp["rwkv_lnx_b"][0][sl]
        pc[:, PCI["rk%d" % hp]] = inp["rwkv_r_k"][0].reshape(-1)[sl]
        for d in range(2):
            pc[:, PCI["w0_%d_%d" % (d, hp)]] = inp["rwkv_w0"][0][d][sl]
            pc[:, PCI["a0_%d_%d" % (d, hp)]] = inp["rwkv_a0"][0][d][sl]
    invf = (10000.0 ** (-np.arange(0, 32, 2, dtype=np.float32) / 32.0)).astype(np.float32)
    pc[64:80, PCI["invf"]] = invf
    pc[80:96, PCI["invf"]] = invf
    pc[64:80, PCI["sgn"]] = -1.0
    pc[80:96, PCI["sgn"]] = 1.0
    for kc in range(8):
        sl = slice(kc * 128, kc * 128 + 128)
        pc[:, PCI["gm%d" % kc]] = inp["norm_mem_pre"][0][sl]
        pc[:, PCI["gt%d" % kc]] = inp["norm_memtok"][0][sl]
        pc[:, PCI["gp%d" % kc]] = inp["norm_mlp_pre"][0][sl]
    return pc


def _consts():
    c = np.zeros((128, 1024), np.float32)
    c[:, 0:128] = np.eye(128)
    t = np.arange(128)[:, None]
    s = np.arange(128)[None, :]
    same = (t // 64) == (s // 64)
    c[:, 128:256] = same & (s > t)
    c[:, 256:384] = same & (s >= t)
    c[:, 384:512] = same & (s < t)
    c[:, 512:640] = same & (s <= t)
    c[:, 640:768] = same / 64.0
    c[:, 768:896] = same * 1.0
    c[:, 896:960] = np.eye(64)[np.arange(128) % 64]
    return c


def _host_inputs(inp, b):
    f = lambda a: np.ascontiguousarray(a, dtype=np.float32)
    w_in = inp["w_in"][0]
    w_in_x = np.concatenate([w_in[:, :1760], w_in[:, 1760:2144], w_in[:, 2144:2176],
                             w_in[:, 2160:2176], w_in[:, 2144:2160]], axis=1)
    w_uq = inp["mla_w_uq"][0].reshape(256, 8, 96)
    w_uq_sw = np.concatenate([w_uq[:, :, 80:96], w_uq[:, :, 64:80]], axis=2).reshape(256, 256)
    w_ukv = inp["mla_w_ukv"][0].reshape(128, 8, 128)
    w_uk = w_ukv[:, :, :64].reshape(128, 512)
    w_uv = w_ukv[:, :, 64:].reshape(128, 512)
    gains = np.stack([inp[n][0] for n in ["norm_mix_pre", "norm_mix_post", "norm_mem_pre", "norm_memtok",
                                         "norm_mem_post", "norm_mlp_pre", "norm_mlp_post"]], 0)
    gq = np.zeros((1, 1024), np.float32)
    gq[0, :256] = inp["mla_q_norm"][0]
    gq[0, 256:384] = inp["mla_kv_norm"][0]
    gains = np.concatenate([gains, gq], 0)
    return {
        "x": f(inp["x"][b]), "mem": f(inp["mem"][b]),
        "pos": np.ascontiguousarray(inp["positions"][b].reshape(1, T)).astype(np.int32),
        "pcols": _pack_pcols(inp), "consts": _consts(), "gains": f(gains),
        "w_in": f(w_in_x), "w2": f(inp["rwkv_w2"][0].reshape(64, 512)), "a2": f(inp["rwkv_a2"][0].reshape(64, 512)),
        "g2": f(inp["rwkv_g2"][0]), "w_uq": f(inp["mla_w_uq"][0]), "w_uq_sw": f(w_uq_sw),
        "w_uk": f(w_uk), "w_uv": f(w_uv), "w_out": f(inp["w_out"][0]),
        "mem_wq": f(inp["mem_wq"][0]), "mem_wkv": f(inp["mem_wkv"][0]), "mem_wo": f(inp["mem_wo"][0]),
        "mlp_w1": f(inp["mlp_w1"][0]), "mlp_w2": f(inp["mlp_w2"][0]),
    }


DRAM_SHAPES = {
    "x": ([T, D], F32), "mem": ([256, D], F32), "pos": ([1, T], I32), "pcols": ([128, NPC], F32),
    "consts": ([128, 1024], F32), "gains": ([8, 1024], F32), "w_in": ([1024, 2208], F32),
    "w2": ([64, 512], F32), "a2": ([64, 512], F32), "g2": ([96, 512], F32), "w_uq": ([256, 768], F32),
    "w_uq_sw": ([256, 256], F32), "w_uk": ([128, 512], F32), "w_uv": ([128, 512], F32),
    "w_out": ([1024, 1024], F32), "mem_wq": ([1024, 1024], F32), "mem_wkv": ([1024, 2048], F32),
    "mem_wo": ([1024, 1024], F32), "mlp_w1": ([1024, 4096], F32), "mlp_w2": ([4096, 1024], F32),
}

ARENA_COLS = 106200


class _Stop(Exception):
    pass


class Builder:
    def tick(self):
        self._ticks = getattr(self, "_ticks", 0) + 1
        if self._ticks >= getattr(self, "gstop", 10 ** 9):
            for (nm, ap, keys, shp) in getattr(self, "stop_taps", []):
                self.tap(nm, ap, keys, shp)
            raise _Stop()

    def tick2(self):
        self._ticks2 = getattr(self, "_ticks2", 0) + 1
        if self._ticks2 >= getattr(self, "bstop", 10 ** 9):
            raise _Stop()

    def __init__(self, stage=99, taps=()):
        self.stage = stage
        self.taps = taps
        self.nc = bass.Bass("TRN2", target_bir_lowering=False)
        nc = self.nc
        self.dr = {k: nc.dram_tensor(k, s, dt, kind="ExternalInput").ap() for k, (s, dt) in DRAM_SHAPES.items()}
        self.out = nc.dram_tensor("out", [T, D], F32, kind="ExternalOutput").ap()
        self.dbg = {}
        self.es = contextlib.ExitStack()
        self.arena = self.es.enter_context(nc.sbuf_tensor("arena", [128, ARENA_COLS], BF16))
        self.PS = [self.es.enter_context(nc.psum_tensor("ps%d" % i, [128, 1024], F32)) for i in range(4)]
        self.S = Sched(nc)
        self.off = 0
        self.live = {}
        self.final_ops = []
        self.uid = 0

    def alloc(self, key, ncols, dt):
        sz = 2 if dt == F32 else 1
        if dt == F32 and self.off % 2:
            self.off += 1
        st, en = self.off, self.off + ncols * sz
        assert en <= ARENA_COLS, ("arena overflow", key, en)
        self.off = en
        olds = [k for k, (a, b) in self.live.items() if a < en and st < b and k != key]
        if olds:
            self.S.alias(key, olds)
        self.live[key] = (st, en)
        ap = self.arena[:, st:en]
        if dt != BF16:
            ap = ap.bitcast(dt)
        return ap

    def mark(self): return self.off
    def reset(self, m): self.off = m

    def tap(self, name, ap, key, shape):
        if name not in self.taps:
            return
        nc = self.nc
        dt = ap.dtype
        o = nc.dram_tensor("dbg_" + name, list(shape), dt, kind="ExternalOutput").ap()
        op = self.S.dma("qsp", o, ap, list(key) if isinstance(key, (list, tuple)) else [key], [])
        self.final_ops.append(op)
        self.dbg[name] = (shape, dt)


def v3(ap, **kw):
    (k, val), = kw.items()
    return ap.rearrange("p (a %s) -> p a %s" % (k, k), **{k: val})


def _phase_A(self):
    S, nc, dr = self.S, self.nc, self.dr
    al = self.alloc
    self.CB = al("CB", 1024, BF16)
    S.dma("qpool", self.CB, dr["consts"], [], ["CB"])
    self.IDENT = self.CB[:, 0:128]
    self.PC = al("PC", NPC, F32)
    S.dma("qsp", self.PC, dr["pcols"], [], ["PC"])
    self.NPCt = al("NPC", NPC, F32)
    S.dve(lambda e: e.tensor_single_scalar(out=self.NPCt, in_=self.PC, scalar=-1.0, op=ALU.mult), ["PC"], ["NPC"])
    self.CT = al("CT", 8, F32)
    for i, v in enumerate([1.0, 1e-6, 64e-5, -math.pi, -0.5, 0.0, 1e-24]):
        S.pool(lambda e, i=i, v=v: e.memset(self.CT[:, i:i + 1], v), [], ["CT"])
    self.ONE, self.EPS6, self.EPSLN, self.NEGPI, self.NEGH, self.ZERO = (self.CT[:, i:i + 1] for i in range(6))
    self.bankrr = 0
    self.COS = al("COS", T, BF16)
    self.SINS = al("SINS", T, BF16)
    self.CQKV3 = CQKV3 = v3(al("CQKV", 16 * 384, F32), n=384)
    self.KR3 = KR3 = v3(al("KR", 2 * T, BF16), t=T)
    ZC = al("ZC", 15 * T, BF16)
    self.ZC3 = ZC3 = v3(ZC, t=T)
    self.mA = self.mark()
    tab_ops = S.capture(self.rope_tables)
    ZR3 = v3(al("ZR", 2 * T, BF16), t=T)
    TMP = [al("CTMP%d" % i, T, F32) for i in range(2)]
    HT = al("HT", 8 * T, BF16)
    self.HT3 = HT3 = v3(HT, t=T)
    XB = [al("XB%d" % i, 1024, F32) for i in range(2)]
    HB = [al("HB%d" % i, 1024, BF16) for i in range(2)]
    JUNK = al("JUNK", 1024, BF16)
    SS = al("SS", 32, F32)
    G1 = al("G1", 1024, F32)
    S.dma("qsp", G1, dr["gains"][0:1, :].partition_broadcast(128), [], ["G1"])
    def norm_loop():
        for i in range(NB):
            xb, kx = XB[i % 2], "XB%d" % (i % 2)
            hb, kh = HB[i % 2], "HB%d" % (i % 2)
            S.dma("qsp", xb, dr["x"][i * 128:(i + 1) * 128, :], [], [kx])
            self.rms_to_bf16(xb, kx, hb, kh, G1, "G1", SS[:, i:i + 1], SS[:, 16 + i:17 + i], "SS%d" % i, JUNK, 1024)
            self.transpose_to(hb, kh, 8, lambda c: HT3[:, c, i * 128:(i + 1) * 128], HT3[:, :, i * 128:(i + 1) * 128], "HT%d" % i, i)

    S.extend(Sched.merge(S.capture(norm_loop), tab_ops))
    WB = [v3(al("WB%d" % i, 8 * 512, BF16), n=512) for i in range(2)]
    groups = [(0, 512), (512, 512), (1024, 512), (1536, 224), (1760, 384), (2144, 64)]
    wsrc = dr["w_in"].rearrange("(kc p) n -> p kc n", p=128)
    ev = 0
    for gi, (c0, ncol) in enumerate(groups):
        wb, kw = WB[gi % 2], "WB%d" % (gi % 2)
        S.dma("qpool", wb[:, :, 0:ncol], wsrc[:, :, c0:c0 + ncol], [], [kw])
        if gi < 4:
            chunks = [(gi * 4 + j, j * 128, 128) for j in range(4)] if gi < 3 else [(12, 0, 64), (13, 64, 64), (14, 128, 96)]
            for (ci, co, m) in chunks:
                for tg in range(4):
                    pb, kb = self.bank()
                    for kc in range(8):
                        S.pe(lambda e, pb=pb, kc=kc, co=co, m=m, tg=tg, wb=wb: e.matmul(
                            pb[0:m, :], lhsT=wb[:, kc, co:co + m], rhs=HT3[:, kc, tg * 512:(tg + 1) * 512],
                            start=(kc == 0), stop=(kc == 7)), [kw] + ["HT%d" % (4 * tg + q) for q in range(4)], [kb])
                    self.evac_copy(ZR3[0:m, ci % 2, tg * 512:(tg + 1) * 512], pb[0:m, :], kb, "ZR%d_%d" % (ci % 2, tg), ev)
                    ev += 1
                self.conv_chunk(ci, m, ZR3, ZC3, TMP)
        elif gi == 4:
            for i in range(NB):
                pb, kb = self.bank()
                for kc in range(8):
                    S.pe(lambda e, pb=pb, kc=kc, i=i, wb=wb: e.matmul(
                        pb[:, 0:384], lhsT=HT3[:, kc, i * 128:(i + 1) * 128], rhs=wb[:, kc, 0:384],
                        start=(kc == 0), stop=(kc == 7)), [kw, "HT%d" % i], [kb])
                self.evac_copy(CQKV3[:, i, :], pb[:, 0:384], kb, "CQKV%d" % i, ev)
                ev += 1
        else:
            for tg in range(4):
                for s in range(2):
                    pb, kb = self.bank()
                    for kc in range(8):
                        S.pe(lambda e, pb=pb, kc=kc, s=s, tg=tg, wb=wb: e.matmul(
                            pb[64:96, :], lhsT=wb[:, kc, 32 * s:32 * s + 32], rhs=HT3[:, kc, tg * 512:(tg + 1) * 512],
                            start=(kc == 0), stop=(kc == 7)), [kw] + ["HT%d" % (4 * tg + q) for q in range(4)], [kb])
                    self.evac_copy(KR3[64:96, s, tg * 512:(tg + 1) * 512], pb[64:96, :], kb, "KR", ev)
                    ev += 1
    self.tap("ZC", ZC[:, 0:12 * T], ["ZC%d" % i for i in range(12)], [128, 12 * T])
    self.tap("CQKV", self.CQKV3.rearrange("p a n -> p (a n)"), ["CQKV%d" % i for i in range(16)], [128, 16 * 384])
    self.tap("KR", self.KR3[64:96].rearrange("p a n -> p (a n)"), "KR", [32, 2 * T])


def _conv_chunk(self, ci, m, ZR3, ZC3, TMP):
    S = self.S
    tmp, kt = TMP[ci % 2], "CTMP%d" % (ci % 2)
    zr = ZR3[0:m, ci % 2, :]
    rk = ["ZR%d_%d" % (ci % 2, tg) for tg in range(4)]
    c0, c1, c2 = (self.PC[0:m, PCI["conv%d_%d" % (ci, t)]:PCI["conv%d_%d" % (ci, t)] + 1] for t in range(3))
    S.act(lambda e: e.activation(out=tmp[0:m, :], in_=zr, func=AF.Copy, scale=c1), rk + ["PC"], [kt])
    S.dve(lambda e: e.scalar_tensor_tensor(
        out=tmp[0:m, 1:T], in0=zr[:, 0:T - 1], scalar=c0, in1=tmp[0:m, 1:T], op0=ALU.mult, op1=ALU.add), rk + ["PC", kt], [kt])
    S.dve(lambda e: e.scalar_tensor_tensor(
        out=ZC3[0:m, ci, 0:T - 1], in0=zr[:, 1:T], scalar=c2, in1=tmp[0:m, 0:T - 1], op0=ALU.mult, op1=ALU.add),
        rk + ["PC", kt], ["ZC%d" % ci])
    S.act(lambda e: e.activation(out=ZC3[0:m, ci, T - 1:T], in_=tmp[0:m, T - 1:T], func=AF.Copy), [kt], ["ZC%d" % ci])


Builder.conv_chunk = _conv_chunk


def _rope_tables(self):
    S, dr = self.S, self.dr
    al = self.alloc
    R = slice(64, 96)
    NQ = 4
    HW = T // NQ
    POSF = al("RPOS", T, F32)
    TY = al("RTY", HW, F32); TY2 = al("RTY2", HW, F32); KF = al("RKF", HW, F32)
    TI = TY.bitcast(I32)
    S.dma("qpool", POSF, dr["pos"].partition_broadcast(128), [], ["RPOS"])
    invf = self.PC[R, PCI["invf"]:PCI["invf"] + 1]
    sgn = self.PC[R, PCI["sgn"]:PCI["sgn"] + 1]
    S.dve(lambda e: e.tensor_scalar(out=POSF[R, :], in0=POSF[R, :], scalar1=invf, scalar2=None, op0=ALU.mult), ["RPOS", "PC"], ["RPOS"])
    for hf in range(NQ):
        cs = slice(hf * HW, (hf + 1) * HW)
        for (dst, kd, shift) in ((self.SINS, "SINS", 0.0), (self.COS, "COS", 0.25)):
            S.act(lambda e, shift=shift, cs=cs: e.activation(out=KF[R, :], in_=POSF[R, cs], func=AF.Copy, scale=1.0 / (2 * math.pi)), ["RPOS"], ["RKF"])
            if shift:
                S.dve(lambda e, shift=shift: e.tensor_single_scalar(out=KF[R, :], in_=KF[R, :], scalar=shift, op=ALU.add), ["RKF"], ["RKF"])
            S.dve(lambda e: e.tensor_copy(out=TI[R, :], in_=KF[R, :]), ["RKF"], ["RTY"])
            S.dve(lambda e: e.tensor_copy(out=TY2[R, :], in_=TI[R, :]), ["RTY"], ["RTY2"])
            S.dve(lambda e: e.tensor_tensor(out=KF[R, :], in0=KF[R, :], in1=TY2[R, :], op=ALU.subtract), ["RKF", "RTY2"], ["RKF"])
            S.dve(lambda e: e.tensor_single_scalar(out=TY2[R, :], in_=KF[R, :], scalar=0.5, op=ALU.is_gt), ["RKF"], ["RTY2"])
            S.dve(lambda e: e.tensor_tensor(out=KF[R, :], in0=KF[R, :], in1=TY2[R, :], op=ALU.subtract), ["RKF", "RTY2"], ["RKF"])
            S.dve(lambda e: e.tensor_single_scalar(out=TY2[R, :], in_=KF[R, :], scalar=-0.5, op=ALU.is_lt), ["RKF"], ["RTY2"])
            S.dve(lambda e: e.tensor_tensor(out=KF[R, :], in0=KF[R, :], in1=TY2[R, :], op=ALU.add), ["RKF", "RTY2"], ["RKF"])
            S.act(lambda e, dst=dst, cs=cs: e.activation(out=dst[R, cs], in_=KF[R, :], func=AF.Sin, scale=2 * math.pi), ["RKF"], [kd + "%d" % hf])
        S.dve(lambda e, cs=cs: e.tensor_scalar(out=self.SINS[R, cs], in0=self.SINS[R, cs], scalar1=sgn, scalar2=None, op0=ALU.mult), ["SINS%d" % hf, "PC"], ["SINS%d" % hf])


Builder.rope_tables = _rope_tables


def _bank(self, nmax=8):
    if getattr(self, "bankset", None):
        return self.bank_sel(self.bankset)
    b = self.bankrr % nmax
    self.bankrr += 1
    return self.PS[b // 2][:, (b % 2) * 512:(b % 2) * 512 + 512], "B%d" % b


def _evac_copy(self, out, in_, kin, kout, parity, extra_reads=()):
    if parity % 2 == 0:
        self.S.act(lambda e: e.activation(out=out, in_=in_, func=AF.Copy), [kin] + list(extra_reads), [kout, kin])
    else:
        self.S.dve(lambda e: e.tensor_copy(out=out, in_=in_), [kin] + list(extra_reads), [kout, kin])


def _rms_to_bf16(self, xin, kx, hout, kh, G, kg, ss, rs, kss, junk, n, gcols=None, kj="JUNK"):
    S = self.S
    gsl = G if gcols is None else gcols
    S.act(lambda e: e.activation(out=junk[:, 0:n], in_=xin, func=AF.Square, accum_out=ss), [kx], [kj, kss])
    S.act(lambda e: e.activation(out=rs, in_=ss, func=AF.Ln, scale=1.0 / n, bias=self.EPS6), [kss, "CT"], [kss + "r"])
    S.act(lambda e: e.activation(out=rs, in_=rs, func=AF.Exp, scale=-0.5), [kss + "r"], [kss + "r"])
    S.dve(lambda e: e.scalar_tensor_tensor(out=hout, in0=xin, scalar=rs, in1=gsl, op0=ALU.mult, op1=ALU.mult),
          [kx, kss + "r", kg], [kh])


def _transpose_to(self, src, ksrc, nch, dst_of, dst_all, kdst, parity, rows=128):
    S = self.S
    pb, kb = self.bank()
    pbb = v3(pb.bitcast(BF16)[:, 0:nch * 128], t=128)
    for c in range(nch):
        S.pe(lambda e, c=c: e.transpose(out=pbb[:, c, :], in_=src[:, c * 128:(c + 1) * 128], identity=self.IDENT),
             [ksrc, "CB"], [kb])
    self.evac_copy(dst_all, pbb, kb, kdst, parity)


Builder.phase_A = _phase_A
def _bank_sel(self, banks):
    b = banks[self.bankrr % len(banks)]
    self.bankrr += 1
    return self.PS[b // 2][:, (b % 2) * 512:(b % 2) * 512 + 512], "B%d" % b


Builder.bank = _bank
Builder.bank_sel = _bank_sel
Builder.evac_copy = _evac_copy
Builder.rms_to_bf16 = _rms_to_bf16
Builder.transpose_to = _transpose_to


def _finish(self):
    self.S.build(final_wait_ops=self.final_ops)
    return self.nc


Builder.finish = _finish


def _alloc_at(self, key, start, ncols, dt, olds):
    sz = 2 if dt == F32 else 1
    self.S.alias(key, olds)
    self.live[key] = (start, start + ncols * sz)
    ap = self.arena[:, start:start + ncols * sz]
    return ap.bitcast(dt) if dt != BF16 else ap


def _sigmoid3(self, out, in_, kin, kout, tmp, ktmp, scale=1.0, bias=None, kb=(), final_scale=-1.0, final_bias=None):
    S = self.S
    nb = self.ZERO_r(tmp) if bias is None else bias
    S.act(lambda e: e.activation(out=tmp, in_=in_, func=AF.Exp, scale=-scale, bias=nb), [kin, "CT", "NPC"] + list(kb), [ktmp] + ([kin] if kin.startswith("B") else []))
    S.act(lambda e: e.activation(out=tmp, in_=tmp, func=AF.Ln, bias=self.ONE_r(tmp)), [ktmp, "CT"], [ktmp])
    fb = self.ZERO_r(tmp) if final_bias is None else final_bias
    S.act(lambda e: e.activation(out=out, in_=tmp, func=AF.Exp, scale=final_scale, bias=fb), [ktmp, "CT"], [kout])


def _rows(ap):
    p0 = ap.base_partition()
    return slice(p0, p0 + ap.shape[0])


Builder.alloc_at = _alloc_at
Builder.sigmoid3 = _sigmoid3
Builder.ONE_r = lambda self, ap: self.CT[_rows(ap), 0:1]
Builder.ZERO_r = lambda self, ap: self.CT[_rows(ap), 5:6]
Builder.NEGH_r = lambda self, ap: self.CT[_rows(ap), 4:5]


def _phase_B(self):
    S, nc, dr = self.S, self.nc, self.dr
    al = self.alloc
    self.reset(self.mA)
    self.YM3 = YM3 = v3(al("YM", 4 * T, BF16), t=T)
    self.mB = self.mark()
    R = slice(64, 96)
    KROT = al("KROT", T, BF16)
    COS, SINS = self.COS, self.SINS
    CQNT3 = v3(al("CQNT", 2 * T, BF16), t=T)
    CKVNT = al("CKVNT", T, BF16)
    VA4 = al("VA", 16 * 8 * 65, BF16).rearrange("p (k h c) -> p k h c", h=8, c=65)
    WQ3 = v3(al("WQ", 2 * 768, BF16), n=768)
    WQS3 = v3(al("WQS", 2 * 256, BF16), n=256)
    WUK = al("WUK", 512, BF16)
    WUV = al("WUV", 512, BF16)
    ONEF = al("ONEF", 64, F32)
    SSB = al("SSB", 64, F32)
    mt = self.mark()
    POSF = al("POSF", T, F32)
    TY = al("TY", T, F32)
    TY2 = al("TY2", T, F32)
    KROTF = al("KROTF", T, F32)
    self.reset(mt)
    S.pool(lambda e: e.memset(ONEF, 1.0), [], ["ONEF"])
    S.pool(lambda e: e.memset(SSB, 0.0), [], ["SSBz"])
    S.pool(lambda e: e.memset(VA4[:, :, :, 64:65], 1.0), [], ["VAone"])
    S.dma("qpool", WQ3, dr["w_uq"].rearrange("(kc p) n -> p kc n", p=128), [], ["WQ"])
    S.dma("qpool", WQS3, dr["w_uq_sw"].rearrange("(kc p) n -> p kc n", p=128), [], ["WQS"])
    S.dma("qpool", WUK, dr["w_uk"], [], ["WUK"])
    S.dma("qpool", WUV, dr["w_uv"], [], ["WUV"])
    S.dve(lambda e: e.tensor_tensor(out=TY[R, :], in0=self.KR3[R, 0, :], in1=COS[R, :], op=ALU.mult), ["KR", "COS0", "COS1", "COS2", "COS3", "TY"], ["TY"])
    S.dve(lambda e: e.tensor_tensor(out=POSF[R, :], in0=self.KR3[R, 1, :], in1=SINS[R, :], op=ALU.mult), ["KR", "SINS0", "SINS1", "SINS2", "SINS3", "POSF"], ["POSF"])
    S.dve(lambda e: e.tensor_tensor(out=KROT[R, :], in0=TY[R, :], in1=POSF[R, :], op=ALU.add), ["TY", "POSF"], ["KROT"])
    QT = [al("QT%d" % i, T, BF16) for i in range(2)]
    KT = [al("KT%d" % i, T, BF16) for i in range(2)]
    PT = [al("PT%d" % i, 512, BF16) for i in range(4)]
    RDEN = al("RDEN", 512, F32)
    BC = al("BCAST", 512, F32)
    TYR = al("TYR", 1024, F32)
    GQ = al("GQ", 384, F32)
    CQN = [al("CQN%d" % i, 384, BF16) for i in range(2)]
    JUNK = al("JUNKB", 384, BF16)
    S.dma("qsp", GQ, dr["gains"][7:8, 0:384].partition_broadcast(128), [], ["GQ"])
    for i in range(NB):
        cn, kc_ = CQN[i % 2], "CQN%d" % (i % 2)
        self.rms_to_bf16(self.CQKV3[:, i, 0:256], "CQKV%d" % i, cn[:, 0:256], kc_, GQ, "GQ", SSB[:, i:i + 1], SSB[:, 16 + i:17 + i],
                         "SSBq%d" % i, JUNK, 256, gcols=GQ[:, 0:256], kj="JUNKB")
        self.rms_to_bf16(self.CQKV3[:, i, 256:384], "CQKV%d" % i, cn[:, 256:384], kc_ + "b", GQ, "GQ", SSB[:, 32 + i:33 + i], SSB[:, 48 + i:49 + i],
                         "SSBk%d" % i, JUNK, 128, gcols=GQ[:, 256:384], kj="JUNKB")
        pb, kb = self.bank()
        pbb = v3(pb.bitcast(BF16)[:, 0:384], t=128)
        for c in range(3):
            S.pe(lambda e, c=c, cn=cn, pbb=pbb: e.transpose(out=pbb[:, c, :], in_=cn[:, c * 128:(c + 1) * 128], identity=self.IDENT),
                 [kc_, kc_ + "b", "CB"], [kb])
        sl = slice(i * 128, (i + 1) * 128)
        self.evac_copy(CQNT3[:, :, sl], pbb[:, 0:2, :], kb, "CQNT%d" % i, i)
        self.evac_copy(CKVNT[:, sl], pbb[:, 2, :], kb, "CKVNT%d" % i, i + 1)
    allq = ["CQNT%d" % i for i in range(NB)]
    allk = ["CKVNT%d" % i for i in range(NB)]
    for kbk in range(NB):
        pb, kb = self.bank()
        S.pe(lambda e, pb=pb, kbk=kbk: e.matmul(pb, lhsT=CKVNT[:, kbk * 128:(kbk + 1) * 128], rhs=WUV, start=True, stop=True),
             ["CKVNT%d" % kbk, "WUV"], [kb])
        self.evac_copy(VA4[:, kbk, :, 0:64], v3(pb, c=64), kb, "VA%d" % kbk, kbk)
    sc = 96.0 ** -0.5
    ev = 0
    pocnt = 0
    pending_tail = None
    pbc, kpbc = self.PS[2][:, 512:1024], "B5"

    ONES64 = self.CB[:, 768:832]
    ONES64 = al("ONES64", 64, BF16)
    S.pool(lambda e: e.memset(ONES64, 1.0), [], ["ONES64"])

    def emit_tail(po, kpo, h, qg, qs):
        S.dve(lambda e: e.reciprocal(out=BC[0:64, :], in_=po[64:128, :]), [kpo], ["BCAST", kpo])
        eo = 64 * (h % 2)
        S.dve(lambda e: e.tensor_tensor(out=YM3[eo:eo + 64, h // 2, qs], in0=po[0:64, :], in1=BC[0:64, :], op=ALU.mult),
              [kpo, "BCAST"], ["YM%d_%d" % (h, qg), kpo])

    for i in range(2):
        S.dve(lambda e, i=i: e.tensor_copy(out=KT[i][R, :], in_=KROT[R, :]), ["KROT"], ["KT%dr" % i])

    def proj(h):
        qt, kq = QT[h % 2], "QT%d" % (h % 2)
        kt, kk_ = KT[h % 2], "KT%d" % (h % 2)
        for tg in range(4):
            ts = slice(tg * 512, (tg + 1) * 512)
            tq = ["CQNT%d" % (4 * tg + q) for q in range(4)]
            tk = ["CKVNT%d" % (4 * tg + q) for q in range(4)]
            pq, kpq = self.bank_sel((0, 1, 2, 3, 4, 5))
            ps, kps = self.bank_sel((0, 1, 2, 3, 4, 5))
            pk, kpk = self.bank_sel((0, 1, 2, 3, 4, 5))
            for kc in range(2):
                S.pe(lambda e, kc=kc, pq=pq, ts=ts, h=h: e.matmul(pq[0:96, :], lhsT=WQ3[:, kc, h * 96:(h + 1) * 96], rhs=CQNT3[:, kc, ts],
                                                                  start=(kc == 0), stop=(kc == 1)), tq + ["WQ"], [kpq])
            for kc in range(2):
                S.pe(lambda e, kc=kc, ps=ps, ts=ts, h=h: e.matmul(ps[64:96, :], lhsT=WQS3[:, kc, h * 32:(h + 1) * 32], rhs=CQNT3[:, kc, ts],
                                                                  start=(kc == 0), stop=(kc == 1)), tq + ["WQS"], [kps])
            S.pe(lambda e, pk=pk, ts=ts, h=h: e.matmul(pk[0:64, :], lhsT=WUK[:, h * 64:(h + 1) * 64], rhs=CKVNT[:, ts], start=True, stop=True),
                 tk + ["WUK"], [kpk])
            self.evac_copy(qt[0:64, ts], pq[0:64, :], kpq, kq + "_%d" % tg, 1)
            S.dve(lambda e, pq=pq, ts=ts: e.tensor_tensor(out=TYR[R, 0:512], in0=pq[R, :], in1=COS[R, ts], op=ALU.mult), [kpq, "COS0", "COS1", "COS2", "COS3"], ["TYRa", kpq])
            S.dve(lambda e, ps=ps, ts=ts: e.tensor_tensor(out=TYR[R, 512:1024], in0=ps[R, :], in1=SINS[R, ts], op=ALU.mult), [kps, "SINS0", "SINS1", "SINS2", "SINS3"], ["TYRb", kps])
            S.dve(lambda e, qt=qt, ts=ts: e.tensor_tensor(out=qt[R, ts], in0=TYR[R, 0:512], in1=TYR[R, 512:1024], op=ALU.add), ["TYRa", "TYRb"], [kq + "_%d" % tg])
            self.evac_copy(kt[0:64, ts], pk[0:64, :], kpk, kk_ + "_%d" % tg, 1)

    proj(0)
    for h in range(8):
        qt, kq = QT[h % 2], "QT%d" % (h % 2)
        kt, kk_ = KT[h % 2], "KT%d" % (h % 2)
        qkeys = [kq + "_%d" % tg for tg in range(4)]
        kkeys = [kk_ + "_%d" % tg for tg in range(4)] + [kk_ + "r"]
        for qg in range(4):
            qs = slice(qg * 512, (qg + 1) * 512)
            pob = 6 + (pocnt % 2)
            pocnt += 1
            po, kpo = self.PS[3][:, (pob - 6) * 512:(pob - 6) * 512 + 512], "B%d" % pob
            scq = {}

            def issue_score(kbk, qs=qs, kt=kt, qt=qt, qkeys=qkeys, kkeys=kkeys, scq=scq):
                pscore, kpsc = self.bank_sel((0, 1, 2, 3, 4))
                S.pe(lambda e, pscore=pscore, kbk=kbk, qs=qs, kt=kt, qt=qt: e.matmul(
                    pscore, lhsT=kt[0:96, kbk * 128:(kbk + 1) * 128], rhs=qt[0:96, qs], start=True, stop=True), qkeys + kkeys, [kpsc])
                scq[kbk] = (pscore, kpsc)

            AHEAD = 3
            for kbk in range(min(AHEAD, NB)):
                issue_score(kbk)
            if pending_tail is not None:
                emit_tail(*pending_tail)
            for kbk in range(NB):
                if kbk + AHEAD < NB:
                    issue_score(kbk + AHEAD)
                pscore, kpsc = scq.pop(kbk)
                pt, kpt = PT[ev % 4], "PT%d" % (ev % 4)
                ev += 1
                S.act(lambda e, pt=pt, pscore=pscore: e.activation(out=pt, in_=pscore, func=AF.Exp, scale=sc), [kpsc], [kpt, kpsc])
                S.pe(lambda e, po=po, pt=pt, kbk=kbk, h=h: e.matmul(po[0:64, :], lhsT=VA4[:, kbk, h, 0:64], rhs=pt,
                                                                     start=(kbk == 0), stop=(kbk == NB - 1)), [kpt, "VA%d" % kbk], [kpo])
                S.pe(lambda e, po=po, pt=pt, kbk=kbk: e.matmul(po[64:128, :], lhsT=ONES64, rhs=pt,
                                                               start=(kbk == 0), stop=(kbk == NB - 1)), [kpt, "ONES64"], [kpo])
            pending_tail = (po, kpo, h, qg, qs)
            if qg == 1 and h + 1 < 8:
                proj(h + 1)
    emit_tail(*pending_tail)
    self.tap("KROT", KROT[R, :], ["KROT"], [32, T])
    self.tap("QT", QT[1][0:96, :], ["QT1_%d" % t for t in range(4)], [96, T])
    self.tap("KT", KT[1][0:96, :], ["KT1_%d" % t for t in range(4)] + ["KT1r"], [96, T])
    self.tap("COS", COS[R, :], ["COS0", "COS1", "COS2", "COS3"], [32, T])
    self.tap("SINS", SINS[R, :], ["SINS0", "SINS1", "SINS2", "SINS3"], [32, T])
    self.tap("CKVNT", CKVNT, allk, [128, T])
    self.tap("YM", self.YM3.rearrange("p a n -> p (a n)"), ["YM%d_%d" % (h, qg) for h in range(8) for qg in range(4)], [128, 4 * T])


Builder.phase_B = _phase_B


def _phase_C(self):
    S, nc, dr = self.S, self.nc, self.dr
    al = self.alloc
    if not hasattr(self, "mB"):
        self.reset(self.mA)
        self.YM3 = v3(al("YM", 4 * T, BF16), t=T)
        self.mB = self.mark()
    self.reset(self.mB)
    ZC3 = self.ZC3
    CB = self.CB
    MS0, MS1 = CB[:, 128:384], CB[:, 384:640]
    BAVG, BONES, ID64 = CB[:, 640:768], CB[:, 768:896], CB[:, 896:960]
    pcol = lambda n, rows=slice(0, 128): self.PC[rows, PCI[n]:PCI[n] + 1]
    ncol = lambda n, rows=slice(0, 128): self.NPCt[rows, PCI[n]:PCI[n] + 1]
    cq0 = self.live["CQKV"][0]
    assert self.live["KR"][1] - cq0 >= 4 * T * 2
    OACC3 = v3(self.alloc_at("OACC", cq0, 4 * T, F32, ["CQKV", "KR"]), t=T)
    W2 = al("W2", 512, BF16); A2 = al("A2", 512, BF16); G2 = al("G2", 512, BF16)
    S.dma("qpool", W2[0:64, :], dr["w2"], [], ["W2"])
    S.dma("qpool", A2[0:64, :], dr["a2"], [], ["A2"])
    S.dma("qpool", G2[0:96, :], dr["g2"], [], ["G2"])
    self.G2 = G2; self.A2 = A2
    KK3 = v3(al("KK", 4 * T, BF16), t=T)
    mC = self.mark()
    self.mC = mC
    tmps = [al("TC%d" % i, 512, F32) for i in range(7)]
    t1, t2, NLW, CP, CWN, tB, tK = tmps
    k1, k2, kNLW, kCP, kCWN, ktB, ktK = ["TC%d" % i for i in range(7)]
    self.stop_taps = [("W2", W2[0:64, :], ["W2"], [64, 512]), ("CTt", self.CT, ["CT"], [128, 8]), ("NPCt", self.NPCt, ["NPC"], [128, NPC]),
                      ("TXWg", ZC3[0:64, 12, 512:1024], ["ZC12"], [64, 512]), ("NLW", NLW, ["TC2"], [128, 512]), ("T1", t1, ["TC0"], [128, 512]), ("T2", t2, ["TC1"], [128, 512]), ("CP", CP, ["TC3"], [128, 512])]
    RESETM = al("RESETM", 512, F32)
    S.pool(lambda e: e.memset(RESETM, 1.0), [], ["RESETM"])
    S.pool(lambda e: e.memset(v3(RESETM, l=64)[:, :, 0:1], 0.0), [], ["RESETM"])
    OMK = al("OMK", 4, F32)
    for hp in range(4):
        S.dve(lambda e, hp=hp: e.tensor_scalar(out=OMK[:, hp:hp + 1], in0=pcol("ka%d" % hp), scalar1=-1.0, scalar2=1.0, op0=ALU.mult, op1=ALU.add), ["PC"], ["OMK"])
    cstop = getattr(self, "cstop", 99)
    if cstop <= 1:
        return
    XW = ZC3[0:64, 12, :]; XA = ZC3[0:64, 13, :]; XG = ZC3[0:96, 14, :]
    for tg in range(4):
        ts = slice(tg * 512, (tg + 1) * 512)
        self.sigmoid3(t1[0:64, :], XW[:, ts], "ZC12", k1, t2[0:64, :], k2, scale=2.0)
        S.dve(lambda e, ts=ts: e.tensor_scalar(out=XW[:, ts], in0=t1[0:64, :], scalar1=2.0, scalar2=-1.0, op0=ALU.mult, op1=ALU.add), [k1], ["ZC12"])
        self.sigmoid3(t1[0:96, :], XG[:, ts], "ZC14", k1, t2[0:96, :], k2)
        S.dve(lambda e, ts=ts: e.tensor_copy(out=XG[:, ts], in_=t1[0:96, :]), [k1], ["ZC14"])
        if cstop <= 2:
            continue
        for hp in range(4):
            kf = ZC3[:, 4 + hp, ts]
            S.dve(lambda e, kf=kf, hp=hp: e.tensor_scalar(out=t1, in0=kf, scalar1=pcol("kk%d" % hp), scalar2=None, op0=ALU.mult), ["ZC%d" % (4 + hp), "PC"], [k1])
            S.act(lambda e: e.activation(out=tB, in_=t1, func=AF.Square), [k1], [ktB])
            S.dve(lambda e: e.tensor_copy(out=tK.bitcast(BF16)[:, 0:512], in_=tB), [ktB], [ktK])
            pb, kb = self.bank()
            S.pe(lambda e, pb=pb: e.matmul(pb, lhsT=BONES, rhs=tK.bitcast(BF16)[:, 0:512], start=True, stop=True), [ktK, "CB"], [kb])
            S.dve(lambda e, pb=pb: e.tensor_scalar(out=t2, in0=pb, scalar1=1e-24, scalar2=None, op0=ALU.max), [kb], [k2, kb])
            S.act(lambda e: e.activation(out=t2, in_=t2, func=AF.Ln), [k2], [k2])
            S.act(lambda e: e.activation(out=t2, in_=t2, func=AF.Exp, scale=-0.5), [k2], [k2])
            S.dve(lambda e, hp=hp, ts=ts: e.tensor_tensor(out=KK3[:, hp, ts], in0=t1, in1=t2, op=ALU.mult), [k1, k2], ["KK%d_%d" % (hp, tg)])
    if cstop <= 3:
        return
    GW = 256
    DSET = []
    for kd in range(2):
        ARk = al("AR%d" % kd, 4 * 2 * 2 * 128, BF16).rearrange("p (h b s t) -> p h b s t", h=4, b=2, s=2)
        BTk = v3(al("BT%d" % kd, 4 * GW, BF16), t=GW); KTLk = v3(al("KTL%d" % kd, 4 * GW, BF16), t=GW)
        BCk = v3(al("BCH%d" % kd, 4 * GW, BF16), t=GW); KCk = v3(al("KCH%d" % kd, 4 * GW, BF16), t=GW)
        ELk = v3(al("EL%d" % kd, 16, F32), c=4)
        WLDk = al("WLD%d" % kd, 4 * 4 * 64, BF16).rearrange("p (h c k) -> p h c k", h=4, c=4)
        DSET.append((ARk, BTk, KTLk, BCk, KCk, ELk, WLDk))
    TSET = []
    for kt in range(2):
        tt = [tm[:, kt * GW:(kt + 1) * GW] for tm in tmps]
        tk_ = ["TC%dh%d" % (i, kt) for i in range(7)]
        for i in range(7):
            S.alias(tk_[i], ["TC%d" % i])
        TSET.append((tt, tk_))
    self._lo = self.live["COS"][0]

    def al_lo(key, ncols):
        ap = self.alloc_at(key, self._lo, ncols, BF16, ["COS", "SINS"])
        self._lo += ncols
        assert self._lo <= self.live["SINS"][1]
        return ap

    VT = al_lo("VT", 512)
    VPAD3 = v3(al("VPAD", 1024, BF16), n=128)
    UVPAD3 = v3(al("UVPAD", 1024, BF16), n=128)
    MY = [v3(al("MY%d" % i, 2048, BF16), n=256) for i in range(2)]
    MT = [v3(al("MT%d" % i, 1024, BF16), n=128) for i in range(2)]
    RB3 = v3(al("RB", 2048, BF16), n=256)
    MAKT3 = v3(al("MAKT", 1024, BF16), n=128)
    MRKT3 = v3(al("MRKT", 1024, BF16), n=128)
    BCT = al_lo("BCT", 512); KCT = al_lo("KCT", 512)
    RH3 = v3(al_lo("RH", 512), n=128)
    GT4 = al_lo("GT", 512).rearrange("p (h c k) -> p h c k", h=4, c=2)
    HH4 = al_lo("HH", 512).rearrange("p (h c k) -> p h c k", h=4, c=2)
    ST = [v3(al_lo("ST%d" % i, 256), n=64) for i in range(2)]
    for kz, z in ((["VPAD"], VPAD3), (["UVPAD"], UVPAD3), (["RB", "RBp", "RBm0", "RBm1"], RB3)):
        S.pool(lambda e, z=z: e.memset(z, 0.0), [], kz)
    P = self.PS
    stc = [0]

    def prep_unit(d, gi, hp, kt, kd):
        (t1, t2, NLW, CP, CWN, tB, tK), (k1, k2, kNLW, kCP, kCWN, ktB, ktK) = TSET[kt]
        ARk, BTk, KTLk, BCk, KCk, ELk, WLDk = DSET[kd]
        ts = slice(gi * GW, (gi + 1) * GW)
        kkk = "KK%d_%d" % (hp, gi // 2)
        hs = slice(hp * 128, (hp + 1) * 128)
        rr = slice(32 * d, 32 * d + 32)
        kdg = "%d_%d" % (kd, hp)
        pb, kb = P[0][:, kt * 256:kt * 256 + 256], "B0u%d" % kt
        S.pe(lambda e: e.matmul(pb[:, 0:GW], lhsT=W2[rr, hs], rhs=ZC3[rr, 12, ts], start=True, stop=True), ["W2", "ZC12"], [kb])
        self.sigmoid3(NLW, pb[:, 0:GW], kb, kNLW, t1, k1, bias=ncol("w0_%d_%d" % (d, hp)), final_bias=self.CT[:, 4:5])
        pb2, kb2 = pb, kb
        S.pe(lambda e: e.matmul(pb2[:, 0:GW], lhsT=A2[rr, hs], rhs=ZC3[rr, 13, ts], start=True, stop=True), ["A2", "ZC13"], [kb2])
        self.sigmoid3(t2, pb2[:, 0:GW], kb2, k2, t1, k1, bias=ncol("a0_%d_%d" % (d, hp)))
        S.dve(lambda e: e.tensor_tensor_scan(out=CP, data0=RESETM[:, 0:GW], data1=NLW, initial=0.0, op0=ALU.mult, op1=ALU.add), [kNLW, "RESETM"], [kCP])
        CP3 = v3(CP, l=64)
        tot = CP3[:, :, 63:64]
        nch = GW // 64
        if d == 0:
            CWN, kCWN = CP, kCP
        else:
            S.dve(lambda e: e.tensor_tensor(out=CWN, in0=NLW, in1=CP, op=ALU.subtract), [kNLW, kCP], [kCWN])
            S.dve(lambda e: e.tensor_tensor(out=v3(CWN, l=64), in0=v3(CWN, l=64), in1=tot.to_broadcast([128, nch, 64]), op=ALU.add), [kCWN, kCP], [kCWN])
        S.act(lambda e: e.activation(out=ELk[:, hp, :].unsqueeze(2), in_=tot, func=AF.Exp, scale=-1.0), [kCP], ["EL" + kdg])
        S.act(lambda e: e.activation(out=t1, in_=CWN, func=AF.Exp, scale=-1.0), [kCWN], [k1])
        S.dve(lambda e: e.tensor_tensor(out=ARk[:, hp, :, 1, :], in0=v3(ZC3[:, hp, ts], t=128), in1=v3(t1, t=128), op=ALU.mult), ["ZC%d" % hp, k1], ["AR" + kdg])
        S.dve(lambda e: e.tensor_tensor(out=t1, in0=NLW, in1=CWN, op=ALU.subtract), [kNLW, kCWN, k1], [k1])
        S.act(lambda e: e.activation(out=t1, in_=t1, func=AF.Exp), [k1], [k1])
        S.dve(lambda e: e.scalar_tensor_tensor(out=ARk[:, hp, :, 0, :], in0=v3(KK3[:, hp, ts], t=128), scalar=-1.0, in1=v3(t1, t=128), op0=ALU.mult, op1=ALU.mult),
              [kkk, k1], ["AR" + kdg])
        S.dve(lambda e: e.tensor_tensor(out=tB, in0=KK3[:, hp, ts], in1=t2, op=ALU.mult), [kkk, k2], [ktB])
        S.dve(lambda e: e.tensor_scalar(out=tK, in0=t2, scalar1=pcol("ka%d" % hp), scalar2=OMK[:, hp:hp + 1], op0=ALU.mult, op1=ALU.add), [k2, "PC", "OMK"], [ktK])
        S.dve(lambda e: e.tensor_tensor(out=tK, in0=tK, in1=ZC3[:, 4 + hp, ts], op=ALU.mult), [ktK, "ZC%d" % (4 + hp)], [ktK])
        S.act(lambda e: e.activation(out=t1, in_=CWN, func=AF.Exp), [kCWN, k1], [k1])
        S.dve(lambda e: e.tensor_tensor(out=BTk[:, hp, :], in0=tB, in1=t1, op=ALU.mult), [ktB, k1], ["BT" + kdg])
        S.dve(lambda e: e.tensor_tensor(out=KTLk[:, hp, :], in0=tK, in1=t1, op=ALU.mult), [ktK, k1], ["KTL" + kdg])
        S.dve(lambda e: e.tensor_tensor(out=v3(t1, l=64), in0=v3(CWN, l=64), in1=tot.to_broadcast([128, nch, 64]), op=ALU.subtract), [kCWN, kCP, k1], [k1])
        S.act(lambda e: e.activation(out=t1, in_=t1, func=AF.Exp), [k1], [k1])
        S.dve(lambda e: e.tensor_tensor(out=BCk[:, hp, :], in0=tB, in1=t1, op=ALU.mult), [ktB, k1], ["BCH" + kdg])
        S.dve(lambda e: e.tensor_tensor(out=KCk[:, hp, :], in0=tK, in1=t1, op=ALU.mult), [ktK, k1], ["KCH" + kdg])
        S.dve(lambda e: e.tensor_tensor(out=WLDk[:, hp, :, :], in0=ID64.unsqueeze(1).to_broadcast([128, nch, 64]),
                                        in1=ELk[:, hp, :].unsqueeze(2).to_broadcast([128, nch, 64]), op=ALU.mult), ["EL" + kdg, "CB"], ["WLD" + kdg])

    def prep_group(d, gi, kd):
        for pair in ((0, 1), (2, 3)):
            a_ = S.capture(lambda: prep_unit(d, gi, pair[0], 0, kd))
            b_ = S.capture(lambda: prep_unit(d, gi, pair[1], 1, kd))
            S.extend(Sched.merge(a_, b_))

    def emit_block(d, gi, bj, kd):
        ARk, BTk, KTLk, BCk, KCk, ELk, WLDk = DSET[kd]
        msT = MS0 if d == 0 else MS1
        msN = MS1[:, 0:128] if d == 0 else MS0[:, 0:128]
        allh = lambda p: [p + "_%d" % hp for hp in range(4)]
        sti = stc[0]
        j = gi * 2 + bj
        bs = slice(bj * 128, (bj + 1) * 128)
        tsl = slice(j * 128, (j + 1) * 128)
        pbb0 = v3(P[1][:, 0:512].bitcast(BF16), t=128)
        pbb1 = v3(P[2][:, 0:512].bitcast(BF16), t=128)
        for hp in range(4):
            S.pe(lambda e, hp=hp: e.transpose(out=pbb0[:, hp, :], in_=ZC3[:, 8 + hp, tsl], identity=self.IDENT), ["ZC%d" % (8 + hp), "CB"], ["B2"])
            S.pe(lambda e, hp=hp: e.transpose(out=pbb0[:, 4 + hp, :], in_=ARk[:, hp, bj, 0, :], identity=self.IDENT), ["AR%d_%d" % (kd, hp), "CB"], ["B2"])
            S.pe(lambda e, hp=hp: e.transpose(out=pbb1[:, hp, :], in_=BCk[:, hp, bs], identity=self.IDENT), ["BCH%d_%d" % (kd, hp), "CB"], ["B4"])
            S.pe(lambda e, hp=hp: e.transpose(out=pbb1[:, 4 + hp, :], in_=KCk[:, hp, bs], identity=self.IDENT), ["KCH%d_%d" % (kd, hp), "CB"], ["B4"])
        S.act(lambda e: e.activation(out=v3(VT, t=128), in_=pbb0[:, 0:4, :], func=AF.Copy), ["B2"], ["VT", "B2"])
        S.dve(lambda e: e.tensor_copy(out=MY[0][:, :, 128:192].rearrange("p (e a) k -> p a e k", e=2), in_=pbb0[:, 4:8, :].rearrange("p a (e k) -> p a e k", e=2)), ["B2"], ["MY0y0", "MY0y1", "B2"])
        S.act(lambda e: e.activation(out=v3(BCT, t=128), in_=pbb1[:, 0:4, :], func=AF.Copy), ["B4"], ["BCT", "B4"])
        S.dve(lambda e: e.tensor_copy(out=v3(KCT, t=128), in_=pbb1[:, 4:8, :]), ["B4"], ["KCT", "B4"])
        VT3h = v3(VT, k=64)
        nat = lambda h: 2 * (h % 4) + h // 4
        BCT3h = v3(BCT, k=64)
        for c in range(2):
            cr = slice(64 * c, 64 * c + 64)
            S.act(lambda e, cr=cr, c=c: e.activation(out=VPAD3[cr, :, 64 * c:64 * c + 64], in_=VT3h[cr, :, :], func=AF.Copy), ["VT"], ["VPAD"])
            S.act(lambda e, cr=cr, c=c: e.activation(out=RB3[cr, :, 128 + 64 * c:192 + 64 * c].rearrange("p (e a) k -> p a e k", e=2),
                                                     in_=BCT3h[cr, :, :].rearrange("p (a e) k -> p a e k", e=2), func=AF.Copy), ["BCT"], ["RBp"])
        bk3 = lambda t, h: "B%d" % (2 * t + h // 4)
        for hp in range(4):
            for e_ in range(2):
                h = 4 * e_ + hp
                pr = slice(64 * e_, 64 * e_ + 64)
                aT = ARk[pr, hp, bj, 0, :]; rT = ARk[pr, hp, bj, 1, :]
                bT = BTk[pr, hp, bs]
                hc_ = slice(h * 128, (h + 1) * 128)
                S.pe(lambda e, hc_=hc_, aT=aT, bT=bT: e.matmul(P[1][:, hc_], lhsT=aT, rhs=bT, start=True, stop=True), ["AR%d_%d" % (kd, hp), "BT%d_%d" % (kd, hp)], [bk3(1, h)])
                S.pe(lambda e, hc_=hc_, aT=aT, bT=bT: e.matmul(P[2][:, hc_], lhsT=bT, rhs=aT, start=True, stop=True), ["AR%d_%d" % (kd, hp), "BT%d_%d" % (kd, hp)], [bk3(2, h)])
                S.pe(lambda e, hc_=hc_, rT=rT, bT=bT: e.matmul(P[3][:, hc_], lhsT=bT, rhs=rT, start=True, stop=True), ["AR%d_%d" % (kd, hp), "BT%d_%d" % (kd, hp)], [bk3(3, h)])
        mb4 = lambda m: m.unsqueeze(1).to_broadcast([128, 4, 128])
        for hh in range(2):
            bsl = slice(512 * hh, 512 * hh + 512)
            hsl = slice(4 * hh, 4 * hh + 4)
            S.dve(lambda e, bsl=bsl, hsl=hsl: e.tensor_tensor(out=MY[0][:, hsl, 0:128], in0=v3(P[1][:, bsl], t=128), in1=mb4(msN), op=ALU.mult),
                  ["B%d" % (2 + hh), "CB"], ["MY0m%d" % hh, "B%d" % (2 + hh)])
            S.dve(lambda e, bsl=bsl, hsl=hsl: e.tensor_tensor(out=MT[0][:, hsl, :], in0=v3(P[2][:, bsl], t=128), in1=mb4(msT[:, 0:128]), op=ALU.mult),
                  ["B%d" % (4 + hh), "CB"], ["MT0_%d" % hh, "B%d" % (4 + hh)])
            S.dve(lambda e, bsl=bsl, hsl=hsl: e.tensor_tensor(out=RB3[:, hsl, 0:128], in0=v3(P[3][:, bsl], t=128), in1=mb4(msT[:, 128:256]), op=ALU.mult),
                  ["B%d" % (6 + hh), "CB"], ["RBm%d" % hh, "B%d" % (6 + hh)])
        for hp in range(4):
            for e_ in range(2):
                h = 4 * e_ + hp
                pr = slice(64 * e_, 64 * e_ + 64)
                aT = ARk[pr, hp, bj, 0, :]; rT = ARk[pr, hp, bj, 1, :]
                kT = KTLk[pr, hp, bs]
                hc_ = slice(h * 128, (h + 1) * 128)
                S.pe(lambda e, hc_=hc_, aT=aT, kT=kT: e.matmul(P[2][:, hc_], lhsT=kT, rhs=aT, start=True, stop=True), ["AR%d_%d" % (kd, hp), "KTL%d_%d" % (kd, hp)], [bk3(2, h)])
                S.pe(lambda e, hc_=hc_, rT=rT, kT=kT: e.matmul(P[3][:, hc_], lhsT=kT, rhs=rT, start=True, stop=True), ["AR%d_%d" % (kd, hp), "KTL%d_%d" % (kd, hp)], [bk3(3, h)])
        for hh in range(2):
            bsl = slice(512 * hh, 512 * hh + 512)
            hsl = slice(4 * hh, 4 * hh + 4)
            S.dve(lambda e, bsl=bsl, hsl=hsl: e.tensor_tensor(out=MAKT3[:, hsl, :], in0=v3(P[2][:, bsl], t=128), in1=mb4(msT[:, 0:128]), op=ALU.mult),
                  ["B%d" % (4 + hh), "CB"], ["MAKT%d" % hh, "B%d" % (4 + hh)])
            S.dve(lambda e, bsl=bsl, hsl=hsl: e.tensor_tensor(out=MRKT3[:, hsl, :], in0=v3(P[3][:, bsl], t=128), in1=mb4(msT[:, 128:256]), op=ALU.mult),
                  ["B%d" % (6 + hh), "CB"], ["MRKT%d" % hh, "B%d" % (6 + hh)])
        for hh in range(2):
            for h in range(4 * hh, 4 * hh + 4):
                S.pe(lambda e, h=h, hh=hh: e.matmul(P[1][:, 512 * hh + (h % 4) * 64:512 * hh + (h % 4) * 64 + 64], lhsT=MAKT3[:, h, :], rhs=VT3h[:, nat(h), :], start=True, stop=True),
                     ["MAKT%d" % hh, "VT"], ["B%d" % (2 + hh)])
            S.act(lambda e, hh=hh: e.activation(out=MY[0][:, 4 * hh:4 * hh + 4, 192:256], in_=v3(P[1][:, 512 * hh:512 * hh + 256], k=64), func=AF.Copy), ["B%d" % (2 + hh)], ["MY0x%d" % hh, "B%d" % (2 + hh)])
        mk = lambda i, hh: ["MY%dm%d" % (i, hh), "MY%dy%d" % (i, hh), "MY%dx%d" % (i, hh)]
        mykeys = lambda i: mk(i, 0) + mk(i, 1)
        for lv in range(6):
            a, b_ = lv % 2, (lv + 1) % 2
            for hh in range(2):
                rk_ = mk(a, hh) + ["MT%d_%d" % (a, hh)]
                for h in range(4 * hh, 4 * hh + 4):
                    hc_ = slice(h * 128, (h + 1) * 128)
                    if lv < 4:
                        S.pe(lambda e, h=h, a=a, hc_=hc_: e.matmul(P[2][:, hc_], lhsT=MT[a][:, h, :], rhs=MY[a][:, h, 0:128], start=True, stop=True), rk_, ["B%d" % (4 + hh)])
                    S.pe(lambda e, h=h, a=a, hc_=hc_: e.matmul(P[3][:, hc_], lhsT=MT[a][:, h, :], rhs=MY[a][:, h, 128:256], start=True, stop=True), rk_, ["B%d" % (6 + hh)])
                    if lv < 5:
                        S.pe(lambda e, h=h, a=a, hc_=hc_: e.matmul(P[1][:, hc_], lhsT=MY[a][:, h, 0:128], rhs=MT[a][:, h, :], start=True, stop=True), rk_, ["B%d" % (2 + hh)])
                bsl = slice(512 * hh, 512 * hh + 512)
                hsl = slice(4 * hh, 4 * hh + 4)
                S.dve(lambda e, bsl=bsl, hsl=hsl, a=a, b_=b_: e.tensor_tensor(out=MY[b_][:, hsl, 128:256], in0=v3(P[3][:, bsl], t=128), in1=MY[a][:, hsl, 128:256], op=ALU.add),
                      ["B%d" % (6 + hh)] + mk(a, hh), ["MY%dy%d" % (b_, hh), "MY%dx%d" % (b_, hh), "B%d" % (6 + hh)])
                if lv < 4:
                    S.act(lambda e, bsl=bsl, hsl=hsl, b_=b_: e.activation(out=MY[b_][:, hsl, 0:128], in_=v3(P[2][:, bsl], t=128), func=AF.Copy),
                          ["B%d" % (4 + hh)], ["MY%dm%d" % (b_, hh), "B%d" % (4 + hh)])
                if lv < 5:
                    S.act(lambda e, bsl=bsl, hsl=hsl, b_=b_: e.activation(out=MT[b_][:, hsl, :], in_=v3(P[1][:, bsl], t=128), func=AF.Copy),
                          ["B%d" % (2 + hh)], ["MT%d_%d" % (b_, hh), "B%d" % (2 + hh)])
        YF = MY[0]
        yk = mykeys(0)
        for c in range(2):
            cr = slice(64 * c, 64 * c + 64)
            S.act(lambda e, cr=cr, c=c: e.activation(out=UVPAD3[cr, :, 64 * c:64 * c + 64], in_=YF[cr, :, 192:256], func=AF.Copy), yk, ["UVPAD"])
        for h in range(8):
            eo = 64 * (h // 4)
            hp_ = h % 4
            S.pe(lambda e, h=h, eo=eo, hp_=hp_: e.matmul(P[2][eo:eo + 64, hp_ * 256:hp_ * 256 + 256], lhsT=YF[:, h, 128:192], rhs=RB3[:, h, :], start=True, stop=True),
                 mk(0, h // 4) + ["RBp", "RBm%d" % (h // 4), "RB"], ["B%d" % (4 + hp_ // 2)])
        cb0 = (2 * bj)
        for bb in range(2):
            pv = v3(P[2][:, 512 * bb:512 * bb + 512], t=256)
            hps = slice(2 * bb, 2 * bb + 2)
            S.dve(lambda e, pv=pv, hps=hps: e.tensor_tensor(out=RH3[:, hps, :], in0=pv[:, :, 0:128], in1=ARk[:, hps, bj, 1, :], op=ALU.add),
                  ["B%d" % (4 + bb)] + allh("AR%d" % kd), ["RH%d" % bb, "B%d" % (4 + bb)])
            S.dve(lambda e, pv=pv, hps=hps: e.tensor_tensor(out=GT4[:, hps, :, :], in0=pv[:, :, 128:256].rearrange("p h (c k) -> p h c k", c=2), in1=WLDk[:, hps, cb0:cb0 + 2, :], op=ALU.add),
                  ["B%d" % (4 + bb)] + allh("WLD%d" % kd), ["GT%d" % bb, "B%d" % (4 + bb)])
        KCT3h = v3(KCT, k=64)
        for h in range(8):
            eo = 64 * (h // 4)
            hp_ = h % 4
            S.pe(lambda e, h=h, eo=eo, hp_=hp_: e.matmul(P[1][eo:eo + 64, hp_ * 128:hp_ * 128 + 128], lhsT=BCT3h[:, nat(h), :], rhs=UVPAD3[:, h, :], start=True, stop=False),
                 ["BCT", "UVPAD"], ["B2"])
            S.pe(lambda e, h=h, eo=eo, hp_=hp_: e.matmul(P[1][eo:eo + 64, hp_ * 128:hp_ * 128 + 128], lhsT=KCT3h[:, nat(h), :], rhs=VPAD3[:, nat(h), :], start=False, stop=True),
                 ["KCT", "VPAD"], ["B2"])
        S.act(lambda e: e.activation(out=HH4, in_=P[1][:, 0:512].rearrange("p (h c k) -> p h c k", h=4, c=2), func=AF.Copy), ["B2"], ["HH", "B2"])
        clist = (0, 1) if d == 0 else (1, 0)
        for c in clist:
            st_in, k_in = ST[sti % 2], "ST%d" % (sti % 2)
            st_out, k_out = ST[(sti + 1) % 2], "ST%d" % ((sti + 1) % 2)
            sti += 1
            cs = slice(64 * c, 64 * c + 64)
            SB = (P[1], P[2]); SBK = ("B3", "B5")
            OB_ = (P[0], P[3]); OBK = ("B1", "B7")
            for h in range(8):
                e_, hp_ = h // 4, h % 4
                pr = slice(64 * e_, 64 * e_ + 64)
                S.pe(lambda e, e_=e_, hp_=hp_, pr=pr, c=c, st_in=st_in: e.matmul(SB[e_][pr, 512 + hp_ * 64:512 + hp_ * 64 + 64], lhsT=GT4[pr, hp_, c, :], rhs=st_in[pr, hp_, :], start=True, stop=True),
                     ["GT0", "GT1", k_in + "e0", k_in + "e1"], [SBK[e_]])
            for h in range(8):
                e_, hp_ = h // 4, h % 4
                pr = slice(64 * e_, 64 * e_ + 64)
                oc = slice(512 + hp_ * 64, 512 + hp_ * 64 + 64)
                S.pe(lambda e, h=h, e_=e_, pr=pr, oc=oc, cs=cs: e.matmul(OB_[e_][pr, oc], lhsT=VT3h[:, nat(h), :], rhs=MRKT3[:, h, cs], start=True, stop=False), ["VT", "MRKT0", "MRKT1"], [OBK[e_]])
                S.pe(lambda e, h=h, e_=e_, pr=pr, oc=oc, cs=cs: e.matmul(OB_[e_][pr, oc], lhsT=YF[:, h, 192:256], rhs=RB3[:, h, cs], start=False, stop=False), mk(0, e_) + ["RBm%d" % e_], [OBK[e_]])
                S.pe(lambda e, hp_=hp_, e_=e_, pr=pr, oc=oc, cs=cs, st_in=st_in: e.matmul(OB_[e_][pr, oc], lhsT=st_in[pr, hp_, :], rhs=RH3[pr, hp_, cs], start=False, stop=True), [k_in + "e0", k_in + "e1", "RH0", "RH1"], [OBK[e_]])
            ocol = slice(j * 128 + 64 * c, j * 128 + 64 * c + 64)
            for e_ in range(2):
                pr = slice(64 * e_, 64 * e_ + 64)
                S.dve(lambda e, e_=e_, pr=pr, c=c, st_out=st_out: e.tensor_tensor(out=st_out[pr], in0=v3(SB[e_][pr, 512:768], k=64), in1=HH4[pr, :, c, :], op=ALU.add),
                      [SBK[e_], "HH"], [k_out + "e%d" % e_, SBK[e_]])
                if d == 0:
                    S.act(lambda e, e_=e_, pr=pr, ocol=ocol: e.activation(out=OACC3[pr, :, ocol], in_=v3(OB_[e_][pr, 512:768], k=64), func=AF.Copy), [OBK[e_]], ["OACC%d_%d" % (j, e_), OBK[e_]])
                else:
                    S.dve(lambda e, e_=e_, pr=pr, ocol=ocol: e.tensor_tensor(out=OACC3[pr, :, ocol], in0=v3(OB_[e_][pr, 512:768], k=64), in1=OACC3[pr, :, ocol], op=ALU.add),
                          [OBK[e_], "OACC%d_%d" % (j, e_)], ["OACC%d_%d" % (j, e_), OBK[e_]])
        stc[0] = sti

    def blocks_group(d, gi, kd):
        for bj in ((0, 1) if d == 0 else (1, 0)):
            emit_block(d, gi, bj, kd)

    steps = [(0, gi) for gi in range(T // GW)] + [(1, gi) for gi in range(T // GW - 1, -1, -1)]
    steps = steps[:getattr(self, "maxsteps", 99)]
    for kst in range(2):
        S.pool(lambda e, kst=kst: e.memset(ST[kst], 0.0), [], ["ST%de0" % kst, "ST%de1" % kst])
    prep_group(steps[0][0], steps[0][1], 0)
    for si, (d, gi) in enumerate(steps):
        if si == T // GW:
            zi = stc[0] % 2
            S.pool(lambda e, zi=zi: e.memset(ST[zi], 0.0), [], ["ST%de0" % zi, "ST%de1" % zi])
        blk_ops = S.capture(lambda: blocks_group(d, gi, si % 2))
        if si + 1 < len(steps):
            nd, ngi = steps[si + 1]
            prep_ops = S.capture(lambda: prep_group(nd, ngi, (si + 1) % 2))
        else:
            prep_ops = []
        S.extend(Sched.merge(blk_ops, prep_ops, bfrac=0.7))
    self.OACC3 = OACC3
    self.tap("OACC", OACC3.rearrange("p a n -> p (a n)"), ["OACC%d_%d" % (j, e_) for j in range(NB) for e_ in range(2)], [128, 4 * T])


Builder.phase_C = _phase_C


RUN_PHASE_C = True


def _build_full():
    B = Builder()
    B.phase_A()
    B.phase_B()
    B.phase_C()
    B.phase_C3()
    B.phase_D()
    return B


def kernel(**inputs):
    inp = {k: np.asarray(v) for k, v in inputs.items()}
    B = _build_full()
    nc = B.finish()
    in_maps = [_host_inputs(inp, b) for b in range(8)]
    res = run_bass_kernel_spmd(nc, in_maps, core_ids=list(range(8)))
    out = np.stack([np.asarray(r["out"]) for r in res.results], 0).astype(np.float32)
    return out


def _phase_C3(self):
    S = self.S
    al = self.alloc
    ZC3, OACC3, A2, G2, CB = self.ZC3, self.OACC3, self.A2, self.G2, self.CB
    BAVG, BONES = CB[:, 640:768], CB[:, 768:896]
    pcol = lambda n: self.PC[:, PCI[n]:PCI[n] + 1]
    ncol = lambda n: self.NPCt[:, PCI[n]:PCI[n] + 1]
    self.reset(self.mC)
    self.YR3 = YR3 = v3(al("YR", 4 * T, BF16), t=T)
    self.mYR = self.mark()
    NSET = 4
    sets = []
    for si in range(NSET):
        tt = [al("TF%d_%d" % (si, i), 512, F32) for i in range(5)]
        tk = ["TF%d_%d" % (si, i) for i in range(5)]
        bb = [al(n + "%d" % si, 512, BF16) for n in ("OBF", "SQF", "PBF")]
        sets.append((tt, tk, bb, ["OBF%d" % si, "SQF%d" % si, "PBF%d" % si]))
    OMK2 = al("OMK2", 4, F32)
    for hp in range(4):
        S.dve(lambda e, hp=hp: e.tensor_scalar(out=OMK2[:, hp:hp + 1], in0=pcol("ka%d" % hp), scalar1=-2.0, scalar2=2.0, op0=ALU.mult, op1=ALU.add), ["PC"], ["OMK2"])

    def unit(tg, hp, st):
        (t1, t2, tA0, tA1, tB), (k1, k2, kA0, kA1, kB), (OB, SQ, PB), (kOB, kSQ, kPB) = st
        ts = slice(tg * 512, (tg + 1) * 512)
        ok = ["OACC%d_%d" % (4 * tg + q, e_) for q in range(4) for e_ in range(2)]
        hs = slice(hp * 128, (hp + 1) * 128)
        o = OACC3[:, hp, ts]
        S.act(lambda e: e.activation(out=OB, in_=o, func=AF.Copy), ok, [kOB]); yield
        pb, kb = self.bank()
        S.pe(lambda e: e.matmul(pb, lhsT=BAVG, rhs=OB, start=True, stop=True), [kOB, "CB"], [kb]); yield
        S.dve(lambda e: e.tensor_tensor(out=t1, in0=o, in1=pb, op=ALU.subtract), ok + [kb], [k1, kb]); yield
        S.act(lambda e: e.activation(out=SQ, in_=t1, func=AF.Square), [k1], [kSQ]); yield
        pb2, kb2 = self.bank()
        S.pe(lambda e: e.matmul(pb2, lhsT=BAVG, rhs=SQ, start=True, stop=True), [kSQ, "CB"], [kb2]); yield
        S.act(lambda e: e.activation(out=t2, in_=pb2, func=AF.Ln, bias=self.CT[:, 2:3]), [kb2, "CT"], [k2, kb2]); yield
        S.act(lambda e: e.activation(out=t2, in_=t2, func=AF.Exp, scale=-0.5), [k2], [k2]); yield
        S.dve(lambda e: e.tensor_tensor(out=t1, in0=t1, in1=t2, op=ALU.mult), [k1, k2], [k1]); yield
        S.dve(lambda e: e.tensor_scalar(out=t1, in0=t1, scalar1=pcol("lnw%d" % hp), scalar2=pcol("lnb%d" % hp), op0=ALU.mult, op1=ALU.add), [k1, "PC"], [k1]); yield
        for d, (tA, kA) in enumerate(((tA0, kA0), (tA1, kA1))):
            rr = slice(32 * d, 32 * d + 32)
            pa, kpa = self.bank()
            S.pe(lambda e, pa=pa, rr=rr: e.matmul(pa, lhsT=A2[rr, hs], rhs=ZC3[rr, 13, ts], start=True, stop=True), ["A2", "ZC13"], [kpa]); yield
            self.sigmoid3(tA, pa, kpa, kA, t2, k2, bias=ncol("a0_%d_%d" % (d, hp))); yield
        S.dve(lambda e: e.tensor_tensor(out=tA0, in0=tA0, in1=tA1, op=ALU.add), [kA0, kA1], [kA0]); yield
        S.dve(lambda e: e.tensor_scalar(out=tA0, in0=tA0, scalar1=pcol("ka%d" % hp), scalar2=OMK2[:, hp:hp + 1], op0=ALU.mult, op1=ALU.add), [kA0, "PC", "OMK2"], [kA0]); yield
        S.dve(lambda e: e.tensor_tensor(out=tA0, in0=tA0, in1=ZC3[:, 4 + hp, ts], op=ALU.mult), [kA0, "ZC%d" % (4 + hp)], [kA0]); yield
        S.dve(lambda e: e.tensor_tensor(out=tA0, in0=tA0, in1=ZC3[:, hp, ts], op=ALU.mult), [kA0, "ZC%d" % hp], [kA0]); yield
        S.dve(lambda e: e.tensor_scalar(out=PB, in0=tA0, scalar1=pcol("rk%d" % hp), scalar2=None, op0=ALU.mult), [kA0, "PC"], [kPB]); yield
        pbb, kbb = self.bank()
        S.pe(lambda e: e.matmul(pbb, lhsT=BONES, rhs=PB, start=True, stop=True), [kPB, "CB"], [kbb]); yield
        S.dve(lambda e: e.tensor_tensor(out=tB, in0=pbb, in1=ZC3[:, 8 + hp, ts], op=ALU.mult), [kbb, "ZC%d" % (8 + hp)], [kB, kbb]); yield
        S.dve(lambda e: e.tensor_tensor(out=t1, in0=t1, in1=tB, op=ALU.add), [k1, kB], [k1]); yield
        pg, kpg = self.bank()
        S.pe(lambda e: e.matmul(pg, lhsT=G2[0:96, hs], rhs=ZC3[0:96, 14, ts], start=True, stop=True), ["G2", "ZC14"], [kpg]); yield
        S.dve(lambda e: e.tensor_tensor(out=YR3[:, hp, ts], in0=pg, in1=t1, op=ALU.mult), [kpg, k1], ["YR%d_%d" % (hp, tg), kpg]); yield

    ulist = [(tg, hp) for tg in range(4) for hp in range(4)]
    for i0_ in range(0, len(ulist), NSET):
        active = [unit(tg, hp, sets[si]) for si, (tg, hp) in enumerate(ulist[i0_:i0_ + NSET])]
        while active:
            for g in list(active):
                try:
                    next(g)
                except StopIteration:
                    active.remove(g)
    self.tap("YR", YR3.rearrange("p a n -> p (a n)"), ["YR%d_%d" % (hp, tg) for hp in range(4) for tg in range(4)], [128, 4 * T])


def _post_norm_residual(self, P2, kbs, G, kg, xin, kxin, xout, kxout, SSc, kss, junk, kjunk, TMPH, ktmp):
    S = self.S
    ssa, ssb, rs = SSc[:, 0:1], SSc[:, 1:2], SSc[:, 2:3]
    S.act(lambda e: e.activation(out=junk[:, 0:512], in_=P2[:, 0:512], func=AF.Square, accum_out=ssa), [kbs[0]], [kjunk, kss + "a", kbs[0]])
    S.act(lambda e: e.activation(out=junk[:, 0:512], in_=P2[:, 512:1024], func=AF.Square, accum_out=ssb), [kbs[1]], [kjunk, kss + "b", kbs[1]])
    S.dve(lambda e: e.tensor_tensor(out=rs, in0=ssa, in1=ssb, op=ALU.add), [kss + "a", kss + "b"], [kss + "r"])
    S.act(lambda e: e.activation(out=rs, in_=rs, func=AF.Ln, scale=1.0 / 1024, bias=self.EPS6), [kss + "r", "CT"], [kss + "r"])
    S.act(lambda e: e.activation(out=rs, in_=rs, func=AF.Exp, scale=-0.5), [kss + "r"], [kss + "r"])
    for half in range(2):
        hs = slice(half * 512, (half + 1) * 512)
        S.dve(lambda e, hs=hs: e.scalar_tensor_tensor(out=TMPH, in0=P2[:, hs], scalar=rs, in1=G[:, hs], op0=ALU.mult, op1=ALU.mult),
              [kbs[half], kss + "r", kg], [ktmp, kbs[half]])
        S.dve(lambda e, hs=hs: e.tensor_tensor(out=xout[:, hs], in0=TMPH, in1=xin[:, hs], op=ALU.add), [ktmp, kxin], [kxout])


Builder.phase_C3 = _phase_C3
Builder.post_norm_residual = _post_norm_residual


def _phase_D(self):
    S, nc, dr = self.S, self.nc, self.dr
    al = self.alloc
    pcol = lambda n: self.PC[:, PCI[n]:PCI[n] + 1]
    P = self.PS
    cq0 = self.live["COS"][0]
    assert cq0 + 32768 <= self.live["ZC"][1]
    X1 = v3(self.alloc_at("X1", cq0, 16 * 1024, F32, ["OACC", "CQKV", "KR", "ZC", "COS", "SINS", "VT", "BCT", "KCT", "RH", "GT", "HH", "ST0", "ST1"]), n=1024)
    X1END = cq0 + 32768
    self.reset(self.mYR)
    WOUT3 = v3(al("WOUT", 8 * 1024, BF16), n=1024)
    S.dma("qpool", WOUT3, dr["w_out"].rearrange("(kc p) n -> p kc n", p=128), [], ["WOUT"])
    GP = al("GPOST", 1024, F32)
    S.dma("qsp", GP, dr["gains"][1:2, :].partition_broadcast(128), [], ["GPOST"])
    XD = [al("XD%d" % i, 1024, F32) for i in range(2)]
    TMPH = al("TMPH", 512, F32)
    JK = al("JKD", 512, BF16)
    SSD = al("SSD", 64, F32)
    ycat = lambda kc: self.YR3[:, kc, :] if kc < 4 else self.YM3[:, kc - 4, :]
    ykeys = lambda kc, i: ["YR%d_%d" % (kc, i // 4)] if kc < 4 else ["YM%d_%d" % (2 * (kc - 4) + e, i // 4) for e in range(2)]
    for i in range(NB):
        P2 = P[2 + i % 2]
        kbs = ["B%d" % (4 + 2 * (i % 2)), "B%d" % (5 + 2 * (i % 2))]
        sl = slice(i * 128, (i + 1) * 128)
        for half in range(2):
            for kc in range(8):
                S.pe(lambda e, P2=P2, half=half, kc=kc, sl=sl: e.matmul(P2[:, half * 512:(half + 1) * 512], lhsT=ycat(kc)[:, sl], rhs=WOUT3[:, kc, half * 512:(half + 1) * 512],
                                                                         start=(kc == 0), stop=(kc == 7)), ["WOUT"] + ykeys(kc, i), [kbs[half]])
        xd, kxd = XD[i % 2], "XD%d" % (i % 2)
        S.dma("qsp", xd, dr["x"][sl, :], [], [kxd])
        self.post_norm_residual(P2, kbs, GP, "GPOST", xd, kxd, X1[:, i, :], "X1_%d" % i, SSD[:, 4 * (i % 8):4 * (i % 8) + 4], "SSD%d" % (i % 8), JK, "JKD", TMPH, "TMPH")
    self.tap("X1", X1.rearrange("p a n -> p (a n)"), ["X1_%d" % i for i in range(NB)], [128, 16 * 1024])
    if self.stage < 3:
        return
    self.reset(X1END)
    WQM3 = v3(al("WQM", 8 * 1024, BF16), n=1024)
    WOM3 = v3(al("WOM", 8 * 1024, BF16), n=1024)
    WKVB3 = v3(al("WKVB", 8 * 1024, BF16), n=1024)
    S.dma("qpool", WQM3, dr["mem_wq"].rearrange("(kc p) n -> p kc n", p=128), [], ["WQM"])
    S.dma("qpool", WOM3, dr["mem_wo"].rearrange("(kc p) n -> p kc n", p=128), [], ["WOM"])
    wkv_src = dr["mem_wkv"].rearrange("(kc p) n -> p kc n", p=128)
    S.dma("qpool", WKVB3, wkv_src[:, :, 0:1024], [], ["WKVB"])
    for kc in range(8):
        S.dve(lambda e, kc=kc: e.tensor_scalar(out=WQM3[:, kc, :], in0=WQM3[:, kc, :], scalar1=pcol("gm%d" % kc), scalar2=None, op0=ALU.mult), ["WQM", "PC"], ["WQM"])
        S.dve(lambda e, kc=kc: e.tensor_scalar(out=WKVB3[:, kc, :], in0=WKVB3[:, kc, :], scalar1=pcol("gt%d" % kc), scalar2=None, op0=ALU.mult), ["WKVB", "PC"], ["WKVB"])
    MTT3 = v3(al("MTT", 8 * 256, BF16), n=256)
    KTM3 = v3(al("KTM", 8 * 256, BF16), n=256)
    VM3 = v3(al("VM", 2 * 1024, BF16), n=1024)
    H2Ts = [v3(al("H2T%d_" % k, 8 * 512, BF16), n=512) for k in range(2)]
    QMs = [v3(al("QM%d_" % k, 8 * 512, BF16), n=512) for k in range(2)]
    OTs = [v3(al("OTM%d_" % k, 8 * 512, BF16), n=512) for k in range(2)]
    PTM = [al("PTM%d" % i, 512, BF16) for i in range(4)]
    RDs = [al("RDM%d" % k, 512, F32) for k in range(2)]
    HBM = [al("HBM%d" % i, 1024, BF16) for i in range(2)]
    MEMB = [al("MEMB0", 1024, F32)] * 2
    GP2 = al("GPOSTM", 1024, F32)
    S.dma("qsp", GP2, dr["gains"][4:5, :].partition_broadcast(128), [], ["GPOSTM"])
    TMPHs = [al("TMPHM%d" % k, 512, F32) for k in range(2)]
    SS2 = al("SSM", 64, F32)
    ONESB = al("ONESB", 128, BF16)
    S.pool(lambda e: e.memset(ONESB, 1.0), [], ["ONESB"])
    for blk in range(2):
        mb, kmb = MEMB[0], "MEMB0"
        hb, khb = HBM[blk], "HBM%d" % blk
        S.dma("qsp", mb, dr["mem"][blk * 128:(blk + 1) * 128, :], [], [kmb])
        ss, rs = SS2[:, 2 * blk:2 * blk + 1], SS2[:, 2 * blk + 1:2 * blk + 2]
        self.rms_plain(mb, kmb, hb, khb, ss, rs, "SSMm%d" % blk, hb, khb)
        self.transpose_to(hb, khb, 8, None, MTT3[:, :, blk * 128:(blk + 1) * 128], "MTT%d" % blk, blk)
    mk = ["MTT0", "MTT1"]
    ev = 0
    for oc in range(8):
        pb, kb = self.bank()
        for kc in range(8):
            S.pe(lambda e, pb=pb, kc=kc, oc=oc: e.matmul(pb[:, 0:256], lhsT=WKVB3[:, kc, oc * 128:(oc + 1) * 128], rhs=MTT3[:, kc, :], start=(kc == 0), stop=(kc == 7)), ["WKVB"] + mk, [kb])
        self.evac_copy(KTM3[:, oc, :], pb[:, 0:256], kb, "KTM%d" % oc, ev); ev += 1
    S.dma("qpool", WKVB3, wkv_src[:, :, 1024:2048], [], ["WKVB"])
    for kc in range(8):
        S.dve(lambda e, kc=kc: e.tensor_scalar(out=WKVB3[:, kc, :], in0=WKVB3[:, kc, :], scalar1=pcol("gt%d" % kc), scalar2=None, op0=ALU.mult), ["WKVB", "PC"], ["WKVB"])
    for blk in range(2):
        for half in range(2):
            pb, kb = self.bank()
            for kc in range(8):
                S.pe(lambda e, pb=pb, kc=kc, blk=blk, half=half: e.matmul(pb, lhsT=MTT3[:, kc, blk * 128:(blk + 1) * 128], rhs=WKVB3[:, kc, half * 512:(half + 1) * 512],
                                                                          start=(kc == 0), stop=(kc == 7)), ["WKVB", "MTT%d" % blk], [kb])
            self.evac_copy(VM3[:, blk, half * 512:(half + 1) * 512], pb, kb, "VM%d_%d" % (blk, half), ev); ev += 1
    ktm = ["KTM%d" % oc for oc in range(8)]
    vmk = ["VM%d_%d" % (b_, h_) for b_ in range(2) for h_ in range(2)]
    def d2_group(tg, k):
        H2T3, QM3, OT3, RD, TMPH2 = H2Ts[k], QMs[k], OTs[k], RDs[k], TMPHs[k]
        JKp = TMPH2.bitcast(BF16)
        kp = "%d_" % k
        self.bankset = (0, 1) if k == 0 else (2, 3)
        ev = 0
        for bi in range(4):
            i = 4 * tg + bi
            hb, khb = HBM[k], "HBM%d" % k
            ss, rs = SS2[:, 8 + 2 * (i % 8):9 + 2 * (i % 8)], SS2[:, 9 + 2 * (i % 8):10 + 2 * (i % 8)]
            self.rms_plain(X1[:, i, :], "X1_%d" % i, hb, khb, ss, rs, "SSMx%d" % (i % 8), hb, khb)
            self.transpose_to(hb, khb, 8, None, H2T3[:, :, bi * 128:(bi + 1) * 128], "H2T" + kp + "%d" % bi, bi)
        hk = ["H2T" + kp + "%d" % bi for bi in range(4)]
        for oc in range(8):
            pb, kb = self.bank()
            for kc in range(8):
                S.pe(lambda e, pb=pb, kc=kc, oc=oc: e.matmul(pb, lhsT=WQM3[:, kc, oc * 128:(oc + 1) * 128], rhs=H2T3[:, kc, :], start=(kc == 0), stop=(kc == 7)), ["WQM"] + hk, [kb])
            self.evac_copy(QM3[:, oc, :], pb, kb, "QM" + kp + "%d" % oc, ev); ev += 1
        pti = 0
        for hd in range(4):
            pts = []
            for kbk in range(2):
                ps_, kps = self.bank()
                for c in range(2):
                    S.pe(lambda e, ps_=ps_, c=c, hd=hd, kbk=kbk: e.matmul(ps_, lhsT=KTM3[:, 2 * hd + c, kbk * 128:(kbk + 1) * 128], rhs=QM3[:, 2 * hd + c, :], start=(c == 0), stop=(c == 1)),
                         ktm + ["QM" + kp + "%d" % (2 * hd), "QM" + kp + "%d" % (2 * hd + 1)], [kps])
                pt, kpt = PTM[2 * k + pti % 2], "PTM%d" % (2 * k + pti % 2)
                pti += 1
                S.act(lambda e, pt=pt, ps_=ps_: e.activation(out=pt, in_=ps_, func=AF.Exp, scale=1.0 / 16.0), [kps], [kpt, kps])
                pts.append((pt, kpt))
            pd, kpd = self.bank()
            for kbk in range(2):
                S.pe(lambda e, pd=pd, kbk=kbk, pts=pts: e.matmul(pd, lhsT=ONESB, rhs=pts[kbk][0], start=(kbk == 0), stop=(kbk == 1)), ["ONESB", pts[kbk][1]], [kpd])
            S.act(lambda e, pd=pd: e.activation(out=RD, in_=pd, func=AF.Ln), [kpd], ["RDM%d" % k, kpd])
            S.act(lambda e: e.activation(out=RD, in_=RD, func=AF.Exp, scale=-1.0), ["RDM%d" % k], ["RDM%d" % k])
            for dc in range(2):
                po, kpo = self.bank()
                for kbk in range(2):
                    S.pe(lambda e, po=po, kbk=kbk, pts=pts, hd=hd, dc=dc: e.matmul(po, lhsT=VM3[:, kbk, (2 * hd + dc) * 128:(2 * hd + dc + 1) * 128], rhs=pts[kbk][0],
                                                                                  start=(kbk == 0), stop=(kbk == 1)), vmk + [pts[kbk][1]], [kpo])
                S.dve(lambda e, po=po, hd=hd, dc=dc: e.tensor_tensor(out=OT3[:, 2 * hd + dc, :], in0=po, in1=RD, op=ALU.mult), [kpo, "RDM%d" % k], ["OTM" + kp + "%d" % (2 * hd + dc), kpo])
        otk = ["OTM" + kp + "%d" % c for c in range(8)]
        P2 = P[2 + k]
        kbs = ["B%d" % (4 + 2 * k), "B%d" % (5 + 2 * k)]
        for bi in range(4):
            i = 4 * tg + bi
            for half in range(2):
                for kc in range(8):
                    S.pe(lambda e, half=half, kc=kc, bi=bi: e.matmul(P2[:, half * 512:(half + 1) * 512], lhsT=OT3[:, kc, bi * 128:(bi + 1) * 128], rhs=WOM3[:, kc, half * 512:(half + 1) * 512],
                                                                      start=(kc == 0), stop=(kc == 7)), ["WOM"] + otk, [kbs[half]])
            self.post_norm_residual(P2, kbs, GP2, "GPOSTM", X1[:, i, :], "X1_%d" % i, X1[:, i, :], "X1_%d" % i, SS2[:, 32 + 4 * (i % 8):36 + 4 * (i % 8)], "SSMp%d" % (i % 8), JKp, "TMPHM%d" % k, TMPH2, "TMPHM%d" % k)
        self.bankset = None

    for tg0 in (0, 2):
        a_ = S.capture(lambda: d2_group(tg0, 0))
        b_ = S.capture(lambda: d2_group(tg0 + 1, 1))
        S.extend(Sched.merge(a_, b_))
    self.tap("X2", X1.rearrange("p a n -> p (a n)"), ["X1_%d" % i for i in range(NB)], [128, 16 * 1024])
    if self.stage < 4:
        return
    self.reset(X1END)
    W1M3 = v3(al("W1M", 8 * 4096, BF16), n=4096)
    W2M3 = v3(al("W2M", 32 * 1024, BF16), n=1024)
    w1src = dr["mlp_w1"].rearrange("(kc p) n -> p kc n", p=128)
    w2src = dr["mlp_w2"].rearrange("(hc p) n -> p hc n", p=128)
    for q in range(4):
        S.dma("qpool", W1M3[:, :, q * 1024:(q + 1) * 1024], w1src[:, :, q * 1024:(q + 1) * 1024], [], ["W1M%d" % q])
    for q in range(4):
        S.dma("qpool", W2M3[:, 8 * q:8 * q + 8, :], w2src[:, 8 * q:8 * q + 8, :], [], ["W2M%d" % q])
    for q in range(4):
        for kc in range(8):
            S.dve(lambda e, kc=kc, q=q: e.tensor_scalar(out=W1M3[:, kc, q * 1024:(q + 1) * 1024], in0=W1M3[:, kc, q * 1024:(q + 1) * 1024], scalar1=pcol("gp%d" % kc), scalar2=None, op0=ALU.mult),
                  ["W1M%d" % q, "PC"], ["W1M%d" % q])
    TG = 256
    H3T3 = v3(al("H3T", 8 * TG, BF16), n=TG)
    HB3 = al("HB3", 1024, BF16)
    TMPH3 = HB3.bitcast(F32)
    HID = [al("HID%d" % i, TG, BF16) for i in range(3)]
    GP3 = al("GPOSTP", 1024, F32)
    S.dma("qsp", GP3, dr["gains"][6:7, :].partition_broadcast(128), [], ["GPOSTP"])
    SS3 = al("SSP", 64, F32)
    hi = 0
    for g in range(NB * 128 // TG):
        blks = [g * (TG // 128) + b for b in range(TG // 128)]
        for b, i in enumerate(blks):
            ss, rs = SS3[:, 2 * (i % 4):2 * (i % 4) + 1], SS3[:, 2 * (i % 4) + 1:2 * (i % 4) + 2]
            self.rms_plain(X1[:, i, :], "X1_%d" % i, HB3, "HB3", ss, rs, "SSPx%d" % (i % 4), HB3, "HB3")
            pbt, kbt = self.bank(4)
            pbb = v3(pbt.bitcast(BF16), t=128)
            for c in range(8):
                S.pe(lambda e, c=c, pbb=pbb: e.transpose(out=pbb[:, c, :], in_=HB3[:, c * 128:(c + 1) * 128], identity=self.IDENT), ["HB3", "CB"], [kbt])
            self.evac_copy(H3T3[:, :, b * 128:(b + 1) * 128], pbb, kbt, "H3T%d" % b, i)
        hkeys = ["H3T%d" % b for b in range(len(blks))]
        hq = {}

        def issue_h(hc, hq=hq, hkeys=hkeys):
            nonlocal hi
            ph, kph = self.bank(4)
            q = hc // 8
            for kc in range(8):
                S.pe(lambda e, ph=ph, kc=kc, hc=hc: e.matmul(ph[:, 0:TG], lhsT=W1M3[:, kc, hc * 128:(hc + 1) * 128], rhs=H3T3[:, kc, :], start=(kc == 0), stop=(kc == 7)),
                     ["W1M%d" % q] + hkeys, [kph])
            hid, khid = HID[hi % 3], "HID%d" % (hi % 3)
            hi += 1
            S.act(lambda e, hid=hid, ph=ph: e.activation(out=hid, in_=ph[:, 0:TG], func=AF.Relu), [kph], [khid, kph])
            S.dve(lambda e, hid=hid: e.tensor_tensor(out=hid, in0=hid, in1=hid, op=ALU.mult), [khid], [khid])
            hq[hc] = (hid, khid)

        issue_h(0); issue_h(1)
        for hc in range(32):
            if hc + 2 < 32:
                issue_h(hc + 2)
            hid, khid = hq.pop(hc)
            q = hc // 8
            for b in range(len(blks)):
                for half in range(2):
                    S.pe(lambda e, b=b, half=half, hid=hid, hc=hc: e.matmul(P[2 + b][:, half * 512:(half + 1) * 512], lhsT=hid[:, b * 128:(b + 1) * 128], rhs=W2M3[:, hc, half * 512:(half + 1) * 512],
                                                                          start=(hc == 0), stop=(hc == 31)), [khid, "W2M%d" % q], ["B%d" % (4 + 2 * b + half)])
        for b, i in enumerate(blks):
            kbs = ["B%d" % (4 + 2 * b), "B%d" % (5 + 2 * b)]
            self.post_norm_residual(P[2 + b], kbs, GP3, "GPOSTP", X1[:, i, :], "X1_%d" % i, X1[:, i, :], "X1_%d" % i, SS3[:, 16 + 4 * (i % 8):20 + 4 * (i % 8)], "SSPp%d" % (i % 8), HB3, "HB3", TMPH3, "HB3")
            op = S.dma("qsp", self.out[i * 128:(i + 1) * 128, :], X1[:, i, :], ["X1_%d" % i], [])
            self.final_ops.append(op)


def _rms_plain(self, xin, kx, hout, kh, ss, rs, kss, junk, kjunk):
    S = self.S
    S.act(lambda e: e.activation(out=junk[:, 0:1024], in_=xin, func=AF.Square, accum_out=ss), [kx], [kjunk, kss])
    S.act(lambda e: e.activation(out=rs, in_=ss, func=AF.Ln, scale=1.0 / 1024, bias=self.EPS6), [kss, "CT"], [kss + "r"])
    S.act(lambda e: e.activation(out=rs, in_=rs, func=AF.Exp, scale=-0.5), [kss + "r"], [kss + "r"])
    S.dve(lambda e: e.tensor_scalar(out=hout, in0=xin, scalar1=rs, scalar2=None, op0=ALU.mult), [kx, kss + "r"], [kh])


Builder.phase_D = _phase_D
Builder.rms_plain = _rms_plain
```

```python
import contextlib
import math
import types
import numpy as np
import concourse.bass as bass
import concourse.mybir as mybir
from concourse.bass_utils import run_bass_kernel_spmd

F32 = mybir.dt.float32
BF16 = mybir.dt.bfloat16
I32 = mybir.dt.int32
AF = mybir.ActivationFunctionType
ALU = mybir.AluOpType

T = 2048
NB = 16
D = 1024
STRICT = False
COMPUTE = ("pe", "act", "dve", "pool")
QUEUES = ("qsp", "qpool")


class Op:
    __slots__ = ("eng", "fn", "reads", "writes", "idx", "deps", "sig", "sem", "val", "pos", "alias")

    def __init__(self, eng, fn, reads, writes):
        self.eng, self.fn, self.reads, self.writes = eng, fn, reads, writes
        self.deps = set()
        self.sig = False
        self.sem = None
        self.val = 0
        self.alias = None


class Sched:
    def __init__(self, nc, n_dma_sems=20):
        self.nc = nc
        self.ops = []
        self.n_dma_sems = n_dma_sems

    @staticmethod
    def _freeze(fn):
        if fn is None or getattr(fn, "__closure__", None) is None:
            return fn
        cells = []
        for c in fn.__closure__:
            try:
                cells.append(types.CellType(c.cell_contents))
            except ValueError:
                cells.append(c)
        return types.FunctionType(fn.__code__, fn.__globals__, fn.__name__, fn.__defaults__, tuple(cells))

    def add(self, eng, fn, reads=(), writes=()):
        fn = self._freeze(fn)
        op = Op(eng, fn, tuple(reads), tuple(writes))
        (self._cap if self._cap is not None else self.ops).append(op)
        return op

    _cap = None

    def capture(self, emit_fn):
        prev = self._cap
        self._cap = []
        try:
            emit_fn()
        finally:
            out, self._cap = self._cap, prev
        return out

    def extend(self, ops):
        (self._cap if self._cap is not None else self.ops).extend(ops)

    @staticmethod
    def merge(a, b):
        out, i, j = [], 0, 0
        na, nb = len(a), len(b)
        while i < na or j < nb:
            if j >= nb or (i < na and i * nb <= j * na):
                out.append(a[i]); i += 1
            else:
                out.append(b[j]); j += 1
        return out

    def pe(self, fn, r, w): return self.add("pe", fn, r, w)
    def act(self, fn, r, w): return self.add("act", fn, r, w)
    def dve(self, fn, r, w): return self.add("dve", fn, r, w)
    def pool(self, fn, r, w): return self.add("pool", fn, r, w)

    def dma(self, q, out, in_, r, w):
        return self.add(q, lambda e: e.dma_start(out=out, in_=in_), r, w)

    def alias(self, new_key, old_keys):
        op = self.add("alias", None, (), ())
        op.alias = (new_key, tuple(old_keys))

    def build(self, final_wait_ops=()):
        nc = self.nc
        ops = self.ops
        for i_, op in enumerate(ops):
            op.idx = i_
        writers, readers = {}, {}
        eng_pos = {e: 0 for e in COMPUTE + QUEUES}
        for op in ops:
            if op.eng == "alias":
                continue
            op.pos = eng_pos[op.eng]
            eng_pos[op.eng] += 1
        pending = {}
        allkeys = set()

        def first_use(b):
            if b in allkeys:
                return
            allkeys.add(b)
            for nk, (w, r) in pending.items():
                if b.startswith(nk):
                    writers[b] = list(writers.get(b, [])) + w
                    readers[b] = list(readers.get(b, [])) + r

        for op in ops:
            if op.eng == "alias":
                nk, olds = op.alias
                w, r = [], []
                for k in list(allkeys):
                    if any(k.startswith(o) for o in olds):
                        w += writers.get(k, [])
                        r += readers.get(k, [])
                for o in olds:
                    if o in pending:
                        w += pending[o][0]
                        r += pending[o][1]
                pending[nk] = (w, r)
                for k in list(allkeys):
                    if k.startswith(nk):
                        allkeys.discard(k)
                        writers.pop(k, None)
                        readers.pop(k, None)
                continue
            for b in op.reads + op.writes:
                first_use(b)
            for b in op.reads:
                for lw in writers.get(b, ()):
                    op.deps.add(lw.idx)
            for b in op.writes:
                for lw in writers.get(b, ()):
                    op.deps.add(lw.idx)
                for rd in readers.get(b, ()):
                    op.deps.add(rd.idx)
            for b in op.reads:
                readers.setdefault(b, []).append(op)
            for b in op.writes:
                writers[b] = [op]
                readers[b] = []
            op.deps.discard(op.idx)
        dma_last = {}
        qcount = {q: 0 for q in QUEUES}
        for op in ops:
            if op.eng in QUEUES:
                slot = (op.eng, qcount[op.eng] % self.n_dma_sems)
                qcount[op.eng] += 1
                prev = dma_last.get(slot)
                if prev is not None:
                    op.deps.add(prev.idx)
                dma_last[slot] = op
                op.sem = slot
                op.sig = True
        for op in ops:
            if op.eng == "alias":
                continue
            keep = set()
            for di in op.deps:
                d = ops[di]
                if d.eng == op.eng and op.eng in COMPUTE:
                    if op.eng == "pe":
                        continue
                    raw = any(b in d.writes for b in op.reads)
                    if not STRICT and (not raw or (op.pos - d.pos) > 3):
                        continue
                keep.add(di)
                d.sig = True
            op.deps = keep
        for fo in final_wait_ops:
            fo.sig = True
        cnt = {e: 0 for e in COMPUTE}
        dcnt = {}
        for op in ops:
            if op.eng == "alias" or not op.sig:
                continue
            if op.eng in COMPUTE:
                cnt[op.eng] += 1
                op.sem = op.eng
                op.val = cnt[op.eng]
            else:
                dcnt[op.sem] = dcnt.get(op.sem, 0) + 16
                op.val = dcnt[op.sem]
        self.stats = dict(eng_pos)
        with contextlib.ExitStack() as es:
            sems = {}
            for e in COMPUTE:
                sems[e] = es.enter_context(nc.semaphore("s_" + e))
            for q in QUEUES:
                for i in range(self.n_dma_sems):
                    sems[(q, i)] = es.enter_context(nc.semaphore("d_%s_%d" % (q, i)))
            block = es.enter_context(nc.Block())

            def emit(engnames, eng, final=False):
                known = {}
                for op in ops:
                    if op.eng not in engnames:
                        continue
                    need = {}
                    for di in op.deps:
                        d = ops[di]
                        if need.get(d.sem, 0) < d.val:
                            need[d.sem] = d.val
                    for s, v in need.items():
                        if known.get(s, 0) >= v:
                            continue
                        eng.wait_ge(sems[s], v)
                        known[s] = v
                    ins = op.fn(eng)
                    if op.sig:
                        ins.then_inc(sems[op.sem], 16 if op.eng in QUEUES else 1)
                if final:
                    for fo in final_wait_ops:
                        if known.get(fo.sem, 0) < fo.val:
                            eng.wait_ge(sems[fo.sem], fo.val)
                            known[fo.sem] = fo.val

            @block.tensor
            def _(e): emit(("pe",), e)

            @block.scalar
            def _(e): emit(("act",), e)

            @block.vector
            def _(e): emit(("dve",), e)

            @block.gpsimd
            def _(e): emit(("pool", "qpool"), e)

            @block.sync
            def _(e): emit(("qsp",), e, final=True)


def _pcol_names():
    names = []
    for ci in range(15):
        for tap in range(3):
            names.append("conv%d_%d" % (ci, tap))
    for hp in range(4):
        names += ["kk%d" % hp, "ka%d" % hp, "lnw%d" % hp, "lnb%d" % hp, "rk%d" % hp]
        for d in range(2):
            names += ["w0_%d_%d" % (d, hp), "a0_%d_%d" % (d, hp)]
    names += ["invf", "sgn"]
    for kc in range(8):
        names += ["gm%d" % kc, "gt%d" % kc, "gp%d" % kc]
    return names


PCN = _pcol_names()
PCI = {n: i for i, n in enumerate(PCN)}
NPC = len(PCN)


def _pack_pcols(inp):
    pc = np.zeros((128, NPC), np.float32)
    conv = inp["conv_rwkv"][0]
    for ci in range(15):
        if ci < 12:
            cols = np.arange(ci * 128, ci * 128 + 128)
        elif ci == 12:
            cols = np.arange(1536, 1600)
        elif ci == 13:
            cols = np.arange(1600, 1664)
        else:
            cols = np.arange(1664, 1760)
        for tap in range(3):
            pc[: len(cols), PCI["conv%d_%d" % (ci, tap)]] = conv[tap, cols]
    for hp in range(4):
        sl = slice(hp * 128, hp * 128 + 128)
        pc[:, PCI["kk%d" % hp]] = inp["rwkv_k_k"][0][sl]
        pc[:, PCI["ka%d" % hp]] = inp["rwkv_k_a"][0][sl]
        pc[:, PCI["lnw%d" % hp]] = inp["rwkv_lnx_w"][0][sl]
        pc[:, PCI["lnb%d" % hp]] = inp["rwkv_lnx_b"][0][sl]
        pc[:, PCI["rk%d" % hp]] = inp["rwkv_r_k"][0].reshape(-1)[sl]
        for d in range(2):
            pc[:, PCI["w0_%d_%d" % (d, hp)]] = inp["rwkv_w0"][0][d][sl]
            pc[:, PCI["a0_%d_%d" % (d, hp)]] = inp["rwkv_a0"][0][d][sl]
    invf = (10000.0 ** (-np.arange(0, 32, 2, dtype=np.float32) / 32.0)).astype(np.float32)
    pc[64:80, PCI["invf"]] = invf
    pc[80:96, PCI["invf"]] = invf
    pc[64:80, PCI["sgn"]] = -1.0
    pc[80:96, PCI["sgn"]] = 1.0
    for kc in range(8):
        sl = slice(kc * 128, kc * 128 + 128)
        pc[:, PCI["gm%d" % kc]] = inp["norm_mem_pre"][0][sl]
        pc[:, PCI["gt%d" % kc]] = inp["norm_memtok"][0][sl]
        pc[:, PCI["gp%d" % kc]] = inp["norm_mlp_pre"][0][sl]
    return pc


def _consts():
    c = np.zeros((128, 1024), np.float32)
    c[:, 0:128] = np.eye(128)
    t = np.arange(128)[:, None]
    s = np.arange(128)[None, :]
    same = (t // 64) == (s // 64)
    c[:, 128:256] = same & (s > t)
    c[:, 256:384] = same & (s >= t)
    c[:, 384:512] = same & (s < t)
    c[:, 512:640] = same & (s <= t)
    c[:, 640:768] = same / 64.0
    c[:, 768:896] = same * 1.0
    c[:, 896:960] = np.eye(64)[np.arange(128) % 64]
    return c


def _host_inputs(inp, b):
    f = lambda a: np.ascontiguousarray(a, dtype=np.float32)
    w_in = inp["w_in"][0]
    w_in_x = np.concatenate([w_in[:, :1760], w_in[:, 1760:2144], w_in[:, 2144:2176],
                             w_in[:, 2160:2176], w_in[:, 2144:2160]], axis=1)
    w_uq = inp["mla_w_uq"][0].reshape(256, 8, 96)
    w_uq_sw = np.concatenate([w_uq[:, :, 80:96], w_uq[:, :, 64:80]], axis=2).reshape(256, 256)
    w_ukv = inp["mla_w_ukv"][0].reshape(128, 8, 128)
    w_uk = w_ukv[:, :, :64].reshape(128, 512)
    w_uv = w_ukv[:, :, 64:].reshape(128, 512)
    gains = np.stack([inp[n][0] for n in ["norm_mix_pre", "norm_mix_post", "norm_mem_pre", "norm_memtok",
                                         "norm_mem_post", "norm_mlp_pre", "norm_mlp_post"]], 0)
    gq = np.zeros((1, 1024), np.float32)
    gq[0, :256] = inp["mla_q_norm"][0]
    gq[0, 256:384] = inp["mla_kv_norm"][0]
    gains = np.concatenate([gains, gq], 0)
    return {
        "x": f(inp["x"][b]), "mem": f(inp["mem"][b]),
        "pos": np.ascontiguousarray(inp["positions"][b].reshape(1, T)).astype(np.int32),
        "pcols": _pack_pcols(inp), "consts": _consts(), "gains": f(gains),
        "w_in": f(w_in_x), "w2": f(inp["rwkv_w2"][0].reshape(64, 512)), "a2": f(inp["rwkv_a2"][0].reshape(64, 512)),
        "g2": f(inp["rwkv_g2"][0]), "w_uq": f(inp["mla_w_uq"][0]), "w_uq_sw": f(w_uq_sw),
        "w_uk": f(w_uk), "w_uv": f(w_uv), "w_out": f(inp["w_out"][0]),
        "mem_wq": f(inp["mem_wq"][0]), "mem_wkv": f(inp["mem_wkv"][0]), "mem_wo": f(inp["mem_wo"][0]),
        "mlp_w1": f(inp["mlp_w1"][0]), "mlp_w2": f(inp["mlp_w2"][0]),
    }


DRAM_SHAPES = {
    "x": ([T, D], F32), "mem": ([256, D], F32), "pos": ([1, T], I32), "pcols": ([128, NPC], F32),
    "consts": ([128, 1024], F32), "gains": ([8, 1024], F32), "w_in": ([1024, 2208], F32),
    "w2": ([64, 512], F32), "a2": ([64, 512], F32), "g2": ([96, 512], F32), "w_uq": ([256, 768], F32),
    "w_uq_sw": ([256, 256], F32), "w_uk": ([128, 512], F32), "w_uv": ([128, 512], F32),
    "w_out": ([1024, 1024], F32), "mem_wq": ([1024, 1024], F32), "mem_wkv": ([1024, 2048], F32),
    "mem_wo": ([1024, 1024], F32), "mlp_w1": ([1024, 4096], F32), "mlp_w2": ([4096, 1024], F32),
}

ARENA_COLS = 106200


class _Stop(Exception):
    pass


class Builder:
    def tick(self):
        self._ticks = getattr(self, "_ticks", 0) + 1
        if self._ticks >= getattr(self, "gstop", 10 ** 9):
            for (nm, ap, keys, shp) in getattr(self, "stop_taps", []):
                self.tap(nm, ap, keys, shp)
            raise _Stop()

    def tick2(self):
        self._ticks2 = getattr(self, "_ticks2", 0) + 1
        if self._ticks2 >= getattr(self, "bstop", 10 ** 9):
            raise _Stop()

    def __init__(self, stage=99, taps=()):
        self.stage = stage
        self.taps = taps
        self.nc = bass.Bass("TRN2", target_bir_lowering=False)
        nc = self.nc
        self.dr = {k: nc.dram_tensor(k, s, dt, kind="ExternalInput").ap() for k, (s, dt) in DRAM_SHAPES.items()}
        self.out = nc.dram_tensor("out", [T, D], F32, kind="ExternalOutput").ap()
        self.dbg = {}
        self.es = contextlib.ExitStack()
        self.arena = self.es.enter_context(nc.sbuf_tensor("arena", [128, ARENA_COLS], BF16))
        self.PS = [self.es.enter_context(nc.psum_tensor("ps%d" % i, [128, 1024], F32)) for i in range(4)]
        self.S = Sched(nc)
        self.off = 0
        self.live = {}
        self.final_ops = []
        self.uid = 0

    def alloc(self, key, ncols, dt):
        sz = 2 if dt == F32 else 1
        if dt == F32 and self.off % 2:
            self.off += 1
        st, en = self.off, self.off + ncols * sz
        assert en <= ARENA_COLS, ("arena overflow", key, en)
        self.off = en
        olds = [k for k, (a, b) in self.live.items() if a < en and st < b and k != key]
        if olds:
            self.S.alias(key, olds)
        self.live[key] = (st, en)
        ap = self.arena[:, st:en]
        if dt != BF16:
            ap = ap.bitcast(dt)
        return ap

    def mark(self): return self.off
    def reset(self, m): self.off = m

    def tap(self, name, ap, key, shape):
        if name not in self.taps:
            return
        nc = self.nc
        dt = ap.dtype
        o = nc.dram_tensor("dbg_" + name, list(shape), dt, kind="ExternalOutput").ap()
        op = self.S.dma("qsp", o, ap, list(key) if isinstance(key, (list, tuple)) else [key], [])
        self.final_ops.append(op)
        self.dbg[name] = (shape, dt)


def v3(ap, **kw):
    (k, val), = kw.items()
    return ap.rearrange("p (a %s) -> p a %s" % (k, k), **{k: val})


def _phase_A(self):
    S, nc, dr = self.S, self.nc, self.dr
    al = self.alloc
    self.CB = al("CB", 1024, BF16)
    S.dma("qpool", self.CB, dr["consts"], [], ["CB"])
    self.IDENT = self.CB[:, 0:128]
    self.PC = al("PC", NPC, F32)
    S.dma("qsp", self.PC, dr["pcols"], [], ["PC"])
    self.NPCt = al("NPC", NPC, F32)
    S.dve(lambda e: e.tensor_single_scalar(out=self.NPCt, in_=self.PC, scalar=-1.0, op=ALU.mult), ["PC"], ["NPC"])
    self.CT = al("CT", 8, F32)
    for i, v in enumerate([1.0, 1e-6, 64e-5, -math.pi, -0.5, 0.0, 1e-24]):
        S.pool(lambda e, i=i, v=v: e.memset(self.CT[:, i:i + 1], v), [], ["CT"])
    self.ONE, self.EPS6, self.EPSLN, self.NEGPI, self.NEGH, self.ZERO = (self.CT[:, i:i + 1] for i in range(6))
    self.bankrr = 0
    self.COS = al("COS", T, BF16)
    self.SINS = al("SINS", T, BF16)
    self.CQKV3 = CQKV3 = v3(al("CQKV", 16 * 384, F32), n=384)
    self.KR3 = KR3 = v3(al("KR", 2 * T, BF16), t=T)
    ZC = al("ZC", 15 * T, BF16)
    self.ZC3 = ZC3 = v3(ZC, t=T)
    self.mA = self.mark()
    tab_ops = S.capture(self.rope_tables)
    ZR3 = v3(al("ZR", 2 * T, BF16), t=T)
    TMP = [al("CTMP%d" % i, T, F32) for i in range(2)]
    HT = al("HT", 8 * T, BF16)
    self.HT3 = HT3 = v3(HT, t=T)
    XB = [al("XB%d" % i, 1024, F32) for i in range(2)]
    HB = [al("HB%d" % i, 1024, BF16) for i in range(2)]
    JUNK = al("JUNK", 1024, BF16)
    SS = al("SS", 32, F32)
    G1 = al("G1", 1024, F32)
    S.dma("qsp", G1, dr["gains"][0:1, :].partition_broadcast(128), [], ["G1"])
    def norm_loop():
        for i in range(NB):
            xb, kx = XB[i % 2], "XB%d" % (i % 2)
            hb, kh = HB[i % 2], "HB%d" % (i % 2)
            S.dma("qsp", xb, dr["x"][i * 128:(i + 1) * 128, :], [], [kx])
            self.rms_to_bf16(xb, kx, hb, kh, G1, "G1", SS[:, i:i + 1], SS[:, 16 + i:17 + i], "SS%d" % i, JUNK, 1024)
            self.transpose_to(hb, kh, 8, lambda c: HT3[:, c, i * 128:(i + 1) * 128], HT3[:, :, i * 128:(i + 1) * 128], "HT%d" % i, i)

    S.extend(Sched.merge(S.capture(norm_loop), tab_ops))
    WB = [v3(al("WB%d" % i, 8 * 512, BF16), n=512) for i in range(2)]
    groups = [(0, 512), (512, 512), (1024, 512), (1536, 224), (1760, 384), (2144, 64)]
    wsrc = dr["w_in"].rearrange("(kc p) n -> p kc n", p=128)
    ev = 0
    for gi, (c0, ncol) in enumerate(groups):
        wb, kw = WB[gi % 2], "WB%d" % (gi % 2)
        S.dma("qpool", wb[:, :, 0:ncol], wsrc[:, :, c0:c0 + ncol], [], [kw])
        if gi < 4:
            chunks = [(gi * 4 + j, j * 128, 128) for j in range(4)] if gi < 3 else [(12, 0, 64), (13, 64, 64), (14, 128, 96)]
            for (ci, co, m) in chunks:
                for tg in range(4):
                    pb, kb = self.bank()
                    for kc in range(8):
                        S.pe(lambda e, pb=pb, kc=kc, co=co, m=m, tg=tg, wb=wb: e.matmul(
                            pb[0:m, :], lhsT=wb[:, kc, co:co + m], rhs=HT3[:, kc, tg * 512:(tg + 1) * 512],
                            start=(kc == 0), stop=(kc == 7)), [kw] + ["HT%d" % (4 * tg + q) for q in range(4)], [kb])
                    self.evac_copy(ZR3[0:m, ci % 2, tg * 512:(tg + 1) * 512], pb[0:m, :], kb, "ZR%d_%d" % (ci % 2, tg), ev)
                    ev += 1
                self.conv_chunk(ci, m, ZR3, ZC3, TMP)
        elif gi == 4:
            for i in range(NB):
                pb, kb = self.bank()
                for kc in range(8):
                    S.pe(lambda e, pb=pb, kc=kc, i=i, wb=wb: e.matmul(
                        pb[:, 0:384], lhsT=HT3[:, kc, i * 128:(i + 1) * 128], rhs=wb[:, kc, 0:384],
                        start=(kc == 0), stop=(kc == 7)), [kw, "HT%d" % i], [kb])
                self.evac_copy(CQKV3[:, i, :], pb[:, 0:384], kb, "CQKV%d" % i, ev)
                ev += 1
        else:
            for tg in range(4):
                for s in range(2):
                    pb, kb = self.bank()
                    for kc in range(8):
                        S.pe(lambda e, pb=pb, kc=kc, s=s, tg=tg, wb=wb: e.matmul(
                            pb[64:96, :], lhsT=wb[:, kc, 32 * s:32 * s + 32], rhs=HT3[:, kc, tg * 512:(tg + 1) * 512],
                            start=(kc == 0), stop=(kc == 7)), [kw] + ["HT%d" % (4 * tg + q) for q in range(4)], [kb])
                    self.evac_copy(KR3[64:96, s, tg * 512:(tg + 1) * 512], pb[64:96, :], kb, "KR", ev)
                    ev += 1
    self.tap("ZC", ZC[:, 0:12 * T], ["ZC%d" % i for i in range(12)], [128, 12 * T])
    self.tap("CQKV", self.CQKV3.rearrange("p a n -> p (a n)"), ["CQKV%d" % i for i in range(16)], [128, 16 * 384])
    self.tap("KR", self.KR3[64:96].rearrange("p a n -> p (a n)"), "KR", [32, 2 * T])


def _conv_chunk(self, ci, m, ZR3, ZC3, TMP):
    S = self.S
    tmp, kt = TMP[ci % 2], "CTMP%d" % (ci % 2)
    zr = ZR3[0:m, ci % 2, :]
    rk = ["ZR%d_%d" % (ci % 2, tg) for tg in range(4)]
    c0, c1, c2 = (self.PC[0:m, PCI["conv%d_%d" % (ci, t)]:PCI["conv%d_%d" % (ci, t)] + 1] for t in range(3))
    S.act(lambda e: e.activation(out=tmp[0:m, :], in_=zr, func=AF.Copy, scale=c1), rk + ["PC"], [kt])
    S.dve(lambda e: e.scalar_tensor_tensor(
        out=tmp[0:m, 1:T], in0=zr[:, 0:T - 1], scalar=c0, in1=tmp[0:m, 1:T], op0=ALU.mult, op1=ALU.add), rk + ["PC", kt], [kt])
    S.dve(lambda e: e.scalar_tensor_tensor(
        out=ZC3[0:m, ci, 0:T - 1], in0=zr[:, 1:T], scalar=c2, in1=tmp[0:m, 0:T - 1], op0=ALU.mult, op1=ALU.add),
        rk + ["PC", kt], ["ZC%d" % ci])
    S.act(lambda e: e.activation(out=ZC3[0:m, ci, T - 1:T], in_=tmp[0:m, T - 1:T], func=AF.Copy), [kt], ["ZC%d" % ci])


Builder.conv_chunk = _conv_chunk


def _rope_tables(self):
    S, dr = self.S, self.dr
    al = self.alloc
    R = slice(64, 96)
    NQ = 4
    HW = T // NQ
    POSF = al("RPOS", T, F32)
    TY = al("RTY", HW, F32); TY2 = al("RTY2", HW, F32); KF = al("RKF", HW, F32)
    TI = TY.bitcast(I32)
    S.dma("qpool", POSF, dr["pos"].partition_broadcast(128), [], ["RPOS"])
    invf = self.PC[R, PCI["invf"]:PCI["invf"] + 1]
    sgn = self.PC[R, PCI["sgn"]:PCI["sgn"] + 1]
    S.dve(lambda e: e.tensor_scalar(out=POSF[R, :], in0=POSF[R, :], scalar1=invf, scalar2=None, op0=ALU.mult), ["RPOS", "PC"], ["RPOS"])
    for hf in range(NQ):
        cs = slice(hf * HW, (hf + 1) * HW)
        for (dst, kd, shift) in ((self.SINS, "SINS", 0.0), (self.COS, "COS", 0.25)):
            S.act(lambda e, shift=shift, cs=cs: e.activation(out=KF[R, :], in_=POSF[R, cs], func=AF.Copy, scale=1.0 / (2 * math.pi)), ["RPOS"], ["RKF"])
            if shift:
                S.dve(lambda e, shift=shift: e.tensor_single_scalar(out=KF[R, :], in_=KF[R, :], scalar=shift, op=ALU.add), ["RKF"], ["RKF"])
            S.dve(lambda e: e.tensor_copy(out=TI[R, :], in_=KF[R, :]), ["RKF"], ["RTY"])
            S.dve(lambda e: e.tensor_copy(out=TY2[R, :], in_=TI[R, :]), ["RTY"], ["RTY2"])
            S.dve(lambda e: e.tensor_tensor(out=KF[R, :], in0=KF[R, :], in1=TY2[R, :], op=ALU.subtract), ["RKF", "RTY2"], ["RKF"])
            S.dve(lambda e: e.tensor_single_scalar(out=TY2[R, :], in_=KF[R, :], scalar=0.5, op=ALU.is_gt), ["RKF"], ["RTY2"])
            S.dve(lambda e: e.tensor_tensor(out=KF[R, :], in0=KF[R, :], in1=TY2[R, :], op=ALU.subtract), ["RKF", "RTY2"], ["RKF"])
            S.dve(lambda e: e.tensor_single_scalar(out=TY2[R, :], in_=KF[R, :], scalar=-0.5, op=ALU.is_lt), ["RKF"], ["RTY2"])
            S.dve(lambda e: e.tensor_tensor(out=KF[R, :], in0=KF[R, :], in1=TY2[R, :], op=ALU.add), ["RKF", "RTY2"], ["RKF"])
            S.act(lambda e, dst=dst, cs=cs: e.activation(out=dst[R, cs], in_=KF[R, :], func=AF.Sin, scale=2 * math.pi), ["RKF"], [kd + "%d" % hf])
        S.dve(lambda e, cs=cs: e.tensor_scalar(out=self.SINS[R, cs], in0=self.SINS[R, cs], scalar1=sgn, scalar2=None, op0=ALU.mult), ["SINS%d" % hf, "PC"], ["SINS%d" % hf])


Builder.rope_tables = _rope_tables


def _bank(self, nmax=8):
    if getattr(self, "bankset", None):
        return self.bank_sel(self.bankset)
    b = self.bankrr % nmax
    self.bankrr += 1
    return self.PS[b // 2][:, (b % 2) * 512:(b % 2) * 512 + 512], "B%d" % b


def _evac_copy(self, out, in_, kin, kout, parity, extra_reads=()):
    if parity % 2 == 0:
        self.S.act(lambda e: e.activation(out=out, in_=in_, func=AF.Copy), [kin] + list(extra_reads), [kout, kin])
    else:
        self.S.dve(lambda e: e.tensor_copy(out=out, in_=in_), [kin] + list(extra_reads), [kout, kin])


def _rms_to_bf16(self, xin, kx, hout, kh, G, kg, ss, rs, kss, junk, n, gcols=None, kj="JUNK"):
    S = self.S
    gsl = G if gcols is None else gcols
    S.act(lambda e: e.activation(out=junk[:, 0:n], in_=xin, func=AF.Square, accum_out=ss), [kx], [kj, kss])
    S.act(lambda e: e.activation(out=rs, in_=ss, func=AF.Ln, scale=1.0 / n, bias=self.EPS6), [kss, "CT"], [kss + "r"])
    S.act(lambda e: e.activation(out=rs, in_=rs, func=AF.Exp, scale=-0.5), [kss + "r"], [kss + "r"])
    S.dve(lambda e: e.scalar_tensor_tensor(out=hout, in0=xin, scalar=rs, in1=gsl, op0=ALU.mult, op1=ALU.mult),
          [kx, kss + "r", kg], [kh])


def _transpose_to(self, src, ksrc, nch, dst_of, dst_all, kdst, parity, rows=128):
    S = self.S
    pb, kb = self.bank()
    pbb = v3(pb.bitcast(BF16)[:, 0:nch * 128], t=128)
    for c in range(nch):
        S.pe(lambda e, c=c: e.transpose(out=pbb[:, c, :], in_=src[:, c * 128:(c + 1) * 128], identity=self.IDENT),
             [ksrc, "CB"], [kb])
    self.evac_copy(dst_all, pbb, kb, kdst, parity)


Builder.phase_A = _phase_A
def _bank_sel(self, banks):
    b = banks[self.bankrr % len(banks)]
    self.bankrr += 1
    return self.PS[b // 2][:, (b % 2) * 512:(b % 2) * 512 + 512], "B%d" % b


Builder.bank = _bank
Builder.bank_sel = _bank_sel
Builder.evac_copy = _evac_copy
Builder.rms_to_bf16 = _rms_to_bf16
Builder.transpose_to = _transpose_to


def _finish(self):
    self.S.build(final_wait_ops=self.final_ops)
    return self.nc


Builder.finish = _finish


def _alloc_at(self, key, start, ncols, dt, olds):
    sz = 2 if dt == F32 else 1
    self.S.alias(key, olds)
    self.live[key] = (start, start + ncols * sz)
    ap = self.arena[:, start:start + ncols * sz]
    return ap.bitcast(dt) if dt != BF16 else ap


def _sigmoid3(self, out, in_, kin, kout, tmp, ktmp, scale=1.0, bias=None, kb=(), final_scale=-1.0, final_bias=None):
    S = self.S
    nb = self.ZERO_r(tmp) if bias is None else bias
    S.act(lambda e: e.activation(out=tmp, in_=in_, func=AF.Exp, scale=-scale, bias=nb), [kin, "CT", "NPC"] + list(kb), [ktmp] + ([kin] if kin.startswith("B") else []))
    S.act(lambda e: e.activation(out=tmp, in_=tmp, func=AF.Ln, bias=self.ONE_r(tmp)), [ktmp, "CT"], [ktmp])
    fb = self.ZERO_r(tmp) if final_bias is None else final_bias
    S.act(lambda e: e.activation(out=out, in_=tmp, func=AF.Exp, scale=final_scale, bias=fb), [ktmp, "CT"], [kout])


def _rows(ap):
    p0 = ap.base_partition()
    return slice(p0, p0 + ap.shape[0])


Builder.alloc_at = _alloc_at
Builder.sigmoid3 = _sigmoid3
Builder.ONE_r = lambda self, ap: self.CT[_rows(ap), 0:1]
Builder.ZERO_r = lambda self, ap: self.CT[_rows(ap), 5:6]
Builder.NEGH_r = lambda self, ap: self.CT[_rows(ap), 4:5]


def _phase_B(self):
    S, nc, dr = self.S, self.nc, self.dr
    al = self.alloc
    self.reset(self.mA)
    self.YM3 = YM3 = v3(al("YM", 4 * T, BF16), t=T)
    self.mB = self.mark()
    R = slice(64, 96)
    KROT = al("KROT", T, BF16)
    COS, SINS = self.COS, self.SINS
    CQNT3 = v3(al("CQNT", 2 * T, BF16), t=T)
    CKVNT = al("CKVNT", T, BF16)
    VA4 = al("VA", 16 * 8 * 65, BF16).rearrange("p (k h c) -> p k h c", h=8, c=65)
    WQ3 = v3(al("WQ", 2 * 768, BF16), n=768)
    WQS3 = v3(al("WQS", 2 * 256, BF16), n=256)
    WUK = al("WUK", 512, BF16)
    WUV = al("WUV", 512, BF16)
    ONEF = al("ONEF", 64, F32)
    SSB = al("SSB", 64, F32)
    mt = self.mark()
    POSF = al("POSF", T, F32)
    TY = al("TY", T, F32)
    TY2 = al("TY2", T, F32)
    KROTF = al("KROTF", T, F32)
    self.reset(mt)
    S.pool(lambda e: e.memset(ONEF, 1.0), [], ["ONEF"])
    S.pool(lambda e: e.memset(SSB, 0.0), [], ["SSBz"])
    S.pool(lambda e: e.memset(VA4[:, :, :, 64:65], 1.0), [], ["VAone"])
    S.dma("qpool", WQ3, dr["w_uq"].rearrange("(kc p) n -> p kc n", p=128), [], ["WQ"])
    S.dma("qpool", WQS3, dr["w_uq_sw"].rearrange("(kc p) n -> p kc n", p=128), [], ["WQS"])
    S.dma("qpool", WUK, dr["w_uk"], [], ["WUK"])
    S.dma("qpool", WUV, dr["w_uv"], [], ["WUV"])
    S.dve(lambda e: e.tensor_tensor(out=TY[R, :], in0=self.KR3[R, 0, :], in1=COS[R, :], op=ALU.mult), ["KR", "COS0", "COS1", "COS2", "COS3", "TY"], ["TY"])
    S.dve(lambda e: e.tensor_tensor(out=POSF[R, :], in0=self.KR3[R, 1, :], in1=SINS[R, :], op=ALU.mult), ["KR", "SINS0", "SINS1", "SINS2", "SINS3", "POSF"], ["POSF"])
    S.dve(lambda e: e.tensor_tensor(out=KROT[R, :], in0=TY[R, :], in1=POSF[R, :], op=ALU.add), ["TY", "POSF"], ["KROT"])
    QT = [al("QT%d" % i, T, BF16) for i in range(2)]
    KT = [al("KT%d" % i, T, BF16) for i in range(2)]
    PT = [al("PT%d" % i, 512, BF16) for i in range(4)]
    RDEN = al("RDEN", 512, F32)
    BC = al("BCAST", 512, F32)
    TYR = al("TYR", 1024, F32)
    GQ = al("GQ", 384, F32)
    CQN = [al("CQN%d" % i, 384, BF16) for i in range(2)]
    JUNK = al("JUNKB", 384, BF16)
    S.dma("qsp", GQ, dr["gains"][7:8, 0:384].partition_broadcast(128), [], ["GQ"])
    for i in range(NB):
        cn, kc_ = CQN[i % 2], "CQN%d" % (i % 2)
        self.rms_to_bf16(self.CQKV3[:, i, 0:256], "CQKV%d" % i, cn[:, 0:256], kc_, GQ, "GQ", SSB[:, i:i + 1], SSB[:, 16 + i:17 + i],
                         "SSBq%d" % i, JUNK, 256, gcols=GQ[:, 0:256], kj="JUNKB")
        self.rms_to_bf16(self.CQKV3[:, i, 256:384], "CQKV%d" % i, cn[:, 256:384], kc_ + "b", GQ, "GQ", SSB[:, 32 + i:33 + i], SSB[:, 48 + i:49 + i],
                         "SSBk%d" % i, JUNK, 128, gcols=GQ[:, 256:384], kj="JUNKB")
        pb, kb = self.bank()
        pbb = v3(pb.bitcast(BF16)[:, 0:384], t=128)
        for c in range(3):
            S.pe(lambda e, c=c, cn=cn, pbb=pbb: e.transpose(out=pbb[:, c, :], in_=cn[:, c * 128:(c + 1) * 128], identity=self.IDENT),
                 [kc_, kc_ + "b", "CB"], [kb])
        sl = slice(i * 128, (i + 1) * 128)
        self.evac_copy(CQNT3[:, :, sl], pbb[:, 0:2, :], kb, "CQNT%d" % i, i)
        self.evac_copy(CKVNT[:, sl], pbb[:, 2, :], kb, "CKVNT%d" % i, i + 1)
    allq = ["CQNT%d" % i for i in range(NB)]
    allk = ["CKVNT%d" % i for i in range(NB)]
    for kbk in range(NB):
        pb, kb = self.bank()
        S.pe(lambda e, pb=pb, kbk=kbk: e.matmul(pb, lhsT=CKVNT[:, kbk * 128:(kbk + 1) * 128], rhs=WUV, start=True, stop=True),
             ["CKVNT%d" % kbk, "WUV"], [kb])
        self.evac_copy(VA4[:, kbk, :, 0:64], v3(pb, c=64), kb, "VA%d" % kbk, kbk)
    sc = 96.0 ** -0.5
    ev = 0
    pocnt = 0
    pending_tail = None
    pbc, kpbc = self.PS[2][:, 512:1024], "B5"

    ONES64 = self.CB[:, 768:832]
    ONES64 = al("ONES64", 64, BF16)
    S.pool(lambda e: e.memset(ONES64, 1.0), [], ["ONES64"])

    def emit_tail(po, kpo, h, qg, qs):
        S.dve(lambda e: e.reciprocal(out=BC[0:64, :], in_=po[64:128, :]), [kpo], ["BCAST", kpo])
        eo = 64 * (h % 2)
        S.dve(lambda e: e.tensor_tensor(out=YM3[eo:eo + 64, h // 2, qs], in0=po[0:64, :], in1=BC[0:64, :], op=ALU.mult),
              [kpo, "BCAST"], ["YM%d_%d" % (h, qg), kpo])

    for i in range(2):
        S.dve(lambda e, i=i: e.tensor_copy(out=KT[i][R, :], in_=KROT[R, :]), ["KROT"], ["KT%dr" % i])

    def proj(h):
        qt, kq = QT[h % 2], "QT%d" % (h % 2)
        kt, kk_ = KT[h % 2], "KT%d" % (h % 2)
        for tg in range(4):
            ts = slice(tg * 512, (tg + 1) * 512)
            tq = ["CQNT%d" % (4 * tg + q) for q in range(4)]
            tk = ["CKVNT%d" % (4 * tg + q) for q in range(4)]
            pq, kpq = self.bank_sel((0, 1, 2, 3, 4, 5))
            ps, kps = self.bank_sel((0, 1, 2, 3, 4, 5))
            pk, kpk = self.bank_sel((0, 1, 2, 3, 4, 5))
            for kc in range(2):
                S.pe(lambda e, kc=kc, pq=pq, ts=ts, h=h: e.matmul(pq[0:96, :], lhsT=WQ3[:, kc, h * 96:(h + 1) * 96], rhs=CQNT3[:, kc, ts],
                                                                  start=(kc == 0), stop=(kc == 1)), tq + ["WQ"], [kpq])
            for kc in range(2):
                S.pe(lambda e, kc=kc, ps=ps, ts=ts, h=h: e.matmul(ps[64:96, :], lhsT=WQS3[:, kc, h * 32:(h + 1) * 32], rhs=CQNT3[:, kc, ts],
                                                                  start=(kc == 0), stop=(kc == 1)), tq + ["WQS"], [kps])
            S.pe(lambda e, pk=pk, ts=ts, h=h: e.matmul(pk[0:64, :], lhsT=WUK[:, h * 64:(h + 1) * 64], rhs=CKVNT[:, ts], start=True, stop=True),
                 tk + ["WUK"], [kpk])
            self.evac_copy(qt[0:64, ts], pq[0:64, :], kpq, kq + "_%d" % tg, 1)
            S.dve(lambda e, pq=pq, ts=ts: e.tensor_tensor(out=TYR[R, 0:512], in0=pq[R, :], in1=COS[R, ts], op=ALU.mult), [kpq, "COS0", "COS1", "COS2", "COS3"], ["TYRa", kpq])
            S.dve(lambda e, ps=ps, ts=ts: e.tensor_tensor(out=TYR[R, 512:1024], in0=ps[R, :], in1=SINS[R, ts], op=ALU.mult), [kps, "SINS0", "SINS1", "SINS2", "SINS3"], ["TYRb", kps])
            S.dve(lambda e, qt=qt, ts=ts: e.tensor_tensor(out=qt[R, ts], in0=TYR[R, 0:512], in1=TYR[R, 512:1024], op=ALU.add), ["TYRa", "TYRb"], [kq + "_%d" % tg])
            self.evac_copy(kt[0:64, ts], pk[0:64, :], kpk, kk_ + "_%d" % tg, 1)

    proj(0)
    for h in range(8):
        qt, kq = QT[h % 2], "QT%d" % (h % 2)
        kt, kk_ = KT[h % 2], "KT%d" % (h % 2)
        qkeys = [kq + "_%d" % tg for tg in range(4)]
        kkeys = [kk_ + "_%d" % tg for tg in range(4)] + [kk_ + "r"]
        for qg in range(4):
            qs = slice(qg * 512, (qg + 1) * 512)
            pob = 6 + (pocnt % 2)
            pocnt += 1
            po, kpo = self.PS[3][:, (pob - 6) * 512:(pob - 6) * 512 + 512], "B%d" % pob
            scq = {}

            def issue_score(kbk, qs=qs, kt=kt, qt=qt, qkeys=qkeys, kkeys=kkeys, scq=scq):
                pscore, kpsc = self.bank_sel((0, 1, 2, 3, 4))
                S.pe(lambda e, pscore=pscore, kbk=kbk, qs=qs, kt=kt, qt=qt: e.matmul(
                    pscore, lhsT=kt[0:96, kbk * 128:(kbk + 1) * 128], rhs=qt[0:96, qs], start=True, stop=True), qkeys + kkeys, [kpsc])
                scq[kbk] = (pscore, kpsc)

            AHEAD = 3
            for kbk in range(min(AHEAD, NB)):
                issue_score(kbk)
            if pending_tail is not None:
                emit_tail(*pending_tail)
            for kbk in range(NB):
                if kbk + AHEAD < NB:
                    issue_score(kbk + AHEAD)
                pscore, kpsc = scq.pop(kbk)
                pt, kpt = PT[ev % 4], "PT%d" % (ev % 4)
                ev += 1
                S.act(lambda e, pt=pt, pscore=pscore: e.activation(out=pt, in_=pscore, func=AF.Exp, scale=sc), [kpsc], [kpt, kpsc])
                S.pe(lambda e, po=po, pt=pt, kbk=kbk, h=h: e.matmul(po[0:64, :], lhsT=VA4[:, kbk, h, 0:64], rhs=pt,
                                                                     start=(kbk == 0), stop=(kbk == NB - 1)), [kpt, "VA%d" % kbk], [kpo])
                S.pe(lambda e, po=po, pt=pt, kbk=kbk: e.matmul(po[64:128, :], lhsT=ONES64, rhs=pt,
                                                               start=(kbk == 0), stop=(kbk == NB - 1)), [kpt, "ONES64"], [kpo])
            pending_tail = (po, kpo, h, qg, qs)
            if qg == 1 and h + 1 < 8:
                proj(h + 1)
    emit_tail(*pending_tail)
    self.tap("KROT", KROT[R, :], ["KROT"], [32, T])
    self.tap("QT", QT[1][0:96, :], ["QT1_%d" % t for t in range(4)], [96, T])
    self.tap("KT", KT[1][0:96, :], ["KT1_%d" % t for t in range(4)] + ["KT1r"], [96, T])
    self.tap("COS", COS[R, :], ["COS0", "COS1", "COS2", "COS3"], [32, T])
    self.tap("SINS", SINS[R, :], ["SINS0", "SINS1", "SINS2", "SINS3"], [32, T])
    self.tap("CKVNT", CKVNT, allk, [128, T])
    self.tap("YM", self.YM3.rearrange("p a n -> p (a n)"), ["YM%d_%d" % (h, qg) for h in range(8) for qg in range(4)], [128, 4 * T])


Builder.phase_B = _phase_B


def _phase_C(self):
    S, nc, dr = self.S, self.nc, self.dr
    al = self.alloc
    if not hasattr(self, "mB"):
        self.reset(self.mA)
        self.YM3 = v3(al("YM", 4 * T, BF16), t=T)
        self.mB = self.mark()
    self.reset(self.mB)
    ZC3 = self.ZC3
    CB = self.CB
    MS0, MS1 = CB[:, 128:384], CB[:, 384:640]
    BAVG, BONES, ID64 = CB[:, 640:768], CB[:, 768:896], CB[:, 896:960]
    pcol = lambda n, rows=slice(0, 128): self.PC[rows, PCI[n]:PCI[n] + 1]
    ncol = lambda n, rows=slice(0, 128): self.NPCt[rows, PCI[n]:PCI[n] + 1]
    cq0 = self.live["CQKV"][0]
    assert self.live["KR"][1] - cq0 >= 4 * T * 2
    OACC3 = v3(self.alloc_at("OACC", cq0, 4 * T, F32, ["CQKV", "KR"]), t=T)
    W2 = al("W2", 512, BF16); A2 = al("A2", 512, BF16); G2 = al("G2", 512, BF16)
    S.dma("qpool", W2[0:64, :], dr["w2"], [], ["W2"])
    S.dma("qpool", A2[0:64, :], dr["a2"], [], ["A2"])
    S.dma("qpool", G2[0:96, :], dr["g2"], [], ["G2"])
    self.G2 = G2; self.A2 = A2
    KK3 = v3(al("KK", 4 * T, BF16), t=T)
    mC = self.mark()
    self.mC = mC
    tmps = [al("TC%d" % i, 512, F32) for i in range(7)]
    t1, t2, NLW, CP, CWN, tB, tK = tmps
    k1, k2, kNLW, kCP, kCWN, ktB, ktK = ["TC%d" % i for i in range(7)]
    self.stop_taps = [("W2", W2[0:64, :], ["W2"], [64, 512]), ("CTt", self.CT, ["CT"], [128, 8]), ("NPCt", self.NPCt, ["NPC"], [128, NPC]),
                      ("TXWg", ZC3[0:64, 12, 512:1024], ["ZC12"], [64, 512]), ("NLW", NLW, ["TC2"], [128, 512]), ("T1", t1, ["TC0"], [128, 512]), ("T2", t2, ["TC1"], [128, 512]), ("CP", CP, ["TC3"], [128, 512])]
    RESETM = al("RESETM", 512, F32)
    S.pool(lambda e: e.memset(RESETM, 1.0), [], ["RESETM"])
    S.pool(lambda e: e.memset(v3(RESETM, l=64)[:, :, 0:1], 0.0), [], ["RESETM"])
    OMK = al("OMK", 4, F32)
    for hp in range(4):
        S.dve(lambda e, hp=hp: e.tensor_scalar(out=OMK[:, hp:hp + 1], in0=pcol("ka%d" % hp), scalar1=-1.0, scalar2=1.0, op0=ALU.mult, op1=ALU.add), ["PC"], ["OMK"])
    cstop = getattr(self, "cstop", 99)
    if cstop <= 1:
        return
    XW = ZC3[0:64, 12, :]; XA = ZC3[0:64, 13, :]; XG = ZC3[0:96, 14, :]
    for tg in range(4):
        ts = slice(tg * 512, (tg + 1) * 512)
        self.sigmoid3(t1[0:64, :], XW[:, ts], "ZC12", k1, t2[0:64, :], k2, scale=2.0)
        S.dve(lambda e, ts=ts: e.tensor_scalar(out=XW[:, ts], in0=t1[0:64, :], scalar1=2.0, scalar2=-1.0, op0=ALU.mult, op1=ALU.add), [k1], ["ZC12"])
        self.sigmoid3(t1[0:96, :], XG[:, ts], "ZC14", k1, t2[0:96, :], k2)
        S.dve(lambda e, ts=ts: e.tensor_copy(out=XG[:, ts], in_=t1[0:96, :]), [k1], ["ZC14"])
        if cstop <= 2:
            continue
        for hp in range(4):
            kf = ZC3[:, 4 + hp, ts]
            S.dve(lambda e, kf=kf, hp=hp: e.tensor_scalar(out=t1, in0=kf, scalar1=pcol("kk%d" % hp), scalar2=None, op0=ALU.mult), ["ZC%d" % (4 + hp), "PC"], [k1])
            S.act(lambda e: e.activation(out=tB, in_=t1, func=AF.Square), [k1], [ktB])
            S.dve(lambda e: e.tensor_copy(out=tK.bitcast(BF16)[:, 0:512], in_=tB), [ktB], [ktK])
            pb, kb = self.bank()
            S.pe(lambda e, pb=pb: e.matmul(pb, lhsT=BONES, rhs=tK.bitcast(BF16)[:, 0:512], start=True, stop=True), [ktK, "CB"], [kb])
            S.dve(lambda e, pb=pb: e.tensor_scalar(out=t2, in0=pb, scalar1=1e-24, scalar2=None, op0=ALU.max), [kb], [k2, kb])
            S.act(lambda e: e.activation(out=t2, in_=t2, func=AF.Ln), [k2], [k2])
            S.act(lambda e: e.activation(out=t2, in_=t2, func=AF.Exp, scale=-0.5), [k2], [k2])
            S.dve(lambda e, hp=hp, ts=ts: e.tensor_tensor(out=KK3[:, hp, ts], in0=t1, in1=t2, op=ALU.mult), [k1, k2], ["KK%d_%d" % (hp, tg)])
    if cstop <= 3:
        return
    GW = 256
    DSET = []
    for kd in range(2):
        ARk = al("AR%d" % kd, 4 * 2 * 2 * 128, BF16).rearrange("p (h b s t) -> p h b s t", h=4, b=2, s=2)
        BTk = v3(al("BT%d" % kd, 4 * GW, BF16), t=GW); KTLk = v3(al("KTL%d" % kd, 4 * GW, BF16), t=GW)
        BCk = v3(al("BCH%d" % kd, 4 * GW, BF16), t=GW); KCk = v3(al("KCH%d" % kd, 4 * GW, BF16), t=GW)
        ELk = v3(al("EL%d" % kd, 16, F32), c=4)
        WLDk = al("WLD%d" % kd, 4 * 4 * 64, BF16).rearrange("p (h c k) -> p h c k", h=4, c=4)
        DSET.append((ARk, BTk, KTLk, BCk, KCk, ELk, WLDk))
    TSET = []
    for kt in range(2):
        tt = [tm[:, kt * GW:(kt + 1) * GW] for tm in tmps]
        tk_ = ["TC%dh%d" % (i, kt) for i in range(7)]
        for i in range(7):
            S.alias(tk_[i], ["TC%d" % i])
        TSET.append((tt, tk_))
    self._lo = self.live["COS"][0]

    def al_lo(key, ncols):
        ap = self.alloc_at(key, self._lo, ncols, BF16, ["COS", "SINS"])
        self._lo += ncols
        assert self._lo <= self.live["SINS"][1]
        return ap

    VT = al_lo("VT", 512)
    VPAD3 = v3(al("VPAD", 1024, BF16), n=128)
    UVPAD3 = v3(al("UVPAD", 1024, BF16), n=128)
    MY = [v3(al("MY%d" % i, 2048, BF16), n=256) for i in range(2)]
    MT = [v3(al("MT%d" % i, 1024, BF16), n=128) for i in range(2)]
    RB3 = v3(al("RB", 2048, BF16), n=256)
    MAKT3 = v3(al("MAKT", 1024, BF16), n=128)
    MRKT3 = v3(al("MRKT", 1024, BF16), n=128)
    BCT = al_lo("BCT", 512); KCT = al_lo("KCT", 512)
    RH3 = v3(al_lo("RH", 512), n=128)
    GT4 = al_lo("GT", 512).rearrange("p (h c k) -> p h c k", h=4, c=2)
    HH4 = al_lo("HH", 512).rearrange("p (h c k) -> p h c k", h=4, c=2)
    ST = [v3(al_lo("ST%d" % i, 256), n=64) for i in range(2)]
    for kz, z in ((["VPAD"], VPAD3), (["UVPAD"], UVPAD3), (["RB", "RBp", "RBm0", "RBm1"], RB3)):
        S.pool(lambda e, z=z: e.memset(z, 0.0), [], kz)
    P = self.PS
    stc = [0]

    def prep_unit(d, gi, hp, kt, kd):
        (t1, t2, NLW, CP, CWN, tB, tK), (k1, k2, kNLW, kCP, kCWN, ktB, ktK) = TSET[kt]
        ARk, BTk, KTLk, BCk, KCk, ELk, WLDk = DSET[kd]
        ts = slice(gi * GW, (gi + 1) * GW)
        kkk = "KK%d_%d" % (hp, gi // 2)
        hs = slice(hp * 128, (hp + 1) * 128)
        rr = slice(32 * d, 32 * d + 32)
        kdg = "%d_%d" % (kd, hp)
        pb, kb = P[0][:, kt * 256:kt * 256 + 256], "B0u%d" % kt
        S.pe(lambda e: e.matmul(pb[:, 0:GW], lhsT=W2[rr, hs], rhs=ZC3[rr, 12, ts], start=True, stop=True), ["W2", "ZC12"], [kb])
        self.sigmoid3(NLW, pb[:, 0:GW], kb, kNLW, t1, k1, bias=ncol("w0_%d_%d" % (d, hp)), final_bias=self.CT[:, 4:5])
        pb2, kb2 = pb, kb
        S.pe(lambda e: e.matmul(pb2[:, 0:GW], lhsT=A2[rr, hs], rhs=ZC3[rr, 13, ts], start=True, stop=True), ["A2", "ZC13"], [kb2])
        self.sigmoid3(t2, pb2[:, 0:GW], kb2, k2, t1, k1, bias=ncol("a0_%d_%d" % (d, hp)))
        S.dve(lambda e: e.tensor_tensor_scan(out=CP, data0=RESETM[:, 0:GW], data1=NLW, initial=0.0, op0=ALU.mult, op1=ALU.add), [kNLW, "RESETM"], [kCP])
        CP3 = v3(CP, l=64)
        tot = CP3[:, :, 63:64]
        nch = GW // 64
        if d == 0:
            CWN, kCWN = CP, kCP
        else:
            S.dve(lambda e: e.tensor_tensor(out=CWN, in0=NLW, in1=CP, op=ALU.subtract), [kNLW, kCP], [kCWN])
            S.dve(lambda e: e.tensor_tensor(out=v3(CWN, l=64), in0=v3(CWN, l=64), in1=tot.to_broadcast([128, nch, 64]), op=ALU.add), [kCWN, kCP], [kCWN])
        S.act(lambda e: e.activation(out=ELk[:, hp, :].unsqueeze(2), in_=tot, func=AF.Exp, scale=-1.0), [kCP], ["EL" + kdg])
        S.act(lambda e: e.activation(out=t1, in_=CWN, func=AF.Exp, scale=-1.0), [kCWN], [k1])
        S.dve(lambda e: e.tensor_tensor(out=ARk[:, hp, :, 1, :], in0=v3(ZC3[:, hp, ts], t=128), in1=v3(t1, t=128), op=ALU.mult), ["ZC%d" % hp, k1], ["AR" + kdg])
        S.dve(lambda e: e.tensor_tensor(out=t1, in0=NLW, in1=CWN, op=ALU.subtract), [kNLW, kCWN, k1], [k1])
        S.act(lambda e: e.activation(out=t1, in_=t1, func=AF.Exp), [k1], [k1])
        S.dve(lambda e: e.scalar_tensor_tensor(out=ARk[:, hp, :, 0, :], in0=v3(KK3[:, hp, ts], t=128), scalar=-1.0, in1=v3(t1, t=128), op0=ALU.mult, op1=ALU.mult),
              [kkk, k1], ["AR" + kdg])
        S.dve(lambda e: e.tensor_tensor(out=tB, in0=KK3[:, hp, ts], in1=t2, op=ALU.mult), [kkk, k2], [ktB])
        S.dve(lambda e: e.tensor_scalar(out=tK, in0=t2, scalar1=pcol("ka%d" % hp), scalar2=OMK[:, hp:hp + 1], op0=ALU.mult, op1=ALU.add), [k2, "PC", "OMK"], [ktK])
        S.dve(lambda e: e.tensor_tensor(out=tK, in0=tK, in1=ZC3[:, 4 + hp, ts], op=ALU.mult), [ktK, "ZC%d" % (4 + hp)], [ktK])
        S.act(lambda e: e.activation(out=t1, in_=CWN, func=AF.Exp), [kCWN, k1], [k1])
        S.dve(lambda e: e.tensor_tensor(out=BTk[:, hp, :], in0=tB, in1=t1, op=ALU.mult), [ktB, k1], ["BT" + kdg])
        S.dve(lambda e: e.tensor_tensor(out=KTLk[:, hp, :], in0=tK, in1=t1, op=ALU.mult), [ktK, k1], ["KTL" + kdg])
        S.dve(lambda e: e.tensor_tensor(out=v3(t1, l=64), in0=v3(CWN, l=64), in1=tot.to_broadcast([128, nch, 64]), op=ALU.subtract), [kCWN, kCP, k1], [k1])
        S.act(lambda e: e.activation(out=t1, in_=t1, func=AF.Exp), [k1], [k1])
        S.dve(lambda e: e.tensor_tensor(out=BCk[:, hp, :], in0=tB, in1=t1, op=ALU.mult), [ktB, k1], ["BCH" + kdg])
        S.dve(lambda e: e.tensor_tensor(out=KCk[:, hp, :], in0=tK, in1=t1, op=ALU.mult), [ktK, k1], ["KCH" + kdg])
        S.dve(lambda e: e.tensor_tensor(out=WLDk[:, hp, :, :], in0=ID64.unsqueeze(1).to_broadcast([128, nch, 64]),
                                        in1=ELk[:, hp, :].unsqueeze(2).to_broadcast([128, nch, 64]), op=ALU.mult), ["EL" + kdg, "CB"], ["WLD" + kdg])

    def prep_group(d, gi, kd):
        for pair in ((0, 1), (2, 3)):
            a_ = S.capture(lambda: prep_unit(d, gi, pair[0], 0, kd))
            b_ = S.capture(lambda: prep_unit(d, gi, pair[1], 1, kd))
            S.extend(Sched.merge(a_, b_))

    def emit_block(d, gi, bj, kd):
        ARk, BTk, KTLk, BCk, KCk, ELk, WLDk = DSET[kd]
        msT = MS0 if d == 0 else MS1
        msN = MS1[:, 0:128] if d == 0 else MS0[:, 0:128]
        allh = lambda p: [p + "_%d" % hp for hp in range(4)]
        sti = stc[0]
        j = gi * 2 + bj
        bs = slice(bj * 128, (bj + 1) * 128)
        tsl = slice(j * 128, (j + 1) * 128)
        pbb0 = v3(P[1][:, 0:512].bitcast(BF16), t=128)
        pbb1 = v3(P[2][:, 0:512].bitcast(BF16), t=128)
        for hp in range(4):
            S.pe(lambda e, hp=hp: e.transpose(out=pbb0[:, hp, :], in_=ZC3[:, 8 + hp, tsl], identity=self.IDENT), ["ZC%d" % (8 + hp), "CB"], ["B2"])
            S.pe(lambda e, hp=hp: e.transpose(out=pbb0[:, 4 + hp, :], in_=ARk[:, hp, bj, 0, :], identity=self.IDENT), ["AR%d_%d" % (kd, hp), "CB"], ["B2"])
            S.pe(lambda e, hp=hp: e.transpose(out=pbb1[:, hp, :], in_=BCk[:, hp, bs], identity=self.IDENT), ["BCH%d_%d" % (kd, hp), "CB"], ["B4"])
            S.pe(lambda e, hp=hp: e.transpose(out=pbb1[:, 4 + hp, :], in_=KCk[:, hp, bs], identity=self.IDENT), ["KCH%d_%d" % (kd, hp), "CB"], ["B4"])
        S.act(lambda e: e.activation(out=v3(VT, t=128), in_=pbb0[:, 0:4, :], func=AF.Copy), ["B2"], ["VT", "B2"])
        S.dve(lambda e: e.tensor_copy(out=MY[0][:, :, 128:192].rearrange("p (e a) k -> p a e k", e=2), in_=pbb0[:, 4:8, :].rearrange("p a (e k) -> p a e k", e=2)), ["B2"], ["MY0y0", "MY0y1", "B2"])
        S.act(lambda e: e.activation(out=v3(BCT, t=128), in_=pbb1[:, 0:4, :], func=AF.Copy), ["B4"], ["BCT", "B4"])
        S.dve(lambda e: e.tensor_copy(out=v3(KCT, t=128), in_=pbb1[:, 4:8, :]), ["B4"], ["KCT", "B4"])
        VT3h = v3(VT, k=64)
        nat = lambda h: 2 * (h % 4) + h // 4
        BCT3h = v3(BCT, k=64)
        for c in range(2):
            cr = slice(64 * c, 64 * c + 64)
            S.act(lambda e, cr=cr, c=c: e.activation(out=VPAD3[cr, :, 64 * c:64 * c + 64], in_=VT3h[cr, :, :], func=AF.Copy), ["VT"], ["VPAD"])
            S.act(lambda e, cr=cr, c=c: e.activation(out=RB3[cr, :, 128 + 64 * c:192 + 64 * c].rearrange("p (e a) k -> p a e k", e=2),
                                                     in_=BCT3h[cr, :, :].rearrange("p (a e) k -> p a e k", e=2), func=AF.Copy), ["BCT"], ["RBp"])
        bk3 = lambda t, h: "B%d" % (2 * t + h // 4)
        for hp in range(4):
            for e_ in range(2):
                h = 4 * e_ + hp
                pr = slice(64 * e_, 64 * e_ + 64)
                aT = ARk[pr, hp, bj, 0, :]; rT = ARk[pr, hp, bj, 1, :]
                bT = BTk[pr, hp, bs]
                hc_ = slice(h * 128, (h + 1) * 128)
                S.pe(lambda e, hc_=hc_, aT=aT, bT=bT: e.matmul(P[1][:, hc_], lhsT=aT, rhs=bT, start=True, stop=True), ["AR%d_%d" % (kd, hp), "BT%d_%d" % (kd, hp)], [bk3(1, h)])
                S.pe(lambda e, hc_=hc_, aT=aT, bT=bT: e.matmul(P[2][:, hc_], lhsT=bT, rhs=aT, start=True, stop=True), ["AR%d_%d" % (kd, hp), "BT%d_%d" % (kd, hp)], [bk3(2, h)])
                S.pe(lambda e, hc_=hc_, rT=rT, bT=bT: e.matmul(P[3][:, hc_], lhsT=bT, rhs=rT, start=True, stop=True), ["AR%d_%d" % (kd, hp), "BT%d_%d" % (kd, hp)], [bk3(3, h)])
        mb4 = lambda m: m.unsqueeze(1).to_broadcast([128, 4, 128])
        for hh in range(2):
            bsl = slice(512 * hh, 512 * hh + 512)
            hsl = slice(4 * hh, 4 * hh + 4)
            S.dve(lambda e, bsl=bsl, hsl=hsl: e.tensor_tensor(out=MY[0][:, hsl, 0:128], in0=v3(P[1][:, bsl], t=128), in1=mb4(msN), op=ALU.mult),
                  ["B%d" % (2 + hh), "CB"], ["MY0m%d" % hh, "B%d" % (2 + hh)])
            S.dve(lambda e, bsl=bsl, hsl=hsl: e.tensor_tensor(out=MT[0][:, hsl, :], in0=v3(P[2][:, bsl], t=128), in1=mb4(msT[:, 0:128]), op=ALU.mult),
                  ["B%d" % (4 + hh), "CB"], ["MT0_%d" % hh, "B%d" % (4 + hh)])
            S.dve(lambda e, bsl=bsl, hsl=hsl: e.tensor_tensor(out=RB3[:, hsl, 0:128], in0=v3(P[3][:, bsl], t=128), in1=mb4(msT[:, 128:256]), op=ALU.mult),
                  ["B%d" % (6 + hh), "CB"], ["RBm%d" % hh, "B%d" % (6 + hh)])
        for hp in range(4):
            for e_ in range(2):
                h = 4 * e_ + hp
                pr = slice(64 * e_, 64 * e_ + 64)
                aT = ARk[pr, hp, bj, 0, :]; rT = ARk[pr, hp, bj, 1, :]
                kT = KTLk[pr, hp, bs]
                hc_ = slice(h * 128, (h + 1) * 128)
                S.pe(lambda e, hc_=hc_, aT=aT, kT=kT: e.matmul(P[2][:, hc_], lhsT=kT, rhs=aT, start=True, stop=True), ["AR%d_%d" % (kd, hp), "KTL%d_%d" % (kd, hp)], [bk3(2, h)])
                S.pe(lambda e, hc_=hc_, rT=rT, kT=kT: e.matmul(P[3][:, hc_], lhsT=kT, rhs=rT, start=True, stop=True), ["AR%d_%d" % (kd, hp), "KTL%d_%d" % (kd, hp)], [bk3(3, h)])
        for hh in range(2):
            bsl = slice(512 * hh, 512 * hh + 512)
            hsl = slice(4 * hh, 4 * hh + 4)
            S.dve(lambda e, bsl=bsl, hsl=hsl: e.tensor_tensor(out=MAKT3[:, hsl, :], in0=v3(P[2][:, bsl], t=128), in1=mb4(msT[:, 0:128]), op=ALU.mult),
                  ["B%d" % (4 + hh), "CB"], ["MAKT%d" % hh, "B%d" % (4 + hh)])
            S.dve(lambda e, bsl=bsl, hsl=hsl: e.tensor_tensor(out=MRKT3[:, hsl, :], in0=v3(P[3][:, bsl], t=128), in1=mb4(msT[:, 128:256]), op=ALU.mult),
                  ["B%d" % (6 + hh), "CB"], ["MRKT%d" % hh, "B%d" % (6 + hh)])
        for hh in range(2):
            for h in range(4 * hh, 4 * hh + 4):
                S.pe(lambda e, h=h, hh=hh: e.matmul(P[1][:, 512 * hh + (h % 4) * 64:512 * hh + (h % 4) * 64 + 64], lhsT=MAKT3[:, h, :], rhs=VT3h[:, nat(h), :], start=True, stop=True),
                     ["MAKT%d" % hh, "VT"], ["B%d" % (2 + hh)])
            S.act(lambda e, hh=hh: e.activation(out=MY[0][:, 4 * hh:4 * hh + 4, 192:256], in_=v3(P[1][:, 512 * hh:512 * hh + 256], k=64), func=AF.Copy), ["B%d" % (2 + hh)], ["MY0x%d" % hh, "B%d" % (2 + hh)])
        mk = lambda i, hh: ["MY%dm%d" % (i, hh), "MY%dy%d" % (i, hh), "MY%dx%d" % (i, hh)]
        mykeys = lambda i: mk(i, 0) + mk(i, 1)
        for lv in range(6):
            a, b_ = lv % 2, (lv + 1) % 2
            for hh in range(2):
                rk_ = mk(a, hh) + ["MT%d_%d" % (a, hh)]
                for h in range(4 * hh, 4 * hh + 4):
                    hc_ = slice(h * 128, (h + 1) * 128)
                    if lv < 4:
                        S.pe(lambda e, h=h, a=a, hc_=hc_: e.matmul(P[2][:, hc_], lhsT=MT[a][:, h, :], rhs=MY[a][:, h, 0:128], start=True, stop=True), rk_, ["B%d" % (4 + hh)])
                    S.pe(lambda e, h=h, a=a, hc_=hc_: e.matmul(P[3][:, hc_], lhsT=MT[a][:, h, :], rhs=MY[a][:, h, 128:256], start=True, stop=True), rk_, ["B%d" % (6 + hh)])
                    if lv < 5:
                        S.pe(lambda e, h=h, a=a, hc_=hc_: e.matmul(P[1][:, hc_], lhsT=MY[a][:, h, 0:128], rhs=MT[a][:, h, :], start=True, stop=True), rk_, ["B%d" % (2 + hh)])
                bsl = slice(512 * hh, 512 * hh + 512)
                hsl = slice(4 * hh, 4 * hh + 4)
                S.dve(lambda e, bsl=bsl, hsl=hsl, a=a, b_=b_: e.tensor_tensor(out=MY[b_][:, hsl, 128:256], in0=v3(P[3][:, bsl], t=128), in1=MY[a][:, hsl, 128:256], op=ALU.add),
                      ["B%d" % (6 + hh)] + mk(a, hh), ["MY%dy%d" % (b_, hh), "MY%dx%d" % (b_, hh), "B%d" % (6 + hh)])
                if lv < 4:
                    S.act(lambda e, bsl=bsl, hsl=hsl, b_=b_: e.activation(out=MY[b_][:, hsl, 0:128], in_=v3(P[2][:, bsl], t=128), func=AF.Copy),
                          ["B%d" % (4 + hh)], ["MY%dm%d" % (b_, hh), "B%d" % (4 + hh)])
                if lv < 5:
                    S.act(lambda e, bsl=bsl, hsl=hsl, b_=b_: e.activation(out=MT[b_][:, hsl, :], in_=v3(P[1][:, bsl], t=128), func=AF.Copy),
                          ["B%d" % (2 + hh)], ["MT%d_%d" % (b_, hh), "B%d" % (2 + hh)])
        YF = MY[0]
        yk = mykeys(0)
        for c in range(2):
            cr = slice(64 * c, 64 * c + 64)
            S.act(lambda e, cr=cr, c=c: e.activation(out=UVPAD3[cr, :, 64 * c:64 * c + 64], in_=YF[cr, :, 192:256], func=AF.Copy), yk, ["UVPAD"])
        for h in range(8):
            eo = 64 * (h // 4)
            hp_ = h % 4
            S.pe(lambda e, h=h, eo=eo, hp_=hp_: e.matmul(P[2][eo:eo + 64, hp_ * 256:hp_ * 256 + 256], lhsT=YF[:, h, 128:192], rhs=RB3[:, h, :], start=True, stop=True),
                 mk(0, h // 4) + ["RBp", "RBm%d" % (h // 4), "RB"], ["B%d" % (4 + hp_ // 2)])
        cb0 = (2 * bj)
        for bb in range(2):
            pv = v3(P[2][:, 512 * bb:512 * bb + 512], t=256)
            hps = slice(2 * bb, 2 * bb + 2)
            S.dve(lambda e, pv=pv, hps=hps: e.tensor_tensor(out=RH3[:, hps, :], in0=pv[:, :, 0:128], in1=ARk[:, hps, bj, 1, :], op=ALU.add),
                  ["B%d" % (4 + bb)] + allh("AR%d" % kd), ["RH%d" % bb, "B%d" % (4 + bb)])
            S.dve(lambda e, pv=pv, hps=hps: e.tensor_tensor(out=GT4[:, hps, :, :], in0=pv[:, :, 128:256].rearrange("p h (c k) -> p h c k", c=2), in1=WLDk[:, hps, cb0:cb0 + 2, :], op=ALU.add),
                  ["B%d" % (4 + bb)] + allh("WLD%d" % kd), ["GT%d" % bb, "B%d" % (4 + bb)])
        KCT3h = v3(KCT, k=64)
        for h in range(8):
            eo = 64 * (h // 4)
            hp_ = h % 4
            S.pe(lambda e, h=h, eo=eo, hp_=hp_: e.matmul(P[1][eo:eo + 64, hp_ * 128:hp_ * 128 + 128], lhsT=BCT3h[:, nat(h), :], rhs=UVPAD3[:, h, :], start=True, stop=False),
                 ["BCT", "UVPAD"], ["B2"])
            S.pe(lambda e, h=h, eo=eo, hp_=hp_: e.matmul(P[1][eo:eo + 64, hp_ * 128:hp_ * 128 + 128], lhsT=KCT3h[:, nat(h), :], rhs=VPAD3[:, nat(h), :], start=False, stop=True),
                 ["KCT", "VPAD"], ["B2"])
        S.act(lambda e: e.activation(out=HH4, in_=P[1][:, 0:512].rearrange("p (h c k) -> p h c k", h=4, c=2), func=AF.Copy), ["B2"], ["HH", "B2"])
        clist = (0, 1) if d == 0 else (1, 0)
        for c in clist:
            st_in, k_in = ST[sti % 2], "ST%d" % (sti % 2)
            st_out, k_out = ST[(sti + 1) % 2], "ST%d" % ((sti + 1) % 2)
            sti += 1
            cs = slice(64 * c, 64 * c + 64)
            SB = (P[1], P[2]); SBK = ("B3", "B5")
            OB_ = (P[0], P[3]); OBK = ("B1", "B7")
            for h in range(8):
                e_, hp_ = h // 4, h % 4
                pr = slice(64 * e_, 64 * e_ + 64)
                S.pe(lambda e, e_=e_, hp_=hp_, pr=pr, c=c, st_in=st_in: e.matmul(SB[e_][pr, 512 + hp_ * 64:512 + hp_ * 64 + 64], lhsT=GT4[pr, hp_, c, :], rhs=st_in[pr, hp_, :], start=True, stop=True),
                     ["GT0", "GT1", k_in + "e0", k_in + "e1"], [SBK[e_]])
            for h in range(8):
                e_, hp_ = h // 4, h % 4
                pr = slice(64 * e_, 64 * e_ + 64)
                oc = slice(512 + hp_ * 64, 512 + hp_ * 64 + 64)
                S.pe(lambda e, h=h, e_=e_, pr=pr, oc=oc, cs=cs: e.matmul(OB_[e_][pr, oc], lhsT=VT3h[:, nat(h), :], rhs=MRKT3[:, h, cs], start=True, stop=False), ["VT", "MRKT0", "MRKT1"], [OBK[e_]])
                S.pe(lambda e, h=h, e_=e_, pr=pr, oc=oc, cs=cs: e.matmul(OB_[e_][pr, oc], lhsT=YF[:, h, 192:256], rhs=RB3[:, h, cs], start=False, stop=False), mk(0, e_) + ["RBm%d" % e_], [OBK[e_]])
                S.pe(lambda e, hp_=hp_, e_=e_, pr=pr, oc=oc, cs=cs, st_in=st_in: e.matmul(OB_[e_][pr, oc], lhsT=st_in[pr, hp_, :], rhs=RH3[pr, hp_, cs], start=False, stop=True), [k_in + "e0", k_in + "e1", "RH0", "RH1"], [OBK[e_]])
            ocol = slice(j * 128 + 64 * c, j * 128 + 64 * c + 64)
            for e_ in range(2):
                pr = slice(64 * e_, 64 * e_ + 64)
                S.dve(lambda e, e_=e_, pr=pr, c=c, st_out=st_out: e.tensor_tensor(out=st_out[pr], in0=v3(SB[e_][pr, 512:768], k=64), in1=HH4[pr, :, c, :], op=ALU.add),
                      [SBK[e_], "HH"], [k_out + "e%d" % e_, SBK[e_]])
                if d == 0:
                    S.act(lambda e, e_=e_, pr=pr, ocol=ocol: e.activation(out=OACC3[pr, :, ocol], in_=v3(OB_[e_][pr, 512:768], k=64), func=AF.Copy), [OBK[e_]], ["OACC%d_%d" % (j, e_), OBK[e_]])
                else:
                    S.dve(lambda e, e_=e_, pr=pr, ocol=ocol: e.tensor_tensor(out=OACC3[pr, :, ocol], in0=v3(OB_[e_][pr, 512:768], k=64), in1=OACC3[pr, :, ocol], op=ALU.add),
                          [OBK[e_], "OACC%d_%d" % (j, e_)], ["OACC%d_%d" % (j, e_), OBK[e_]])
        stc[0] = sti

    def blocks_group(d, gi, kd):
        for bj in ((0, 1) if d == 0 else (1, 0)):
            emit_block(d, gi, bj, kd)

    steps = [(0, gi) for gi in range(T // GW)] + [(1, gi) for gi in range(T // GW - 1, -1, -1)]
    steps = steps[:getattr(self, "maxsteps", 99)]
    for kst in range(2):
        S.pool(lambda e, kst=kst: e.memset(ST[kst], 0.0), [], ["ST%de0" % kst, "ST%de1" % kst])
    prep_group(steps[0][0], steps[0][1], 0)
    for si, (d, gi) in enumerate(steps):
        if si == T // GW:
            zi = stc[0] % 2
            S.pool(lambda e, zi=zi: e.memset(ST[zi], 0.0), [], ["ST%de0" % zi, "ST%de1" % zi])
        blk_ops = S.capture(lambda: blocks_group(d, gi, si % 2))
        if si + 1 < len(steps):
            nd, ngi = steps[si + 1]
            prep_ops = S.capture(lambda: prep_group(nd, ngi, (si + 1) % 2))
        else:
            prep_ops = []
        S.extend(Sched.merge(blk_ops, prep_ops))
    self.OACC3 = OACC3
    self.tap("OACC", OACC3.rearrange("p a n -> p (a n)"), ["OACC%d_%d" % (j, e_) for j in range(NB) for e_ in range(2)], [128, 4 * T])


Builder.phase_C = _phase_C


RUN_PHASE_C = True


def _build_full():
    B = Builder()
    B.phase_A()
    B.phase_B()
    B.phase_C()
    B.phase_C3()
    B.phase_D()
    return B


def kernel(**inputs):
    inp = {k: np.asarray(v) for k, v in inputs.items()}
    B = _build_full()
    nc = B.finish()
    in_maps = [_host_inputs(inp, b) for b in range(8)]
    res = run_bass_kernel_spmd(nc, in_maps, core_ids=list(range(8)))
    out = np.stack([np.asarray(r["out"]) for r in res.results], 0).astype(np.float32)
    return out


def _phase_C3(self):
    S = self.S
    al = self.alloc
    ZC3, OACC3, A2, G2, CB = self.ZC3, self.OACC3, self.A2, self.G2, self.CB
    BAVG, BONES = CB[:, 640:768], CB[:, 768:896]
    pcol = lambda n: self.PC[:, PCI[n]:PCI[n] + 1]
    ncol = lambda n: self.NPCt[:, PCI[n]:PCI[n] + 1]
    self.reset(self.mC)
    self.YR3 = YR3 = v3(al("YR", 4 * T, BF16), t=T)
    self.mYR = self.mark()
    NSET = 4
    sets = []
    for si in range(NSET):
        tt = [al("TF%d_%d" % (si, i), 512, F32) for i in range(5)]
        tk = ["TF%d_%d" % (si, i) for i in range(5)]
        bb = [al(n + "%d" % si, 512, BF16) for n in ("OBF", "SQF", "PBF")]
        sets.append((tt, tk, bb, ["OBF%d" % si, "SQF%d" % si, "PBF%d" % si]))
    OMK2 = al("OMK2", 4, F32)
    for hp in range(4):
        S.dve(lambda e, hp=hp: e.tensor_scalar(out=OMK2[:, hp:hp + 1], in0=pcol("ka%d" % hp), scalar1=-2.0, scalar2=2.0, op0=ALU.mult, op1=ALU.add), ["PC"], ["OMK2"])

    def unit(tg, hp, st):
        (t1, t2, tA0, tA1, tB), (k1, k2, kA0, kA1, kB), (OB, SQ, PB), (kOB, kSQ, kPB) = st
        ts = slice(tg * 512, (tg + 1) * 512)
        ok = ["OACC%d_%d" % (4 * tg + q, e_) for q in range(4) for e_ in range(2)]
        hs = slice(hp * 128, (hp + 1) * 128)
        o = OACC3[:, hp, ts]
        S.act(lambda e: e.activation(out=OB, in_=o, func=AF.Copy), ok, [kOB]); yield
        pb, kb = self.bank()
        S.pe(lambda e: e.matmul(pb, lhsT=BAVG, rhs=OB, start=True, stop=True), [kOB, "CB"], [kb]); yield
        S.dve(lambda e: e.tensor_tensor(out=t1, in0=o, in1=pb, op=ALU.subtract), ok + [kb], [k1, kb]); yield
        S.act(lambda e: e.activation(out=SQ, in_=t1, func=AF.Square), [k1], [kSQ]); yield
        pb2, kb2 = self.bank()
        S.pe(lambda e: e.matmul(pb2, lhsT=BAVG, rhs=SQ, start=True, stop=True), [kSQ, "CB"], [kb2]); yield
        S.act(lambda e: e.activation(out=t2, in_=pb2, func=AF.Ln, bias=self.CT[:, 2:3]), [kb2, "CT"], [k2, kb2]); yield
        S.act(lambda e: e.activation(out=t2, in_=t2, func=AF.Exp, scale=-0.5), [k2], [k2]); yield
        S.dve(lambda e: e.tensor_tensor(out=t1, in0=t1, in1=t2, op=ALU.mult), [k1, k2], [k1]); yield
        S.dve(lambda e: e.tensor_scalar(out=t1, in0=t1, scalar1=pcol("lnw%d" % hp), scalar2=pcol("lnb%d" % hp), op0=ALU.mult, op1=ALU.add), [k1, "PC"], [k1]); yield
        for d, (tA, kA) in enumerate(((tA0, kA0), (tA1, kA1))):
            rr = slice(32 * d, 32 * d + 32)
            pa, kpa = self.bank()
            S.pe(lambda e, pa=pa, rr=rr: e.matmul(pa, lhsT=A2[rr, hs], rhs=ZC3[rr, 13, ts], start=True, stop=True), ["A2", "ZC13"], [kpa]); yield
            self.sigmoid3(tA, pa, kpa, kA, t2, k2, bias=ncol("a0_%d_%d" % (d, hp))); yield
        S.dve(lambda e: e.tensor_tensor(out=tA0, in0=tA0, in1=tA1, op=ALU.add), [kA0, kA1], [kA0]); yield
        S.dve(lambda e: e.tensor_scalar(out=tA0, in0=tA0, scalar1=pcol("ka%d" % hp), scalar2=OMK2[:, hp:hp + 1], op0=ALU.mult, op1=ALU.add), [kA0, "PC", "OMK2"], [kA0]); yield
        S.dve(lambda e: e.tensor_tensor(out=tA0, in0=tA0, in1=ZC3[:, 4 + hp, ts], op=ALU.mult), [kA0, "ZC%d" % (4 + hp)], [kA0]); yield
        S.dve(lambda e: e.tensor_tensor(out=tA0, in0=tA0, in1=ZC3[:, hp, ts], op=ALU.mult), [kA0, "ZC%d" % hp], [kA0]); yield
        S.dve(lambda e: e.tensor_scalar(out=PB, in0=tA0, scalar1=pcol("rk%d" % hp), scalar2=None, op0=ALU.mult), [kA0, "PC"], [kPB]); yield
        pbb, kbb = self.bank()
        S.pe(lambda e: e.matmul(pbb, lhsT=BONES, rhs=PB, start=True, stop=True), [kPB, "CB"], [kbb]); yield
        S.dve(lambda e: e.tensor_tensor(out=tB, in0=pbb, in1=ZC3[:, 8 + hp, ts], op=ALU.mult), [kbb, "ZC%d" % (8 + hp)], [kB, kbb]); yield
        S.dve(lambda e: e.tensor_tensor(out=t1, in0=t1, in1=tB, op=ALU.add), [k1, kB], [k1]); yield
        pg, kpg = self.bank()
        S.pe(lambda e: e.matmul(pg, lhsT=G2[0:96, hs], rhs=ZC3[0:96, 14, ts], start=True, stop=True), ["G2", "ZC14"], [kpg]); yield
        S.dve(lambda e: e.tensor_tensor(out=YR3[:, hp, ts], in0=pg, in1=t1, op=ALU.mult), [kpg, k1], ["YR%d_%d" % (hp, tg), kpg]); yield

    ulist = [(tg, hp) for tg in range(4) for hp in range(4)]
    for i0_ in range(0, len(ulist), NSET):
        active = [unit(tg, hp, sets[si]) for si, (tg, hp) in enumerate(ulist[i0_:i0_ + NSET])]
        while active:
            for g in list(active):
                try:
                    next(g)
                except StopIteration:
                    active.remove(g)
    self.tap("YR", YR3.rearrange("p a n -> p (a n)"), ["YR%d_%d" % (hp, tg) for hp in range(4) for tg in range(4)], [128, 4 * T])


def _post_norm_residual(self, P2, kbs, G, kg, xin, kxin, xout, kxout, SSc, kss, junk, kjunk, TMPH, ktmp):
    S = self.S
    ssa, ssb, rs = SSc[:, 0:1], SSc[:, 1:2], SSc[:, 2:3]
    S.act(lambda e: e.activation(out=junk[:, 0:512], in_=P2[:, 0:512], func=AF.Square, accum_out=ssa), [kbs[0]], [kjunk, kss + "a", kbs[0]])
    S.act(lambda e: e.activation(out=junk[:, 0:512], in_=P2[:, 512:1024], func=AF.Square, accum_out=ssb), [kbs[1]], [kjunk, kss + "b", kbs[1]])
    S.dve(lambda e: e.tensor_tensor(out=rs, in0=ssa, in1=ssb, op=ALU.add), [kss + "a", kss + "b"], [kss + "r"])
    S.act(lambda e: e.activation(out=rs, in_=rs, func=AF.Ln, scale=1.0 / 1024, bias=self.EPS6), [kss + "r", "CT"], [kss + "r"])
    S.act(lambda e: e.activation(out=rs, in_=rs, func=AF.Exp, scale=-0.5), [kss + "r"], [kss + "r"])
    for half in range(2):
        hs = slice(half * 512, (half + 1) * 512)
        S.dve(lambda e, hs=hs: e.scalar_tensor_tensor(out=TMPH, in0=P2[:, hs], scalar=rs, in1=G[:, hs], op0=ALU.mult, op1=ALU.mult),
              [kbs[half], kss + "r", kg], [ktmp, kbs[half]])
        S.dve(lambda e, hs=hs: e.tensor_tensor(out=xout[:, hs], in0=TMPH, in1=xin[:, hs], op=ALU.add), [ktmp, kxin], [kxout])


Builder.phase_C3 = _phase_C3
Builder.post_norm_residual = _post_norm_residual


def _phase_D(self):
    S, nc, dr = self.S, self.nc, self.dr
    al = self.alloc
    pcol = lambda n: self.PC[:, PCI[n]:PCI[n] + 1]
    P = self.PS
    cq0 = self.live["COS"][0]
    assert cq0 + 32768 <= self.live["ZC"][1]
    X1 = v3(self.alloc_at("X1", cq0, 16 * 1024, F32, ["OACC", "CQKV", "KR", "ZC", "COS", "SINS", "VT", "BCT", "KCT", "RH", "GT", "HH", "ST0", "ST1"]), n=1024)
    X1END = cq0 + 32768
    self.reset(self.mYR)
    WOUT3 = v3(al("WOUT", 8 * 1024, BF16), n=1024)
    S.dma("qpool", WOUT3, dr["w_out"].rearrange("(kc p) n -> p kc n", p=128), [], ["WOUT"])
    GP = al("GPOST", 1024, F32)
    S.dma("qsp", GP, dr["gains"][1:2, :].partition_broadcast(128), [], ["GPOST"])
    XD = [al("XD%d" % i, 1024, F32) for i in range(2)]
    TMPH = al("TMPH", 512, F32)
    JK = al("JKD", 512, BF16)
    SSD = al("SSD", 64, F32)
    ycat = lambda kc: self.YR3[:, kc, :] if kc < 4 else self.YM3[:, kc - 4, :]
    ykeys = lambda kc, i: ["YR%d_%d" % (kc, i // 4)] if kc < 4 else ["YM%d_%d" % (2 * (kc - 4) + e, i // 4) for e in range(2)]
    for i in range(NB):
        P2 = P[2 + i % 2]
        kbs = ["B%d" % (4 + 2 * (i % 2)), "B%d" % (5 + 2 * (i % 2))]
        sl = slice(i * 128, (i + 1) * 128)
        for half in range(2):
            for kc in range(8):
                S.pe(lambda e, P2=P2, half=half, kc=kc, sl=sl: e.matmul(P2[:, half * 512:(half + 1) * 512], lhsT=ycat(kc)[:, sl], rhs=WOUT3[:, kc, half * 512:(half + 1) * 512],
                                                                         start=(kc == 0), stop=(kc == 7)), ["WOUT"] + ykeys(kc, i), [kbs[half]])
        xd, kxd = XD[i % 2], "XD%d" % (i % 2)
        S.dma("qsp", xd, dr["x"][sl, :], [], [kxd])
        self.post_norm_residual(P2, kbs, GP, "GPOST", xd, kxd, X1[:, i, :], "X1_%d" % i, SSD[:, 4 * (i % 8):4 * (i % 8) + 4], "SSD%d" % (i % 8), JK, "JKD", TMPH, "TMPH")
    self.tap("X1", X1.rearrange("p a n -> p (a n)"), ["X1_%d" % i for i in range(NB)], [128, 16 * 1024])
    if self.stage < 3:
        return
    self.reset(X1END)
    WQM3 = v3(al("WQM", 8 * 1024, BF16), n=1024)
    WOM3 = v3(al("WOM", 8 * 1024, BF16), n=1024)
    WKVB3 = v3(al("WKVB", 8 * 1024, BF16), n=1024)
    S.dma("qpool", WQM3, dr["mem_wq"].rearrange("(kc p) n -> p kc n", p=128), [], ["WQM"])
    S.dma("qpool", WOM3, dr["mem_wo"].rearrange("(kc p) n -> p kc n", p=128), [], ["WOM"])
    wkv_src = dr["mem_wkv"].rearrange("(kc p) n -> p kc n", p=128)
    S.dma("qpool", WKVB3, wkv_src[:, :, 0:1024], [], ["WKVB"])
    for kc in range(8):
        S.dve(lambda e, kc=kc: e.tensor_scalar(out=WQM3[:, kc, :], in0=WQM3[:, kc, :], scalar1=pcol("gm%d" % kc), scalar2=None, op0=ALU.mult), ["WQM", "PC"], ["WQM"])
        S.dve(lambda e, kc=kc: e.tensor_scalar(out=WKVB3[:, kc, :], in0=WKVB3[:, kc, :], scalar1=pcol("gt%d" % kc), scalar2=None, op0=ALU.mult), ["WKVB", "PC"], ["WKVB"])
    MTT3 = v3(al("MTT", 8 * 256, BF16), n=256)
    KTM3 = v3(al("KTM", 8 * 256, BF16), n=256)
    VM3 = v3(al("VM", 2 * 1024, BF16), n=1024)
    H2Ts = [v3(al("H2T%d_" % k, 8 * 512, BF16), n=512) for k in range(2)]
    QMs = [v3(al("QM%d_" % k, 8 * 512, BF16), n=512) for k in range(2)]
    OTs = [v3(al("OTM%d_" % k, 8 * 512, BF16), n=512) for k in range(2)]
    PTM = [al("PTM%d" % i, 512, BF16) for i in range(4)]
    RDs = [al("RDM%d" % k, 512, F32) for k in range(2)]
    HBM = [al("HBM%d" % i, 1024, BF16) for i in range(2)]
    MEMB = [al("MEMB0", 1024, F32)] * 2
    GP2 = al("GPOSTM", 1024, F32)
    S.dma("qsp", GP2, dr["gains"][4:5, :].partition_broadcast(128), [], ["GPOSTM"])
    TMPHs = [al("TMPHM%d" % k, 512, F32) for k in range(2)]
    SS2 = al("SSM", 64, F32)
    ONESB = al("ONESB", 128, BF16)
    S.pool(lambda e: e.memset(ONESB, 1.0), [], ["ONESB"])
    for blk in range(2):
        mb, kmb = MEMB[0], "MEMB0"
        hb, khb = HBM[blk], "HBM%d" % blk
        S.dma("qsp", mb, dr["mem"][blk * 128:(blk + 1) * 128, :], [], [kmb])
        ss, rs = SS2[:, 2 * blk:2 * blk + 1], SS2[:, 2 * blk + 1:2 * blk + 2]
        self.rms_plain(mb, kmb, hb, khb, ss, rs, "SSMm%d" % blk, hb, khb)
        self.transpose_to(hb, khb, 8, None, MTT3[:, :, blk * 128:(blk + 1) * 128], "MTT%d" % blk, blk)
    mk = ["MTT0", "MTT1"]
    ev = 0
    for oc in range(8):
        pb, kb = self.bank()
        for kc in range(8):
            S.pe(lambda e, pb=pb, kc=kc, oc=oc: e.matmul(pb[:, 0:256], lhsT=WKVB3[:, kc, oc * 128:(oc + 1) * 128], rhs=MTT3[:, kc, :], start=(kc == 0), stop=(kc == 7)), ["WKVB"] + mk, [kb])
        self.evac_copy(KTM3[:, oc, :], pb[:, 0:256], kb, "KTM%d" % oc, ev); ev += 1
    S.dma("qpool", WKVB3, wkv_src[:, :, 1024:2048], [], ["WKVB"])
    for kc in range(8):
        S.dve(lambda e, kc=kc: e.tensor_scalar(out=WKVB3[:, kc, :], in0=WKVB3[:, kc, :], scalar1=pcol("gt%d" % kc), scalar2=None, op0=ALU.mult), ["WKVB", "PC"], ["WKVB"])
    for blk in range(2):
        for half in range(2):
            pb, kb = self.bank()
            for kc in range(8):
                S.pe(lambda e, pb=pb, kc=kc, blk=blk, half=half: e.matmul(pb, lhsT=MTT3[:, kc, blk * 128:(blk + 1) * 128], rhs=WKVB3[:, kc, half * 512:(half + 1) * 512],
                                                                          start=(kc == 0), stop=(kc == 7)), ["WKVB", "MTT%d" % blk], [kb])
            self.evac_copy(VM3[:, blk, half * 512:(half + 1) * 512], pb, kb, "VM%d_%d" % (blk, half), ev); ev += 1
    ktm = ["KTM%d" % oc for oc in range(8)]
    vmk = ["VM%d_%d" % (b_, h_) for b_ in range(2) for h_ in range(2)]
    def d2_group(tg, k):
        H2T3, QM3, OT3, RD, TMPH2 = H2Ts[k], QMs[k], OTs[k], RDs[k], TMPHs[k]
        JKp = TMPH2.bitcast(BF16)
        kp = "%d_" % k
        self.bankset = (0, 1) if k == 0 else (2, 3)
        ev = 0
        for bi in range(4):
            i = 4 * tg + bi
            hb, khb = HBM[k], "HBM%d" % k
            ss, rs = SS2[:, 8 + 2 * (i % 8):9 + 2 * (i % 8)], SS2[:, 9 + 2 * (i % 8):10 + 2 * (i % 8)]
            self.rms_plain(X1[:, i, :], "X1_%d" % i, hb, khb, ss, rs, "SSMx%d" % (i % 8), hb, khb)
            self.transpose_to(hb, khb, 8, None, H2T3[:, :, bi * 128:(bi + 1) * 128], "H2T" + kp + "%d" % bi, bi)
        hk = ["H2T" + kp + "%d" % bi for bi in range(4)]
        for oc in range(8):
            pb, kb = self.bank()
            for kc in range(8):
                S.pe(lambda e, pb=pb, kc=kc, oc=oc: e.matmul(pb, lhsT=WQM3[:, kc, oc * 128:(oc + 1) * 128], rhs=H2T3[:, kc, :], start=(kc == 0), stop=(kc == 7)), ["WQM"] + hk, [kb])
            self.evac_copy(QM3[:, oc, :], pb, kb, "QM" + kp + "%d" % oc, ev); ev += 1
        pti = 0
        for hd in range(4):
            pts = []
            for kbk in range(2):
                ps_, kps = self.bank()
                for c in range(2):
                    S.pe(lambda e, ps_=ps_, c=c, hd=hd, kbk=kbk: e.matmul(ps_, lhsT=KTM3[:, 2 * hd + c, kbk * 128:(kbk + 1) * 128], rhs=QM3[:, 2 * hd + c, :], start=(c == 0), stop=(c == 1)),
                         ktm + ["QM" + kp + "%d" % (2 * hd), "QM" + kp + "%d" % (2 * hd + 1)], [kps])
                pt, kpt = PTM[2 * k + pti % 2], "PTM%d" % (2 * k + pti % 2)
                pti += 1
                S.act(lambda e, pt=pt, ps_=ps_: e.activation(out=pt, in_=ps_, func=AF.Exp, scale=1.0 / 16.0), [kps], [kpt, kps])
                pts.append((pt, kpt))
            pd, kpd = self.bank()
            for kbk in range(2):
                S.pe(lambda e, pd=pd, kbk=kbk, pts=pts: e.matmul(pd, lhsT=ONESB, rhs=pts[kbk][0], start=(kbk == 0), stop=(kbk == 1)), ["ONESB", pts[kbk][1]], [kpd])
            S.act(lambda e, pd=pd: e.activation(out=RD, in_=pd, func=AF.Ln), [kpd], ["RDM%d" % k, kpd])
            S.act(lambda e: e.activation(out=RD, in_=RD, func=AF.Exp, scale=-1.0), ["RDM%d" % k], ["RDM%d" % k])
            for dc in range(2):
                po, kpo = self.bank()
                for kbk in range(2):
                    S.pe(lambda e, po=po, kbk=kbk, pts=pts, hd=hd, dc=dc: e.matmul(po, lhsT=VM3[:, kbk, (2 * hd + dc) * 128:(2 * hd + dc + 1) * 128], rhs=pts[kbk][0],
                                                                                  start=(kbk == 0), stop=(kbk == 1)), vmk + [pts[kbk][1]], [kpo])
                S.dve(lambda e, po=po, hd=hd, dc=dc: e.tensor_tensor(out=OT3[:, 2 * hd + dc, :], in0=po, in1=RD, op=ALU.mult), [kpo, "RDM%d" % k], ["OTM" + kp + "%d" % (2 * hd + dc), kpo])
        otk = ["OTM" + kp + "%d" % c for c in range(8)]
        P2 = P[2 + k]
        kbs = ["B%d" % (4 + 2 * k), "B%d" % (5 + 2 * k)]
        for bi in range(4):
            i = 4 * tg + bi
            for half in range(2):
                for kc in range(8):
                    S.pe(lambda e, half=half, kc=kc, bi=bi: e.matmul(P2[:, half * 512:(half + 1) * 512], lhsT=OT3[:, kc, bi * 128:(bi + 1) * 128], rhs=WOM3[:, kc, half * 512:(half + 1) * 512],
                                                                      start=(kc == 0), stop=(kc == 7)), ["WOM"] + otk, [kbs[half]])
            self.post_norm_residual(P2, kbs, GP2, "GPOSTM", X1[:, i, :], "X1_%d" % i, X1[:, i, :], "X1_%d" % i, SS2[:, 32 + 4 * (i % 8):36 + 4 * (i % 8)], "SSMp%d" % (i % 8), JKp, "TMPHM%d" % k, TMPH2, "TMPHM%d" % k)
        self.bankset = None

    for tg0 in (0, 2):
        a_ = S.capture(lambda: d2_group(tg0, 0))
        b_ = S.capture(lambda: d2_group(tg0 + 1, 1))
        S.extend(Sched.merge(a_, b_))
    self.tap("X2", X1.rearrange("p a n -> p (a n)"), ["X1_%d" % i for i in range(NB)], [128, 16 * 1024])
    if self.stage < 4:
        return
    self.reset(X1END)
    W1M3 = v3(al("W1M", 8 * 4096, BF16), n=4096)
    W2M3 = v3(al("W2M", 32 * 1024, BF16), n=1024)
    w1src = dr["mlp_w1"].rearrange("(kc p) n -> p kc n", p=128)
    w2src = dr["mlp_w2"].rearrange("(hc p) n -> p hc n", p=128)
    for q in range(4):
        S.dma("qpool", W1M3[:, :, q * 1024:(q + 1) * 1024], w1src[:, :, q * 1024:(q + 1) * 1024], [], ["W1M%d" % q])
    for q in range(4):
        S.dma("qpool", W2M3[:, 8 * q:8 * q + 8, :], w2src[:, 8 * q:8 * q + 8, :], [], ["W2M%d" % q])
    for q in range(4):
        for kc in range(8):
            S.dve(lambda e, kc=kc, q=q: e.tensor_scalar(out=W1M3[:, kc, q * 1024:(q + 1) * 1024], in0=W1M3[:, kc, q * 1024:(q + 1) * 1024], scalar1=pcol("gp%d" % kc), scalar2=None, op0=ALU.mult),
                  ["W1M%d" % q, "PC"], ["W1M%d" % q])
    TG = 256
    H3T3 = v3(al("H3T", 8 * TG, BF16), n=TG)
    HB3 = al("HB3", 1024, BF16)
    TMPH3 = HB3.bitcast(F32)
    HID = [al("HID%d" % i, TG, BF16) for i in range(4)]
    GP3 = al("GPOSTP", 1024, F32)
    S.dma("qsp", GP3, dr["gains"][6:7, :].partition_broadcast(128), [], ["GPOSTP"])
    SS3 = al("SSP", 64, F32)
    hi = 0
    for g in range(NB * 128 // TG):
        blks = [g * (TG // 128) + b for b in range(TG // 128)]
        for b, i in enumerate(blks):
            ss, rs = SS3[:, 2 * (i % 4):2 * (i % 4) + 1], SS3[:, 2 * (i % 4) + 1:2 * (i % 4) + 2]
            self.rms_plain(X1[:, i, :], "X1_%d" % i, HB3, "HB3", ss, rs, "SSPx%d" % (i % 4), HB3, "HB3")
            pbt, kbt = self.bank(4)
            pbb = v3(pbt.bitcast(BF16), t=128)
            for c in range(8):
                S.pe(lambda e, c=c, pbb=pbb: e.transpose(out=pbb[:, c, :], in_=HB3[:, c * 128:(c + 1) * 128], identity=self.IDENT), ["HB3", "CB"], [kbt])
            self.evac_copy(H3T3[:, :, b * 128:(b + 1) * 128], pbb, kbt, "H3T%d" % b, i)
        hkeys = ["H3T%d" % b for b in range(len(blks))]
        hq = {}

        def issue_h(hc, hq=hq, hkeys=hkeys):
            nonlocal hi
            ph, kph = self.bank(4)
            q = hc // 8
            for kc in range(8):
                S.pe(lambda e, ph=ph, kc=kc, hc=hc: e.matmul(ph[:, 0:TG], lhsT=W1M3[:, kc, hc * 128:(hc + 1) * 128], rhs=H3T3[:, kc, :], start=(kc == 0), stop=(kc == 7)),
                     ["W1M%d" % q] + hkeys, [kph])
            hid, khid = HID[hi % 4], "HID%d" % (hi % 4)
            hi += 1
            S.act(lambda e, hid=hid, ph=ph: e.activation(out=hid, in_=ph[:, 0:TG], func=AF.Relu), [kph], [khid, kph])
            S.dve(lambda e, hid=hid: e.tensor_tensor(out=hid, in0=hid, in1=hid, op=ALU.mult), [khid], [khid])
            hq[hc] = (hid, khid)

        issue_h(0); issue_h(1); issue_h(2)
        for hc in range(32):
            if hc + 3 < 32:
                issue_h(hc + 3)
            hid, khid = hq.pop(hc)
            q = hc // 8
            for b in range(len(blks)):
                for half in range(2):
                    S.pe(lambda e, b=b, half=half, hid=hid, hc=hc: e.matmul(P[2 + b][:, half * 512:(half + 1) * 512], lhsT=hid[:, b * 128:(b + 1) * 128], rhs=W2M3[:, hc, half * 512:(half + 1) * 512],
                                                                          start=(hc == 0), stop=(hc == 31)), [khid, "W2M%d" % q], ["B%d" % (4 + 2 * b + half)])
        for b, i in enumerate(blks):
            kbs = ["B%d" % (4 + 2 * b), "B%d" % (5 + 2 * b)]
            self.post_norm_residual(P[2 + b], kbs, GP3, "GPOSTP", X1[:, i, :], "X1_%d" % i, X1[:, i, :], "X1_%d" % i, SS3[:, 16 + 4 * (i % 8):20 + 4 * (i % 8)], "SSPp%d" % (i % 8), HB3, "HB3", TMPH3, "HB3")
            op = S.dma("qsp", self.out[i * 128:(i + 1) * 128, :], X1[:, i, :], ["X1_%d" % i], [])
            self.final_ops.append(op)


def _rms_plain(self, xin, kx, hout, kh, ss, rs, kss, junk, kjunk):
    S = self.S
    S.act(lambda e: e.activation(out=junk[:, 0:1024], in_=xin, func=AF.Square, accum_out=ss), [kx], [kjunk, kss])
    S.act(lambda e: e.activation(out=rs, in_=ss, func=AF.Ln, scale=1.0 / 1024, bias=self.EPS6), [kss, "CT"], [kss + "r"])
    S.act(lambda e: e.activation(out=rs, in_=rs, func=AF.Exp, scale=-0.5), [kss + "r"], [kss + "r"])
    S.dve(lambda e: e.tensor_scalar(out=hout, in0=xin, scalar1=rs, scalar2=None, op0=ALU.mult), [kx, kss + "r"], [kh])


Builder.phase_D = _phase_D
Builder.rms_plain = _rms_plain
```
